# Optimizing a Trainium2 kernel written in Bass

```python
import math
import jax
import jax.numpy as jnp
from jax import lax
import numpy as np

D_MODEL = 1024
BATCH = 16
SEQ = 4096
DEPTH = 4
DEC_BATCH = 4
DEC_SEQ = 4096
PAST_LEN = 128

RET_WIDTH = D_MODEL // 2
RET_HEADS = 4
RET_HEAD_DIM = RET_WIDTH // RET_HEADS
HY_WIDTH = D_MODEL // 4
POOL_WIDTH = D_MODEL // 4
POOL_GROUPS = 4
POOL_GROUP_DIM = POOL_WIDTH // POOL_GROUPS
POOL_WINDOWS = (2, 4, 8, 16)
IN_WIDTH = 4 * RET_WIDTH + 3 * HY_WIDTH + POOL_WIDTH
CHUNK = 128
ROPE_THETA = 10000.0
HY_ORDER = 2
HY_EMB_BANDS = 16
HY_EMB_DIM = 1 + 2 * HY_EMB_BANDS
HY_FILTER_HIDDEN = 64
HY_MIN_DECAY = math.log(1e-2) / 1.5
HY_MAX_DECAY = math.log(1e-2) / 0.3
D_FF = 2816
PLE_DIM = 256
EPS = 1e-6

kernel_name = "hybrid_retention_hyena_pool_encoder"


def _rmsnorm(x, g):
    xf = x.astype(jnp.float32)
    xf = xf * lax.rsqrt(jnp.mean(xf * xf, axis=-1, keepdims=True) + EPS)
    return (xf * g.astype(jnp.float32)).astype(x.dtype)


def _dwconv3(x, w):
    xp = jnp.pad(x, ((0, 0), (1, 1), (0, 0)))
    return xp[:, :-2] * w[0] + xp[:, 1:-1] * w[1] + xp[:, 2:] * w[2]


def _rope(x):
    L, dk = x.shape[1], x.shape[-1]
    half = dk // 2
    inv = ROPE_THETA ** (-jnp.arange(half, dtype=jnp.float32) / half)
    ang = jnp.arange(L, dtype=jnp.float32)[:, None] * inv[None, :]
    cos = jnp.cos(ang)[None, :, None, :]
    sin = jnp.sin(ang)[None, :, None, :]
    xf = x.astype(jnp.float32)
    x1, x2 = xf[..., :half], xf[..., half:]
    return jnp.concatenate([x1 * cos - x2 * sin, x2 * cos + x1 * sin], axis=-1)


def _retention(q, k, v, logit_f, logit_b):
    B, L, H, dk = q.shape
    dv = v.shape[-1]
    N = L // CHUNK
    f32 = jnp.float32
    lgf = jax.nn.log_sigmoid(logit_f.astype(f32))
    lgb = jax.nn.log_sigmoid(logit_b.astype(f32))
    qc = q.astype(f32).reshape(B, N, CHUNK, H, dk)
    kc = k.astype(f32).reshape(B, N, CHUNK, H, dk)
    vc = v.astype(f32).reshape(B, N, CHUNK, H, dv)
    pos = jnp.arange(CHUNK, dtype=f32)
    dist = jnp.abs(pos[:, None] - pos[None, :])
    lower = pos[:, None] >= pos[None, :]
    dmask = jnp.where(lower[None], jnp.exp(dist[None] * lgf[:, None, None]),
                      jnp.exp(dist[None] * lgb[:, None, None]))
    scores = jnp.einsum('bnqhd,bnkhd->bnhqk', qc, kc) * dmask[None, None]
    intra = jnp.einsum('bnhqk,bnkhe->bnqhe', scores, vc)
    kv_f = jnp.einsum('bnkhd,bnkhe,hk->nbhde', kc, vc, jnp.exp((CHUNK - 1 - pos)[None] * lgf[:, None]))
    kv_b = jnp.einsum('bnkhd,bnkhe,hk->nbhde', kc, vc, jnp.exp(pos[None] * lgb[:, None]))
    dec_f = jnp.exp(CHUNK * lgf)[None, :, None, None]
    dec_b = jnp.exp(CHUNK * lgb)[None, :, None, None]

    def step_f(s, kv):
        return s * dec_f + kv, s

    def step_b(s, kv):
        return s * dec_b + kv, s

    s0 = jnp.zeros((B, H, dk, dv), f32)
    _, s_f = lax.scan(step_f, s0, kv_f)
    _, s_b = lax.scan(step_b, s0, kv_b, reverse=True)
    cross_f = jnp.einsum('bnqhd,nbhde,hq->bnqhe', qc, s_f, jnp.exp((pos + 1.0)[None] * lgf[:, None]))
    cross_b = jnp.einsum('bnqhd,nbhde,hq->bnqhe', qc, s_b, jnp.exp((CHUNK - pos)[None] * lgb[:, None]))
    return (intra + cross_f + cross_b).reshape(B, L, H, dv)


def _head_layernorm(o, gain):
    mu = jnp.mean(o, axis=-1, keepdims=True)
    oc = o - mu
    var = jnp.mean(oc * oc, axis=-1, keepdims=True)
    return oc * lax.rsqrt(var + EPS) * gain.astype(jnp.float32).reshape(RET_HEADS, RET_HEAD_DIM)


def _hyena_filters(L, w1, b1, freq, w2, b2, w3):
    f32 = jnp.float32
    t = jnp.linspace(0.0, 1.0, L, dtype=f32)[:, None]
    bands = jnp.linspace(1e-4, HY_EMB_BANDS - 1, HY_EMB_BANDS, dtype=f32)
    w = (2.0 * math.pi / L) * jnp.arange(L, dtype=f32)[:, None] * bands[None, :]
    feats = jnp.concatenate([t, jnp.cos(w), -jnp.sin(w)], axis=-1)
    fr = freq.astype(f32)
    h = jnp.sin(fr * (feats @ w1.astype(f32) + b1.astype(f32)))
    h = jnp.sin(fr * (h @ w2.astype(f32) + b2.astype(f32)))
    h = (h @ w3.astype(f32)).reshape(L, HY_ORDER, 2, HY_WIDTH)
    deltas = jnp.abs(jnp.linspace(HY_MIN_DECAY, HY_MAX_DECAY, HY_WIDTH, dtype=f32))
    h = h * jnp.exp(-t[:, :, None, None] * deltas)
    h_f, h_b = h[:, :, 0], h[:, :, 1]
    k_full = jnp.concatenate([h_f, jnp.zeros((1, HY_ORDER, HY_WIDTH), f32), h_b[:0:-1]], axis=0)
    k_full = k_full / jnp.sum(jnp.abs(k_full), axis=0, keepdims=True)
    return jnp.fft.rfft(k_full, axis=0)


def _fftconv(u, khat):
    L = u.shape[1]
    uh = jnp.fft.rfft(u, n=2 * L, axis=1)
    return jnp.fft.irfft(uh * khat[None], n=2 * L, axis=1)[:, :L]


def _hyena(u, conv_w, w1, b1, freq, w2, b2, w3, bias):
    L = u.shape[1]
    u = _dwconv3(u, conv_w).astype(jnp.float32)
    hv, hx1, hx2 = jnp.split(u, 3, axis=-1)
    khat = _hyena_filters(L, w1, b1, freq, w2, b2, w3)
    bias = bias.astype(jnp.float32)
    z = hx1 * (_fftconv(hv, khat[:, 0]) + hv * bias[0])
    return hx2 * (_fftconv(z, khat[:, 1]) + z * bias[1])


def _pool_mixer(u, pool_w, pool_scale):
    B, L, _ = u.shape
    t = jnp.arange(L)
    outs = []
    for g, win in enumerate(POOL_WINDOWS):
        ug = u[..., g * POOL_GROUP_DIM:(g + 1) * POOL_GROUP_DIM].astype(jnp.float32)
        cs = jnp.concatenate([jnp.zeros((B, 1, POOL_GROUP_DIM), jnp.float32), jnp.cumsum(ug, axis=1)], axis=1)
        lo = jnp.clip(t - win // 2, 0, L - 1)
        hi = jnp.clip(t + win // 2 - 1, 0, L - 1)
        cnt = (hi - lo + 1).astype(jnp.float32)
        mean = (cs[:, hi + 1] - cs[:, lo]) / cnt[None, :, None]
        outs.append((mean - ug) @ pool_w[g].astype(jnp.float32))
    return jnp.concatenate(outs, axis=-1) * pool_scale.astype(jnp.float32)


def _layer(x, p, norm_mix, w_in, ret_decay_fwd, ret_decay_bwd, ret_gn, hy_short_conv,
           hy_w1, hy_b1, hy_freq, hy_w2, hy_b2, hy_w3, hy_bias, pool_w, pool_scale, w_out,
           norm_ffn, ffn_w_up, ffn_conv, ffn_w_down, ple_w, ple_gate_w, ple_norm):
    B, L, _ = x.shape
    h = _rmsnorm(x, norm_mix)
    proj = h @ w_in
    R = RET_WIDTH
    q, k, v, g, hy_in, pool_in = jnp.split(proj, [R, 2 * R, 3 * R, 4 * R, 4 * R + 3 * HY_WIDTH], axis=-1)
    q = _rope(q.reshape(B, L, RET_HEADS, RET_HEAD_DIM)) * (RET_HEAD_DIM ** -0.5)
    k = _rope(k.reshape(B, L, RET_HEADS, RET_HEAD_DIM))
    v = v.reshape(B, L, RET_HEADS, RET_HEAD_DIM)
    ret = _retention(q, k, v, ret_decay_fwd, ret_decay_bwd)
    ret = _head_layernorm(ret, ret_gn).reshape(B, L, RET_WIDTH) * jax.nn.silu(g.astype(jnp.float32))
    hy_out = _hyena(hy_in, hy_short_conv, hy_w1, hy_b1, hy_freq, hy_w2, hy_b2, hy_w3, hy_bias)
    pool_out = _pool_mixer(pool_in, pool_w, pool_scale)
    mix = jnp.concatenate([ret, hy_out, pool_out], axis=-1).astype(x.dtype) @ w_out
    x = x + mix
    h = _rmsnorm(x, norm_ffn)
    gate, up = jnp.split(h @ ffn_w_up, 2, axis=-1)
    gate = _dwconv3(gate, ffn_conv)
    x = x + (jax.nn.gelu(gate) * up) @ ffn_w_down
    e = _rmsnorm(p @ ple_w, ple_norm)
    return x + jax.nn.sigmoid(x @ ple_gate_w) * e


def _trunk(x, p, layer_weights, norm_final):
    for i in range(DEPTH):
        x = _layer(x, p[i], *[w[i] for w in layer_weights])
    return _rmsnorm(x, norm_final)


def setup_inputs(seed: int = 0) -> dict:
    key = jax.random.key(seed)
    ks = jax.random.split(key, 32)
    f32 = jnp.float32
    nrm = lambda k, shape, s: (jax.random.normal(k, shape, f32) * s).astype(f32)
    gain = lambda k, shape: (1.0 + 0.05 * jax.random.normal(k, shape, f32)).astype(f32)
    base_logit = jnp.asarray(np.log(2.0 ** (5 + np.arange(RET_HEADS)) - 1.0), dtype=f32)
    return {
        "x_prompt": nrm(ks[0], (BATCH, SEQ, D_MODEL), 1.0),
        "x_sample": nrm(ks[1], (DEC_BATCH, DEC_SEQ, D_MODEL), 1.0),
        "p_prompt": nrm(ks[2], (DEPTH, BATCH, SEQ, PLE_DIM), 1.0),
        "p_sample": nrm(ks[3], (DEPTH, DEC_BATCH, DEC_SEQ, PLE_DIM), 1.0),
        "norm_mix": gain(ks[4], (DEPTH, D_MODEL)),
        "w_in": nrm(ks[5], (DEPTH, D_MODEL, IN_WIDTH), D_MODEL ** -0.5),
        "ret_decay_fwd": base_logit[None] + nrm(ks[6], (DEPTH, RET_HEADS), 0.1),
        "ret_decay_bwd": base_logit[None] + nrm(ks[7], (DEPTH, RET_HEADS), 0.1),
        "ret_gn": gain(ks[8], (DEPTH, RET_WIDTH)),
        "hy_short_conv": nrm(ks[9], (DEPTH, 3, 3 * HY_WIDTH), 3 ** -0.5),
        "hy_w1": nrm(ks[10], (DEPTH, HY_EMB_DIM, HY_FILTER_HIDDEN), HY_EMB_DIM ** -0.5),
        "hy_b1": nrm(ks[11], (DEPTH, HY_FILTER_HIDDEN), 0.1),
        "hy_freq": gain(ks[12], (DEPTH, HY_FILTER_HIDDEN)),
        "hy_w2": nrm(ks[13], (DEPTH, HY_FILTER_HIDDEN, HY_FILTER_HIDDEN), HY_FILTER_HIDDEN ** -0.5),
        "hy_b2": nrm(ks[14], (DEPTH, HY_FILTER_HIDDEN), 0.1),
        "hy_w3": nrm(ks[15], (DEPTH, HY_FILTER_HIDDEN, HY_ORDER * 2 * HY_WIDTH), HY_FILTER_HIDDEN ** -0.5),
        "hy_bias": nrm(ks[16], (DEPTH, HY_ORDER, HY_WIDTH), 1.0),
        "pool_w": nrm(ks[17], (DEPTH, POOL_GROUPS, POOL_GROUP_DIM, POOL_GROUP_DIM), POOL_GROUP_DIM ** -0.5),
        "pool_scale": (0.5 + 0.05 * jax.random.normal(ks[18], (DEPTH, POOL_WIDTH), f32)).astype(f32),
        "w_out": nrm(ks[19], (DEPTH, D_MODEL, D_MODEL), D_MODEL ** -0.5),
        "norm_ffn": gain(ks[20], (DEPTH, D_MODEL)),
        "ffn_w_up": nrm(ks[21], (DEPTH, D_MODEL, 2 * D_FF), D_MODEL ** -0.5),
        "ffn_conv": nrm(ks[22], (DEPTH, 3, D_FF), 3 ** -0.5),
        "ffn_w_down": nrm(ks[23], (DEPTH, D_FF, D_MODEL), D_FF ** -0.5),
        "ple_w": nrm(ks[24], (DEPTH, PLE_DIM, D_MODEL), PLE_DIM ** -0.5),
        "ple_gate_w": nrm(ks[25], (DEPTH, D_MODEL, D_MODEL), D_MODEL ** -0.5),
        "ple_norm": gain(ks[26], (DEPTH, D_MODEL)),
        "norm_final": gain(ks[27], (D_MODEL,)),
    }


def reference(x_prompt, x_sample, p_prompt, p_sample, norm_mix, w_in, ret_decay_fwd, ret_decay_bwd,
              ret_gn, hy_short_conv, hy_w1, hy_b1, hy_freq, hy_w2, hy_b2, hy_w3, hy_bias, pool_w,
              pool_scale, w_out, norm_ffn, ffn_w_up, ffn_conv, ffn_w_down, ple_w, ple_gate_w,
              ple_norm, norm_final):
    layer_weights = (norm_mix, w_in, ret_decay_fwd, ret_decay_bwd, ret_gn, hy_short_conv,
                     hy_w1, hy_b1, hy_freq, hy_w2, hy_b2, hy_w3, hy_bias, pool_w, pool_scale, w_out,
                     norm_ffn, ffn_w_up, ffn_conv, ffn_w_down, ple_w, ple_gate_w, ple_norm)
    y_prompt = _trunk(x_prompt, p_prompt, layer_weights, norm_final)
    y_sample = _trunk(x_sample, p_sample, layer_weights, norm_final)
    return (y_prompt, y_sample)
```

```python
import math
from contextlib import ExitStack
import numpy as np
import concourse.bass as bass
import concourse.mybir as mybir
from concourse.bass_utils import run_bass_kernel_spmd
from concourse.ap import AP

F32 = mybir.dt.float32
BF16 = mybir.dt.bfloat16
I32 = mybir.dt.int32
AF = mybir.ActivationFunctionType
ALU = mybir.AluOpType
AX = mybir.AxisListType

D = 1024
KC = 8
DEPTH_FULL = 4
L_FULL = 4096
NSLOT = 3
RW = 512
HYW = 256
INW = 3072
DFF = 2816
NJ = 22
EPS = 1e-6
PI = math.pi
HY_MIN_DECAY = math.log(1e-2) / 1.5
HY_MAX_DECAY = math.log(1e-2) / 0.3


class Sched:
    ENG = ('pe', 'act', 'dve', 'pool', 'sp')

    def __init__(s, nc, n_dma=40):
        s.nc = nc
        s.e = dict(pe=nc.tensor, act=nc.scalar, dve=nc.vector, pool=nc.gpsimd, sp=nc.sync)
        s.sem = {k: nc.alloc_semaphore("s_" + k) for k in ('pe', 'act', 'dve', 'pool')}
        s.cnt = {k: 0 for k in s.sem}
        s.dsem = [nc.alloc_semaphore("d%d" % i) for i in range(n_dma)]
        s.dcnt = [0] * n_dma
        s.drr = 0
        s.known = {k: {} for k in s.ENG}
        s.res = {}
        s.n_wait = 0
        s.n_ops = 0
        s.excl_ps = True

    def _semobj(s, name):
        return s.sem[name] if name in s.sem else s.dsem[name]

    def _wait(s, eng, name, val):
        if val <= 0 or s.known[eng].get(name, 0) >= val:
            return
        s.e[eng].wait_ge(s._semobj(name), val)
        s.known[eng][name] = val
        s.n_wait += 1

    def _deps(s, eng, reads, writes):
        best = {}
        for k in reads:
            r = s.res.get(k)
            if r and r[0]:
                n, v = r[0]
                if best.get(n, 0) < v:
                    best[n] = v
        for k in writes:
            r = s.res.get(k)
            if r:
                if r[0]:
                    n, v = r[0]
                    if best.get(n, 0) < v:
                        best[n] = v
                for n, v in r[1].items():
                    if best.get(n, 0) < v:
                        best[n] = v
        for n, v in best.items():
            if n == 'pe' and eng == 'pe':
                continue
            s._wait(eng, n, v)

    def _commit(s, ev, reads, writes):
        n, v = ev
        for k in reads:
            r = s.res.get(k)
            if r is None:
                r = s.res[k] = [None, {}]
            r[1][n] = v
        for k in writes:
            s.res[k] = [ev, {}]

    def op(s, eng, fn, r=(), w=()):
        if s.excl_ps:
            pr = [k for k in r if isinstance(k, tuple) and k[0] == 'ps']
            if pr:
                r = [k for k in r if k not in pr]
                w = list(w) + pr
        s._deps(eng, r, w)
        ins = fn(s.e[eng])
        s.cnt[eng] += 1
        ins.then_inc(s.sem[eng], 1)
        s._commit((eng, s.cnt[eng]), r, w)
        s.n_ops += 1

    def dma(s, q, out, in_, r=(), w=()):
        j = s.drr
        s.drr = (s.drr + 1) % len(s.dsem)
        s._wait(q, j, s.dcnt[j])
        s._deps(q, r, w)
        ins = s.e[q].dma_start(out=out, in_=in_)
        s.dcnt[j] += 16
        ins.then_inc(s.dsem[j], 16)
        s._commit((j, s.dcnt[j]), r, w)
        s.n_ops += 1

    def barrier(s, engs=None):
        for eng in (engs or s.ENG):
            for k in s.sem:
                s._wait(eng, k, s.cnt[k])
            for j in range(len(s.dsem)):
                s._wait(eng, j, s.dcnt[j])
        s.res = {}


def sb_ap(t, off, dims):
    pst = t[:].ap[0][0]
    return AP(t, off, [[pst, 128]] + [list(d) for d in dims])


class Ctx:
    pass


_UID = [0]


def uname(n):
    _UID[0] += 1
    return "%s_u%d" % (n, _UID[0])


ALL_STAGES = ('filt', 'tr', 'p1', 'ret', 'pool', 'hy', 'p3', 'p4', 'epi')


def build(L=L_FULL, NSEQ=NSLOT, DEPTH=DEPTH_FULL, taps=(), stages=ALL_STAGES):
    NT = L // 512
    NB = L // 128
    NLAG = 2 * NB - 1
    nc = bass.Bass("TRN2", target_bir_lowering=False)
    S = Sched(nc)
    c = Ctx()
    c.nc, c.S, c.L, c.NSEQ, c.DEPTH, c.NT, c.NB, c.NLAG = nc, S, L, NSEQ, DEPTH, NT, NB, NLAG
    c.taps = taps
    c.stages = stages

    def din(name, shape, dt=F32):
        return nc.dram_tensor(name, list(shape), dt, kind="ExternalInput")

    c.x = din("x", [NSEQ, L, D])
    c.p = din("p", [DEPTH, NSEQ, L, 256])
    c.norm_mix = din("norm_mix", [DEPTH, 128, KC])
    c.w_in = din("w_in", [DEPTH, D, INW])
    c.dec_f = din("ret_decay_fwd", [DEPTH, 4])
    c.dec_b = din("ret_decay_bwd", [DEPTH, 4])
    c.ret_gn = din("ret_gn", [DEPTH, RW])
    c.hy_conv = din("hy_short_conv", [DEPTH, 128, 6, 3])
    c.hy_w1 = din("hy_w1", [DEPTH, 33, 64])
    c.hy_b1 = din("hy_b1", [DEPTH, 64, 1])
    c.hy_freq = din("hy_freq", [DEPTH, 64, 1])
    c.hy_w2 = din("hy_w2", [DEPTH, 64, 64])
    c.hy_b2 = din("hy_b2", [DEPTH, 64, 1])
    c.hy_w3 = din("hy_w3", [DEPTH, 64, 1024])
    c.hy_bias = din("hy_bias", [DEPTH, 128, 4])
    c.pool_w = din("pool_w", [DEPTH, 4, 64, 64])
    c.pool_scale = din("pool_scale", [DEPTH, 128, 2])
    c.w_out = din("w_out", [DEPTH, D, D])
    c.norm_ffn = din("norm_ffn", [DEPTH, 128, KC])
    c.w_up = din("ffn_w_up", [DEPTH, D, 2 * DFF])
    c.ffn_conv = din("ffn_conv", [DEPTH, 128, NJ, 3])
    c.w_down = din("ffn_w_down", [DEPTH, DFF, D])
    c.ple_w = din("ple_w", [DEPTH, 256, D])
    c.ple_gate = din("ple_gate_w", [DEPTH, D, D])
    c.ple_norm = din("ple_norm", [DEPTH, 128, KC])
    c.norm_final = din("norm_final", [1, D])
    c.k_ident = din("k_ident", [128, 128])
    c.k_J = din("k_J", [128, 128])
    c.k_rot = din("k_rot", [128, 128])
    c.k_cos = din("k_cos", [128, L])
    c.k_sin = din("k_sin", [128, L])
    c.k_feats = din("k_feats", [2, 33, L])
    c.k_negdelta = din("k_negdelta", [128, 2])
    c.k_df = din("k_df", [128, 128])
    c.k_db = din("k_db", [128, 128])
    c.k_kvec = din("k_kvec", [128, 2])
    c.k_qrow = din("k_qrow", [2, 128])
    c.k_invcnt = din("k_invcnt", [1, 64])
    c.y = nc.dram_tensor("y", [NSEQ, L, D], F32, kind="ExternalOutput")

    def dscr(name, shape, dt):
        return nc.dram_tensor(name, list(shape), dt)

    c.XT = dscr("XT", [NSEQ, D, L], F32)
    c.PT = dscr("PT", [DEPTH, NSEQ, 256, L], BF16)
    c.QT = dscr("QT", [NSEQ, RW, L], BF16)
    c.KT = dscr("KT", [NSEQ, RW, L], BF16)
    c.V = dscr("V", [NSEQ, L, RW], BF16)
    c.GS = dscr("GS", [NSEQ, L, RW], BF16)
    c.HYT = dscr("HYT", [NSEQ, 768, L], F32)
    c.PLT = dscr("PLT", [NSEQ, 256, L], F32)
    c.MIXT = dscr("MIXT", [NSEQ, D, L], BF16)
    c.HID = dscr("HID", [NSEQ, DFF, L], BF16)
    c.X1T = dscr("X1T", [NSEQ, D, L], F32)
    c.G = dscr("G", [DEPTH, 512, 2 * L], BF16)
    c.dbg = {}
    for nm in taps:
        t_ = getattr(c, nm)
        c.dbg[nm] = nc.dram_tensor("dbg_" + nm, list(t_.shape), t_.dtype, kind="ExternalOutput")

    c.ps = [nc.alloc_psum_tensor("ps%d" % i, [128, 512], F32) for i in range(8)]

    with ExitStack() as gst:
        def galloc(name, shape, dt):
            return gst.enter_context(nc.sbuf_tensor(uname(name), list(shape), dt))
        c.ident_f = galloc("ident_f", [128, 128], F32)
        c.ident_b = galloc("ident_b", [128, 128], BF16)
        c.J_b = galloc("J_b", [128, 128], BF16)
        c.rot_b = galloc("rot_b", [128, 128], BF16)
        c.ones_b = galloc("ones_b", [128, 128], BF16)
        stg = galloc("cstage", [128, 128], F32)
        S.dma('sp', c.ident_f[:], c.k_ident.ap(), w=['ident_f'])
        S.op('dve', lambda e: e.tensor_copy(out=c.ident_b[:], in_=c.ident_f[:]), r=['ident_f'], w=['ident_b'])
        S.dma('sp', stg[:], c.k_J.ap(), w=['cstage'])
        S.op('dve', lambda e: e.tensor_copy(out=c.J_b[:], in_=stg[:]), r=['cstage'], w=['J_b'])
        S.dma('sp', stg[:], c.k_rot.ap(), w=['cstage'])
        S.op('dve', lambda e: e.tensor_copy(out=c.rot_b[:], in_=stg[:]), r=['cstage'], w=['rot_b'])
        S.op('dve', lambda e: e.memset(c.ones_b[:], 1.0), w=['ones_b'])
        S.barrier()

        st_ = c.stages
        if 'filt' in st_:
            prologue_filters(c)
        if 'tr' in st_:
            prologue_transposes(c)
        for l in range(DEPTH):
            if 'p1' in st_:
                pass_p1(c, l)
            if 'ret' in st_:
                pass_ret(c, l)
            if 'pool' in st_:
                pass_pool(c, l)
            if 'hy' in st_:
                pass_hyena(c, l)
            if 'p3' in st_:
                pass_p3(c, l)
            if 'p4' in st_:
                pass_p4(c, l)
        if 'epi' in st_:
            epilogue(c)
        S.barrier()
        for nm in taps:
            S.dma('sp', c.dbg[nm].ap(), getattr(c, nm).ap())
        S.barrier(['sp', 'pool'])
    return nc, c


def prologue_transposes(c):
    nc, S, L, NSEQ, DEPTH, NT = c.nc, c.S, c.L, c.NSEQ, c.DEPTH, c.NT
    with ExitStack() as st:
        def alloc(name, shape, dt):
            return st.enter_context(nc.sbuf_tensor(uname(name), list(shape), dt))
        xin = [alloc("pt_xin%d" % i, [128, 4, D], F32) for i in range(2)]
        xst = [alloc("pt_xst%d" % i, [128, KC, 512], F32) for i in range(2)]
        pin = [alloc("pt_pin%d" % i, [128, 4, 256], F32) for i in range(2)]
        pbf = [alloc("pt_pbf%d" % i, [128, 4, 256], BF16) for i in range(2)]
        pst = [alloc("pt_pst%d" % i, [128, 2, 512], BF16) for i in range(2)]
        it = 0
        for s in range(NSEQ):
            for i in range(NT):
                b = it % 2
                it += 1
                src = c.x.ap()[s].rearrange("(j p) f -> p j f", p=128)[:, 4 * i:4 * i + 4, :]
                S.dma('sp', xin[b][:], src, w=[('xin', b)])
                g = 0
                for jb in range(4):
                    for half in range(2):
                        bank = 1 + (g % 4)
                        g += 1
                        for q in range(4):
                            kc = half * 4 + q
                            S.op('pe', lambda e, bank=bank, q=q, kc=kc, jb=jb, b=b: e.transpose(
                                c.ps[bank][:, q * 128:(q + 1) * 128], xin[b][:, jb, kc * 128:(kc + 1) * 128], c.ident_f[:]),
                                r=[('xin', b)], w=[('ps', bank)])
                        eng = 'dve' if (g % 2) else 'act'
                        if eng == 'dve':
                            S.op('dve', lambda e, bank=bank, half=half, jb=jb, b=b: e.tensor_copy(
                                out=xst[b][:, half * 4:half * 4 + 4, jb * 128:(jb + 1) * 128],
                                in_=c.ps[bank][:].rearrange("p (q t) -> p q t", q=4)),
                                r=[('ps', bank)], w=[('xst', b)])
                        else:
                            S.op('act', lambda e, bank=bank, half=half, jb=jb, b=b: e.activation(
                                out=xst[b][:, half * 4:half * 4 + 4, jb * 128:(jb + 1) * 128],
                                in_=c.ps[bank][:].rearrange("p (q t) -> p q t", q=4), func=AF.Copy),
                                r=[('ps', bank)], w=[('xst', b)])
                dst = c.XT.ap()[s].rearrange("(k p) t -> p k t", p=128)[:, :, i * 512:(i + 1) * 512]
                S.dma('pool', dst, xst[b][:], r=[('xst', b)], w=[('XT', s, i)])
        it = 0
        for l in range(DEPTH):
            for s in range(NSEQ):
                for i in range(NT):
                    b = it % 2
                    it += 1
                    src = c.p.ap()[l, s].rearrange("(j p) f -> p j f", p=128)[:, 4 * i:4 * i + 4, :]
                    S.dma('sp', pin[b][:], src, w=[('pin', b)])
                    S.op('dve', lambda e, b=b: e.tensor_copy(out=pbf[b][:], in_=pin[b][:]), r=[('pin', b)], w=[('pbf', b)])
                    for kc in range(2):
                        bank = 5 + kc
                        psb = c.ps[bank][:].bitcast(BF16)
                        for jb in range(4):
                            S.op('pe', lambda e, psb=psb, jb=jb, kc=kc, b=b: e.transpose(
                                psb[:, jb * 128:(jb + 1) * 128], pbf[b][:, jb, kc * 128:(kc + 1) * 128], c.ident_b[:]),
                                r=[('pbf', b)], w=[('ps', bank)])
                        S.op('act', lambda e, psb=psb, kc=kc, b=b: e.activation(
                            out=pst[b][:, kc, :], in_=psb[:, 0:512], func=AF.Copy),
                            r=[('ps', bank)], w=[('pst', b)])
                    dst = c.PT.ap()[l, s].rearrange("(k p) t -> p k t", p=128)[:, :, i * 512:(i + 1) * 512]
                    S.dma('pool', dst, pst[b][:], r=[('pst', b)], w=[('PT', l, s, i)])
        S.barrier()


def load_weight_bf16(c, st, name, src_rows_ap, nk, ncols, key, stage_cols=1024, rowscale=None, rskey=None):
    nc, S = c.nc, c.S
    wb = st.enter_context(nc.sbuf_tensor(uname(name), [128, nk, ncols], BF16))
    if not hasattr(c, 'wstage'):
        raise RuntimeError("wstage missing")
    src = src_rows_ap.rearrange("(k p) n -> p k n", p=128)
    idx = 0
    for k in range(nk):
        for c0 in range(0, ncols, stage_cols):
            cw = min(stage_cols, ncols - c0)
            b = c.wstage_i % 2
            c.wstage_i += 1
            S.dma('sp', c.wstage[b][:, 0:cw], src[:, k, c0:c0 + cw], w=[('wstage', b)])
            eng = ('dve', 'pool', 'act')[idx % 3]
            idx += 1
            rk = [('wstage', b)] + ([rskey] if rskey else [])
            if rowscale is None:
                if eng == 'act':
                    S.op('act', lambda e, b=b, k=k, c0=c0, cw=cw: e.activation(
                        out=wb[:, k, c0:c0 + cw], in_=c.wstage[b][:, 0:cw], func=AF.Copy), r=rk, w=[key])
                else:
                    S.op(eng, lambda e, b=b, k=k, c0=c0, cw=cw: e.tensor_copy(
                        out=wb[:, k, c0:c0 + cw], in_=c.wstage[b][:, 0:cw]), r=rk, w=[key])
            else:
                if eng == 'act':
                    S.op('act', lambda e, b=b, k=k, c0=c0, cw=cw: e.activation(
                        out=wb[:, k, c0:c0 + cw], in_=c.wstage[b][:, 0:cw], func=AF.Copy, scale=rowscale[:, k:k + 1]), r=rk, w=[key])
                else:
                    S.op(eng, lambda e, b=b, k=k, c0=c0, cw=cw: e.tensor_scalar(
                        out=wb[:, k, c0:c0 + cw], in0=c.wstage[b][:, 0:cw], scalar1=rowscale[:, k:k + 1], scalar2=None, op0=ALU.mult),
                        r=rk, w=[key])
    return wb


def alloc_wstage(c, st, cols=1024):
    c.wstage = [st.enter_context(c.nc.sbuf_tensor(uname("wstage%d" % i), [128, cols], F32)) for i in range(2)]
    c.wstage_i = 0


def rms_rstd(c, ps_bank, rs, n, key_ps, key_rs, dim):
    S = c.S
    S.op('act', lambda e: e.activation(out=rs[:, 0:n], in_=c.ps[ps_bank][:, 0:n], func=AF.Sqrt,
                                       bias=EPS, scale=1.0 / dim), r=[key_ps], w=[key_rs])
    S.op('dve', lambda e: e.reciprocal(out=rs[:, 0:n], in_=rs[:, 0:n]), r=[key_rs], w=[key_rs])


def pass_p1(c, l):
    nc, S, L, NSEQ, NT = c.nc, c.S, c.L, c.NSEQ, c.NT
    with ExitStack() as st:
        def alloc(name, shape, dt):
            return st.enter_context(nc.sbuf_tensor(uname(name), list(shape), dt))
        alloc_wstage(c, st)
        gm = alloc("p1_gm", [128, KC], F32)
        S.dma('sp', gm[:], c.norm_mix.ap()[l], w=['gm'])
        Wb = load_weight_bf16(c, st, "p1_W", c.w_in.ap()[l], KC, INW, 'Wb', rowscale=gm, rskey='gm')
        cos = alloc("p1_cos", [128, L], F32)
        sin = alloc("p1_sin", [128, L], F32)
        gn = alloc("p1_gn", [128, RW], F32)
        S.dma('sp', cos[:], c.k_cos.ap(), w=['cos'])
        S.dma('sp', sin[:], c.k_sin.ap(), w=['sin'])
        S.dma('sp', gn[:], c.ret_gn.ap()[l:l + 1, :].partition_broadcast(128), w=['gn'])
        xt = alloc("p1_xt", [128, KC, 512], F32)
        xsq = alloc("p1_xsq", [128, KC, 512], BF16)
        rs = alloc("p1_rs", [128, 512], F32)
        h = [alloc("p1_h%d" % i, [128, KC, 512], BF16) for i in range(2)]
        qb = [alloc("p1_qb%d" % i, [128, 512], BF16) for i in range(2)]
        t1 = [alloc("p1_t1%d" % i, [128, 512], F32) for i in range(2)]
        t2 = [alloc("p1_t2%d" % i, [128, 512], F32) for i in range(2)]
        sl = [alloc("p1_sl%d" % i, [128, 512], F32) for i in range(2)]
        qk_out = alloc("p1_qko", [128, 8, 512], BF16)
        hy_out = alloc("p1_hyo", [128, 6, 512], F32)
        pl_out = alloc("p1_plo", [128, 2, 512], F32)
        v_out = alloc("p1_vo", [128, 4, 512], BF16)
        gs_out = alloc("p1_gso", [128, 4, 512], BF16)
        scl_q = 128.0 ** -0.5
        it = 0
        rr = 0
        for s in range(NSEQ):
            for i in range(NT):
                t0 = i * 512
                hb = it % 2
                it += 1
                S.dma('sp', xt[:], c.XT.ap()[s].rearrange("(k p) t -> p k t", p=128)[:, :, t0:t0 + 512],
                      r=[('XT', s, i)], w=['xt'])
                S.op('act', lambda e: e.activation(out=xsq[:], in_=xt[:], func=AF.Square), r=['xt'], w=['xsq'])
                for k in range(KC):
                    S.op('pe', lambda e, k=k: e.matmul(c.ps[0][:], lhsT=c.ones_b[:], rhs=xsq[:, k, :],
                                                       start=(k == 0), stop=(k == KC - 1)),
                         r=['xsq', 'ones_b'], w=[('ps', 0)])
                rms_rstd(c, 0, rs, 512, ('ps', 0), 'rs', D)
                for k in range(KC):
                    eng = 'dve' if k % 2 == 0 else 'pool'
                    S.op(eng, lambda e, k=k, hb=hb: e.tensor_tensor(
                        out=h[hb][:, k, :], in0=xt[:, k, :], in1=rs[:], op=ALU.mult), r=['xt', 'rs'], w=[('h', hb)])
                mt = [('q', j, j * 128) for j in range(4)] + [('k', j, 512 + j * 128) for j in range(4)] + \
                     [('hy', j, 2048 + j * 128) for j in range(6)] + [('pl', j, 2816 + j * 128) for j in range(2)]
                for kind, j, col in mt:
                    bank = 1 + (rr % 4)
                    rr += 1
                    for k in range(KC):
                        S.op('pe', lambda e, k=k, bank=bank, col=col, hb=hb: e.matmul(
                            c.ps[bank][:], lhsT=Wb[:, k, col:col + 128], rhs=h[hb][:, k, :],
                            start=(k == 0), stop=(k == KC - 1)), r=['Wb', ('h', hb)], w=[('ps', bank)])
                    if kind in ('q', 'k'):
                        b2 = rr % 2
                        rb = 5 + b2
                        oi = j if kind == 'q' else 4 + j
                        sc = scl_q if kind == 'q' else 1.0
                        S.op('act', lambda e, bank=bank, b2=b2: e.activation(out=qb[b2][:], in_=c.ps[bank][:], func=AF.Copy),
                             r=[('ps', bank)], w=[('qb', b2)])
                        S.op('pe', lambda e, rb=rb, b2=b2: e.matmul(c.ps[rb][:], lhsT=c.rot_b[:], rhs=qb[b2][:],
                                                                   start=True, stop=True),
                             r=[('qb', b2), 'rot_b'], w=[('ps', rb)])
                        S.op('dve', lambda e, bank=bank, b2=b2, sc=sc: e.scalar_tensor_tensor(
                            out=t1[b2][:], in0=c.ps[bank][:], scalar=sc, in1=cos[:, t0:t0 + 512],
                            op0=ALU.mult, op1=ALU.mult), r=[('ps', bank), 'cos'], w=[('t1', b2)])
                        S.op('dve', lambda e, rb=rb, b2=b2, sc=sc: e.scalar_tensor_tensor(
                            out=t2[b2][:], in0=c.ps[rb][:], scalar=sc, in1=sin[:, t0:t0 + 512],
                            op0=ALU.mult, op1=ALU.mult), r=[('ps', rb), 'sin'], w=[('t2', b2)])
                        S.op('pool', lambda e, b2=b2, oi=oi: e.tensor_tensor(
                            out=qk_out[:, oi, :], in0=t1[b2][:], in1=t2[b2][:], op=ALU.add),
                            r=[('t1', b2), ('t2', b2)], w=['qk_out'])
                    elif kind == 'hy':
                        S.op('act', lambda e, bank=bank, j=j: e.activation(out=hy_out[:, j, :], in_=c.ps[bank][:], func=AF.Copy),
                             r=[('ps', bank)], w=['hy_out'])
                    else:
                        S.op('dve', lambda e, bank=bank, j=j: e.tensor_copy(out=pl_out[:, j, :], in_=c.ps[bank][:]),
                             r=[('ps', bank)], w=['pl_out'])
                for j in range(4):
                    bank = 1 + (rr % 4)
                    rr += 1
                    for k in range(KC):
                        S.op('pe', lambda e, k=k, bank=bank, j=j, hb=hb: e.matmul(
                            c.ps[bank][:], lhsT=h[hb][:, k, j * 128:(j + 1) * 128], rhs=Wb[:, k, 1024:1536],
                            start=(k == 0), stop=(k == KC - 1)), r=['Wb', ('h', hb)], w=[('ps', bank)])
                    S.op('dve', lambda e, bank=bank, j=j: e.tensor_copy(out=v_out[:, j, :], in_=c.ps[bank][:]),
                         r=[('ps', bank)], w=['v_out'])
                    bank = 1 + (rr % 4)
                    rr += 1
                    b2 = rr % 2
                    for k in range(KC):
                        S.op('pe', lambda e, k=k, bank=bank, j=j, hb=hb: e.matmul(
                            c.ps[bank][:], lhsT=h[hb][:, k, j * 128:(j + 1) * 128], rhs=Wb[:, k, 1536:2048],
                            start=(k == 0), stop=(k == KC - 1)), r=['Wb', ('h', hb)], w=[('ps', bank)])
                    S.op('act', lambda e, bank=bank, b2=b2: e.activation(out=sl[b2][:], in_=c.ps[bank][:], func=AF.Silu),
                         r=[('ps', bank)], w=[('sl', b2)])
                    S.op('pool', lambda e, b2=b2, j=j: e.tensor_tensor(out=gs_out[:, j, :], in0=sl[b2][:], in1=gn[:], op=ALU.mult),
                         r=[('sl', b2), 'gn'], w=['gs_out'])
                fm = lambda T, n: T.ap()[s].rearrange("(k p) t -> p k t", p=128)[:, 0:n, t0:t0 + 512]
                tm = lambda T: T.ap()[s].rearrange("(j p) f -> p j f", p=128)[:, 4 * i:4 * i + 4, :]
                S.dma('pool', fm(c.QT, 4), qk_out[:, 0:4, :], r=['qk_out'], w=[('QT', s, i)])
                S.dma('pool', fm(c.KT, 4), qk_out[:, 4:8, :], r=['qk_out'], w=[('KT', s, i)])
                S.dma('pool', fm(c.HYT, 6), hy_out[:], r=['hy_out'], w=[('HYT', s, i)])
                S.dma('pool', fm(c.PLT, 2), pl_out[:], r=['pl_out'], w=[('PLT', s, i)])
                S.dma('pool', tm(c.V), v_out[:], r=['v_out'], w=[('V', s, i)])
                S.dma('pool', tm(c.GS), gs_out[:], r=['gs_out'], w=[('GS', s, i)])
        S.barrier()


def pass_ret(c, l):
    nc, S, L, NSEQ, NT, NB = c.nc, c.S, c.L, c.NSEQ, c.NT, c.NB
    with ExitStack() as st:
        def alloc(name, shape, dt):
            return st.enter_context(nc.sbuf_tensor(uname(name), list(shape), dt))
        lgr = alloc("rt_lgr", [128, 8], F32)
        lg = alloc("rt_lg", [128, 8], F32)
        df = alloc("rt_df", [128, 128], F32)
        db = alloc("rt_db", [128, 128], F32)
        kvec = alloc("rt_kvec", [128, 2], F32)
        qrow = alloc("rt_qrow", [128, 2, 128], F32)
        mask = alloc("rt_mask", [128, 4, 128], F32)
        mtmp = alloc("rt_mtmp", [128, 128], F32)
        wk = alloc("rt_wk", [128, 2, 4], F32)
        wq = alloc("rt_wq", [128, 2, 4, 128], F32)
        decv = alloc("rt_decv", [128, 8], F32)
        S.dma('sp', lgr[:, 0:4], c.dec_f.ap()[l:l + 1, :].partition_broadcast(128), w=['lgr'])
        S.dma('sp', lgr[:, 4:8], c.dec_b.ap()[l:l + 1, :].partition_broadcast(128), w=['lgr'])
        S.dma('sp', df[:], c.k_df.ap(), w=['df'])
        S.dma('sp', db[:], c.k_db.ap(), w=['db'])
        S.dma('sp', kvec[:], c.k_kvec.ap(), w=['kvec'])
        S.dma('sp', qrow[:, 0, :], c.k_qrow.ap()[0:1, :].partition_broadcast(128), w=['qrow'])
        S.dma('sp', qrow[:, 1, :], c.k_qrow.ap()[1:2, :].partition_broadcast(128), w=['qrow'])
        S.op('act', lambda e: e.activation(out=lg[:], in_=lgr[:], func=AF.Exp, scale=-1.0), r=['lgr'], w=['lg'])
        S.op('act', lambda e: e.activation(out=lg[:], in_=lg[:], func=AF.Ln, bias=1.0, scale=1.0), r=['lg'], w=['lg'])
        S.op('dve', lambda e: e.tensor_scalar(out=lg[:], in0=lg[:], scalar1=-1.0, scalar2=None, op0=ALU.mult), r=['lg'], w=['lg'])
        for hh in range(4):
            S.op('dve', lambda e, hh=hh: e.tensor_scalar(out=mtmp[:], in0=df[:], scalar1=lg[:, hh:hh + 1], scalar2=None,
                                                         op0=ALU.mult), r=['df', 'lg'], w=['mtmp'])
            S.op('dve', lambda e, hh=hh: e.scalar_tensor_tensor(out=mtmp[:], in0=db[:], scalar=lg[:, 4 + hh:5 + hh], in1=mtmp[:],
                                                                op0=ALU.mult, op1=ALU.add), r=['db', 'lg', 'mtmp'], w=['mtmp'])
            S.op('act', lambda e, hh=hh: e.activation(out=mask[:, hh, :], in_=mtmp[:], func=AF.Exp), r=['mtmp'], w=['mask'])
            for d in range(2):
                S.op('act', lambda e, hh=hh, d=d: e.activation(out=wq[:, d, hh, :], in_=qrow[:, d, :], func=AF.Exp,
                                                               scale=lg[:, 4 * d + hh:4 * d + hh + 1]),
                     r=['qrow', 'lg'], w=['wq'])
        for d in range(2):
            S.op('act', lambda e, d=d: e.activation(out=wk[:, d, :], in_=lg[:, 4 * d:4 * d + 4], func=AF.Exp,
                                                    scale=kvec[:, d:d + 1]), r=['kvec', 'lg'], w=['wk'])
        S.op('act', lambda e: e.activation(out=decv[:], in_=lg[:], func=AF.Exp, scale=128.0), r=['lg'], w=['decv'])

        QTs = alloc("rt_QT", [128, 4, L], BF16)
        KTs = alloc("rt_KT", [128, 4, L], BF16)
        Vs = alloc("rt_V", [128, NB, RW], BF16)
        SB = alloc("rt_SB", [128, NB, 4, 128], BF16)
        stt = alloc("rt_st", [128, 4, 128], F32)
        stb = alloc("rt_stb", [128, 4, 128], BF16)
        ktw = [alloc("rt_ktw%d" % i, [128, 4, 128], BF16) for i in range(2)]
        PT_ = [alloc("rt_PT%d" % i, [128, 4, 128], BF16) for i in range(2)]
        qf = [alloc("rt_qf%d" % i, [128, 4, 512], BF16) for i in range(2)]
        qbk = [alloc("rt_qbk%d" % i, [128, 4, 512], BF16) for i in range(2)]
        GSt = [alloc("rt_GS%d" % i, [128, 4, RW], BF16) for i in range(2)]
        osq = alloc("rt_osq", [128, 4, 128], F32)
        sm = alloc("rt_sm", [128, 8], F32)
        sm2 = alloc("rt_sm2", [128, 20], F32)
        tmpo = [alloc("rt_tmpo%d" % i, [128, 4, 128], F32) for i in range(2)]
        rtok = [alloc("rt_rtok%d" % i, [128, 4, 128], BF16) for i in range(2)]
        mixst = [alloc("rt_mix%d" % i, [128, 4, 512], BF16) for i in range(2)]
        psT = c.ps[0][:].bitcast(BF16)
        psR = c.ps[5][:].bitcast(BF16)

        def bc4(t, off):
            return sb_ap(t, off, [[1, 4], [0, 128]])

        def k_transposes(n, w_dir, kb):
            for hh in range(4):
                S.op('pe', lambda e, hh=hh: e.transpose(psT[:, hh * 128:(hh + 1) * 128], KTs[:, hh, n * 128:(n + 1) * 128],
                                                        c.ident_b[:]), r=['KTs', 'ident_b'], w=[('ps', 0)])
            S.op('dve', lambda e: e.tensor_tensor(out=ktw[kb][:], in0=psT[:, 0:512].rearrange("p (h t) -> p h t", h=4),
                                                  in1=bc4(wk, w_dir * 4), op=ALU.mult),
                 r=[('ps', 0), 'wk'], w=[('ktw', kb)])

        def kv_update(n, d, kb):
            for hh in range(4):
                S.op('pe', lambda e, hh=hh: e.matmul(c.ps[4][:, hh * 128:(hh + 1) * 128], lhsT=ktw[kb][:, hh, :],
                                                     rhs=Vs[:, n, hh * 128:(hh + 1) * 128], start=True, stop=True),
                     r=[('ktw', kb), 'Vs'], w=[('ps', 4)])
            S.op('pool', lambda e: e.tensor_tensor(out=stt[:], in0=stt[:], in1=bc4(decv, d * 4), op=ALU.mult),
                 r=['stt', 'decv'], w=['stt'])
            S.op('dve', lambda e: e.tensor_tensor(out=stt[:], in0=stt[:], in1=c.ps[4][:].rearrange("p (h t) -> p h t", h=4),
                                                  op=ALU.add), r=['stt', ('ps', 4)], w=['stt'])

        it = 0
        ck = 0
        for s in range(NSEQ):
            allt = [(nm, s, i) for nm in ('QT',) for i in range(NT)]
            S.dma('sp', QTs[:], c.QT.ap()[s].rearrange("(h p) t -> p h t", p=128), r=[('QT', s, i) for i in range(NT)], w=['QTs'])
            S.dma('sp', KTs[:], c.KT.ap()[s].rearrange("(h p) t -> p h t", p=128), r=[('KT', s, i) for i in range(NT)], w=['KTs'])
            S.dma('sp', Vs[:], c.V.ap()[s].rearrange("(n p) f -> p n f", p=128), r=[('V', s, i) for i in range(NT)], w=['Vs'])
            S.op('pool', lambda e: e.memset(stt[:], 0.0), w=['stt'])
            for n in range(NB - 1, -1, -1):
                S.op('act', lambda e, n=n: e.activation(out=SB[:, n, :, :], in_=stt[:], func=AF.Copy), r=['stt'], w=['SB'])
                if n > 0:
                    kb = ck % 2
                    ck += 1
                    k_transposes(n, 1, kb)
                    kv_update(n, 1, kb)
            S.op('pool', lambda e: e.memset(stt[:], 0.0), w=['stt'])
            S.op('pool', lambda e: e.memset(stb[:], 0.0), w=['stb'])
            for i in range(NT):
                t0 = i * 512
                tb = it % 2
                it += 1
                S.dma('sp', GSt[tb][:], c.GS.ap()[s].rearrange("(j p) f -> p j f", p=128)[:, 4 * i:4 * i + 4, :],
                      r=[('GS', s, i)], w=[('GSt', tb)])
                for d, dst in ((0, qf), (1, qbk)):
                    eng = 'dve' if d == 0 else 'pool'
                    S.op(eng, lambda e, d=d, dst=dst: e.tensor_tensor(
                        out=dst[tb][:].rearrange("p h (j t) -> p h j t", j=4),
                        in0=QTs[:, :, t0:t0 + 512].rearrange("p h (j t) -> p h j t", j=4),
                        in1=sb_ap(wq, d * 512, [[128, 4], [0, 4], [1, 128]]), op=ALU.mult),
                        r=['QTs', 'wq'], w=[(('qf', 'qbk')[d], tb)])
                for j in range(4):
                    n = 4 * i + j
                    kb = ck % 2
                    ck += 1
                    ob = 2 + (ck % 2)
                    k_transposes(n, 0, kb)
                    for hh in range(4):
                        S.op('pe', lambda e, hh=hh, n=n: e.matmul(c.ps[1][:, hh * 128:(hh + 1) * 128],
                                                                 lhsT=KTs[:, hh, n * 128:(n + 1) * 128],
                                                                 rhs=QTs[:, hh, n * 128:(n + 1) * 128], start=True, stop=True),
                             r=['KTs', 'QTs'], w=[('ps', 1)])
                    S.op('dve', lambda e, kb=kb: e.tensor_tensor(out=PT_[kb][:], in0=c.ps[1][:].rearrange("p (h t) -> p h t", h=4),
                                                                 in1=mask[:], op=ALU.mult), r=[('ps', 1), 'mask'], w=[('PT_', kb)])
                    for hh in range(4):
                        o_ap = c.ps[ob][:, hh * 128:(hh + 1) * 128]
                        S.op('pe', lambda e, hh=hh, n=n, o_ap=o_ap, kb=kb: e.matmul(o_ap, lhsT=PT_[kb][:, hh, :],
                                                                                  rhs=Vs[:, n, hh * 128:(hh + 1) * 128],
                                                                                  start=True, stop=False),
                             r=[('PT_', kb), 'Vs'], w=[('ps', ob)])
                        S.op('pe', lambda e, hh=hh, j=j, o_ap=o_ap: e.matmul(o_ap, lhsT=qf[tb][:, hh, j * 128:(j + 1) * 128],
                                                                           rhs=stb[:, hh, :], start=False, stop=False),
                             r=[('qf', tb), 'stb'], w=[('ps', ob)])
                        S.op('pe', lambda e, hh=hh, j=j, n=n, o_ap=o_ap: e.matmul(o_ap, lhsT=qbk[tb][:, hh, j * 128:(j + 1) * 128],
                                                                                rhs=SB[:, n, hh, :], start=False, stop=True),
                             r=[('qbk', tb), 'SB'], w=[('ps', ob)])
                    if n < NB - 1:
                        kv_update(n, 0, kb)
                        S.op('act', lambda e: e.activation(out=stb[:], in_=stt[:], func=AF.Copy), r=['stt'], w=['stb'])
                    o3 = c.ps[ob][:].rearrange("p (h t) -> p h t", h=4)
                    ub = ck % 2
                    S.op('dve', lambda e, o3=o3: e.tensor_reduce(out=sm[:, 0:4], in_=o3, axis=AX.X, op=ALU.add),
                         r=[('ps', ob)], w=['sm'])
                    S.op('act', lambda e, o3=o3: e.activation(out=osq[:], in_=o3, func=AF.Square), r=[('ps', ob)], w=['osq'])
                    S.op('dve', lambda e: e.tensor_reduce(out=sm[:, 4:8], in_=osq[:], axis=AX.X, op=ALU.add), r=['osq'], w=['sm'])
                    S.op('dve', lambda e: e.tensor_scalar(out=sm2[:, 0:4], in0=sm[:, 0:4], scalar1=-1.0 / 128, scalar2=None,
                                                          op0=ALU.mult), r=['sm'], w=['sm2'])
                    S.op('dve', lambda e: e.tensor_tensor(out=sm2[:, 4:8], in0=sm2[:, 0:4], in1=sm2[:, 0:4], op=ALU.mult),
                         r=['sm2'], w=['sm2'])
                    S.op('dve', lambda e: e.scalar_tensor_tensor(out=sm2[:, 8:12], in0=sm[:, 4:8], scalar=1.0 / 128, in1=sm2[:, 4:8],
                                                                 op0=ALU.mult, op1=ALU.subtract), r=['sm', 'sm2'], w=['sm2'])
                    S.op('act', lambda e: e.activation(out=sm2[:, 12:16], in_=sm2[:, 8:12], func=AF.Sqrt, bias=EPS, scale=1.0),
                         r=['sm2'], w=['sm2'])
                    S.op('dve', lambda e: e.reciprocal(out=sm2[:, 16:20], in_=sm2[:, 12:16]), r=['sm2'], w=['sm2'])
                    S.op('dve', lambda e, o3=o3, ub=ub: e.tensor_tensor(out=tmpo[ub][:], in0=o3, in1=bc4(sm2, 0), op=ALU.add),
                         r=[('ps', ob), 'sm2'], w=[('tmpo', ub)])
                    S.op('pool', lambda e, ub=ub: e.tensor_tensor(out=tmpo[ub][:], in0=tmpo[ub][:], in1=bc4(sm2, 16), op=ALU.mult),
                         r=[('tmpo', ub), 'sm2'], w=[('tmpo', ub)])
                    S.op('pool', lambda e, ub=ub, j=j: e.tensor_tensor(
                        out=rtok[ub][:], in0=tmpo[ub][:], in1=GSt[tb][:, j, :].rearrange("p (h t) -> p h t", h=4), op=ALU.mult),
                        r=[('tmpo', ub), ('GSt', tb)], w=[('rtok', ub)])
                    for hh in range(4):
                        S.op('pe', lambda e, hh=hh, ub=ub: e.transpose(psR[:, hh * 128:(hh + 1) * 128], rtok[ub][:, hh, :], c.ident_b[:]),
                             r=[('rtok', ub), 'ident_b'], w=[('ps', 5)])
                    S.op('act', lambda e, j=j: e.activation(out=mixst[tb][:, :, j * 128:(j + 1) * 128],
                                                            in_=psR[:, 0:512].rearrange("p (h t) -> p h t", h=4), func=AF.Copy),
                         r=[('ps', 5)], w=[('mixst', tb)])
                S.dma('pool', c.MIXT.ap()[s].rearrange("(k p) t -> p k t", p=128)[:, 0:4, t0:t0 + 512], mixst[tb][:],
                      r=[('mixst', tb)], w=[('MIXT_r', s, i)])
        S.barrier()


POOL_WINDOWS = (2, 4, 8, 16)


def pass_pool(c, l):
    nc, S, L, NSEQ, NT = c.nc, c.S, c.L, c.NSEQ, c.NT
    LP = L + 16
    with ExitStack() as st:
        def alloc(name, shape, dt):
            return st.enter_context(nc.sbuf_tensor(uname(name), list(shape), dt))
        U = alloc("pl_U", [128, LP], F32)
        W2 = alloc("pl_W2", [128, LP], F32)
        W4 = alloc("pl_W4", [128, LP], F32)
        W8 = alloc("pl_W8", [128, LP], F32)
        W16 = alloc("pl_W16", [128, LP], F32)
        Wn = {2: W2, 4: W4, 8: W8, 16: W16}
        M = alloc("pl_M", [128, L], F32)
        Mb = alloc("pl_Mb", [128, L], BF16)
        O = alloc("pl_O", [128, L], BF16)
        ic = alloc("pl_ic", [128, 4, 16], F32)
        wst = alloc("pl_wst", [128, 2, 128], F32)
        wpb = alloc("pl_wpb", [128, 2, 128], BF16)
        psc = alloc("pl_psc", [128, 2], F32)
        S.dma('sp', ic[:].rearrange("p a b -> p (a b)"), c.k_invcnt.ap().partition_broadcast(128), w=['ic'])
        S.dma('sp', psc[:], c.pool_scale.ap()[l], w=['psc'])
        S.op('dve', lambda e: e.memset(wst[:], 0.0), w=['wst'])
        for g in range(4):
            ct, hf = g // 2, g % 2
            S.dma('sp', wst[hf * 64:(hf + 1) * 64, ct, hf * 64:(hf + 1) * 64], c.pool_w.ap()[l, g], w=['wst'])
        S.op('dve', lambda e: e.tensor_copy(out=wpb[:], in_=wst[:]), r=['wst'], w=['wpb'])
        S.op('pool', lambda e: e.memset(U[:], 0.0), w=['U'])
        rr = 0
        for s in range(NSEQ):
            for ct in range(2):
                S.dma('sp', U[:, 8:8 + L], c.PLT.ap()[s, ct * 128:(ct + 1) * 128, :], r=[('PLT', s, i) for i in range(NT)], w=['U'])
                S.op('dve', lambda e: e.tensor_tensor(out=W2[:, 1:LP], in0=U[:, 0:LP - 1], in1=U[:, 1:LP], op=ALU.add),
                     r=['U'], w=['W2'])
                S.op('pool', lambda e: e.tensor_tensor(out=W4[:, 2:LP - 1], in0=W2[:, 1:LP - 2], in1=W2[:, 3:LP], op=ALU.add),
                     r=['W2'], w=['W4'])
                S.op('dve', lambda e: e.tensor_tensor(out=W8[:, 4:LP - 3], in0=W4[:, 2:LP - 5], in1=W4[:, 6:LP - 1], op=ALU.add),
                     r=['W4'], w=['W8'])
                S.op('pool', lambda e: e.tensor_tensor(out=W16[:, 8:LP - 7], in0=W8[:, 4:LP - 11], in1=W8[:, 12:LP - 3], op=ALU.add),
                     r=['W8'], w=['W16'])
                for hf in range(2):
                    g = ct * 2 + hf
                    w = POOL_WINDOWS[g]
                    Wt = Wn[w]
                    p0, p1 = hf * 64, (hf + 1) * 64
                    eng = 'dve' if hf == 0 else 'pool'
                    S.op('dve', lambda e, Wt=Wt, w=w, p0=p0, p1=p1: e.scalar_tensor_tensor(
                        out=M[p0:p1, :], in0=Wt[p0:p1, 8:8 + L], scalar=1.0 / w, in1=U[p0:p1, 8:8 + L],
                        op0=ALU.mult, op1=ALU.subtract), r=['W%d' % w, 'U'], w=['M'])
                    for (a0, io) in ((0, 0), (L - 8, 8)):
                        S.op(eng, lambda e, Wt=Wt, p0=p0, p1=p1, a0=a0, io=io, g=g: e.tensor_tensor(
                            out=M[p0:p1, a0:a0 + 8], in0=Wt[p0:p1, 8 + a0:16 + a0], in1=ic[p0:p1, g, io:io + 8], op=ALU.mult),
                            r=['W%d' % w, 'ic', 'M'], w=['M'])
                        S.op(eng, lambda e, p0=p0, p1=p1, a0=a0: e.tensor_tensor(
                            out=M[p0:p1, a0:a0 + 8], in0=M[p0:p1, a0:a0 + 8], in1=U[p0:p1, 8 + a0:16 + a0], op=ALU.subtract),
                            r=['U', 'M'], w=['M'])
                S.op('act', lambda e: e.activation(out=Mb[:], in_=M[:], func=AF.Copy), r=['M'], w=['Mb'])
                for i in range(NT):
                    bank = 1 + (rr % 4)
                    rr += 1
                    S.op('pe', lambda e, bank=bank, i=i, ct=ct: e.matmul(c.ps[bank][:], lhsT=wpb[:, ct, :], rhs=Mb[:, i * 512:(i + 1) * 512],
                                                                    start=True, stop=True), r=['wpb', 'Mb'], w=[('ps', bank)])
                    S.op('act', lambda e, bank=bank, i=i, ct=ct: e.activation(out=O[:, i * 512:(i + 1) * 512], in_=c.ps[bank][:],
                                                                         func=AF.Copy, scale=psc[:, ct:ct + 1]),
                         r=[('ps', bank), 'psc'], w=['O'])
                S.dma('pool', c.MIXT.ap()[s, 768 + ct * 128:768 + (ct + 1) * 128, :], O[:], r=['O'], w=[('MIXT_p', s, ct)])
        S.barrier()


def prologue_filters(c):
    nc, S, L, DEPTH, NT = c.nc, c.S, c.L, c.DEPTH, c.NT
    for l in range(DEPTH):
        with ExitStack() as st:
            def alloc(name, shape, dt):
                return st.enter_context(nc.sbuf_tensor(uname(name), list(shape), dt))
            w1 = alloc("hf_w1", [33, 64], F32)
            w2 = alloc("hf_w2", [64, 64], F32)
            w3 = alloc("hf_w3", [64, 1024], F32)
            b1 = alloc("hf_b1", [64, 1], F32)
            b2 = alloc("hf_b2", [64, 1], F32)
            fr = alloc("hf_fr", [64, 1], F32)
            fb = alloc("hf_fb", [64, 2], F32)
            negd = alloc("hf_negd", [128, 2], F32)
            bias = alloc("hf_bias", [128, 4], F32)
            S.dma('sp', w1[:], c.hy_w1.ap()[l], w=['w1'])
            S.dma('sp', w2[:], c.hy_w2.ap()[l], w=['w2'])
            S.dma('sp', w3[:], c.hy_w3.ap()[l], w=['w3'])
            S.dma('sp', b1[:], c.hy_b1.ap()[l], w=['b1'])
            S.dma('sp', b2[:], c.hy_b2.ap()[l], w=['b2'])
            S.dma('sp', fr[:], c.hy_freq.ap()[l], w=['fr'])
            S.dma('sp', negd[:], c.k_negdelta.ap(), w=['negd'])
            S.dma('sp', bias[:], c.hy_bias.ap()[l], w=['bias'])
            S.op('dve', lambda e: e.tensor_tensor(out=fb[:, 0:1], in0=b1[:], in1=fr[:], op=ALU.mult), r=['b1', 'fr'], w=['fb'])
            S.op('dve', lambda e: e.tensor_tensor(out=fb[:, 1:2], in0=b2[:], in1=fr[:], op=ALU.mult), r=['b2', 'fr', 'fb'], w=['fb'])
            FB = [[[alloc("hf_FB%d%d%d" % (g, o, ct), [128, L], F32) for ct in range(2)] for o in range(2)] for g in range(2)]
            feats = alloc("hf_feats", [33, 512], F32)
            tb = alloc("hf_tb", [128, 512], F32)
            a_sb = alloc("hf_a", [64, 512], F32)
            ki = alloc("hf_ki", [64, 512], I32)
            rr_ = alloc("hf_r", [64, 512], F32)
            h1 = alloc("hf_h1", [64, 512], F32)
            h2 = alloc("hf_h2", [64, 512], F32)
            dec = [alloc("hf_dec%d" % i, [128, 512], F32) for i in range(2)]
            asum = alloc("hf_asum", [128, 8], F32)
            tot = alloc("hf_tot", [128, 4], F32)
            stg = [alloc("hf_stg%d" % i, [128, L], BF16) for i in range(2)]

            def sin_layer(psb, fcol, dst, dkey):
                S.op('dve', lambda e: e.tensor_scalar(out=a_sb[:], in0=c.ps[psb][0:64, :], scalar1=fr[:, 0:1], scalar2=fb[:, fcol:fcol + 1],
                                                      op0=ALU.mult, op1=ALU.add), r=[('ps', psb), 'fr', 'fb'], w=['a_sb'])
                S.op('dve', lambda e: e.tensor_scalar(out=ki[:], in0=a_sb[:], scalar1=float(1.0 / (2 * PI)), scalar2=None, op0=ALU.mult),
                     r=['a_sb'], w=['ki'])
                S.op('dve', lambda e: e.scalar_tensor_tensor(out=rr_[:], in0=ki[:], scalar=float(-2 * PI), in1=a_sb[:],
                                                             op0=ALU.mult, op1=ALU.add), r=['ki', 'a_sb'], w=['rr_'])
                S.op('dve', lambda e: e.tensor_scalar(out=rr_[:], in0=rr_[:], scalar1=-3.141592, scalar2=3.141592,
                                                      op0=ALU.max, op1=ALU.min), r=['rr_'], w=['rr_'])
                S.op('act', lambda e: e.activation(out=dst[:], in_=rr_[:], func=AF.Sin), r=['rr_'], w=[dkey])

            rb = 0
            for g in range(2):
                for i in range(NT):
                    S.dma('sp', feats[:], c.k_feats.ap()[g, :, i * 512:(i + 1) * 512], w=['feats'])
                    S.dma('sp', tb[:], c.k_feats.ap()[g, 0:1, i * 512:(i + 1) * 512].partition_broadcast(128), w=['tb'])
                    S.op('pe', lambda e: e.matmul(c.ps[0][0:64, :], lhsT=w1[:], rhs=feats[:], start=True, stop=True),
                         r=['w1', 'feats'], w=[('ps', 0)])
                    sin_layer(0, 0, h1, 'h1')
                    S.op('pe', lambda e: e.matmul(c.ps[1][0:64, :], lhsT=w2[:], rhs=h1[:], start=True, stop=True),
                         r=['w2', 'h1'], w=[('ps', 1)])
                    sin_layer(1, 1, h2, 'h2')
                    for ct in range(2):
                        S.op('act', lambda e, ct=ct: e.activation(out=dec[ct][:], in_=tb[:], func=AF.Exp, scale=negd[:, ct:ct + 1]),
                             r=['tb', 'negd'], w=[('dec', ct)])
                    for o in range(2):
                        for ct in range(2):
                            col = o * 512 + g * 256 + ct * 128
                            bank = 2 + (rb % 4)
                            rb += 1
                            S.op('pe', lambda e, bank=bank, col=col: e.matmul(c.ps[bank][:], lhsT=w3[:, col:col + 128], rhs=h2[:],
                                                                             start=True, stop=True), r=['w3', 'h2'], w=[('ps', bank)])
                            S.op('dve', lambda e, bank=bank, g=g, o=o, ct=ct, i=i: e.tensor_tensor(
                                out=FB[g][o][ct][:, i * 512:(i + 1) * 512], in0=c.ps[bank][:], in1=dec[ct][:], op=ALU.mult),
                                r=[('ps', bank), ('dec', ct)], w=[('FB', g, o, ct)])
            for g in range(2):
                n = L if g == 0 else L - 1
                for o in range(2):
                    for ct in range(2):
                        idx = g * 4 + o * 2 + ct
                        S.op('dve', lambda e, g=g, o=o, ct=ct, idx=idx, n=n: e.tensor_reduce(
                            out=asum[:, idx:idx + 1], in_=FB[g][o][ct][:, 0:n], axis=AX.X, op=ALU.add, apply_absolute_value=True),
                            r=[('FB', g, o, ct)], w=['asum'])
            S.op('dve', lambda e: e.tensor_tensor(out=tot[:], in0=asum[:, 0:4], in1=asum[:, 4:8], op=ALU.add), r=['asum'], w=['tot'])
            S.op('dve', lambda e: e.reciprocal(out=tot[:], in_=tot[:]), r=['tot'], w=['tot'])
            sb_i = 0
            for o in range(2):
                for ct in range(2):
                    oc = o * 2 + ct
                    S.op('dve', lambda e, o=o, ct=ct, oc=oc: e.tensor_scalar(out=FB[0][o][ct][:], in0=FB[0][o][ct][:], scalar1=tot[:, oc:oc + 1],
                                                                          scalar2=None, op0=ALU.mult), r=[('FB', 0, o, ct), 'tot'], w=[('FB', 0, o, ct)])
                    S.op('dve', lambda e, o=o, ct=ct, oc=oc: e.tensor_tensor(out=FB[0][o][ct][:, 0:1], in0=FB[0][o][ct][:, 0:1], in1=bias[:, oc:oc + 1],
                                                                          op=ALU.add), r=[('FB', 0, o, ct), 'bias'], w=[('FB', 0, o, ct)])
                    rows = c.G.ap()[l, o * 256 + ct * 128:o * 256 + (ct + 1) * 128, :]
                    b = sb_i % 2
                    sb_i += 1
                    S.op('act', lambda e, o=o, ct=ct, b=b: e.activation(out=stg[b][:], in_=FB[0][o][ct][:], func=AF.Copy),
                         r=[('FB', 0, o, ct)], w=[('stg', b)])
                    S.dma('pool', rows[:, L - 1:2 * L - 1], stg[b][:], r=[('stg', b)], w=[('G', l, o, ct, 0)])
                    b = sb_i % 2
                    sb_i += 1
                    S.op('pool', lambda e, o=o, ct=ct, oc=oc, b=b: e.tensor_scalar(out=stg[b][:], in0=FB[1][o][ct][:], scalar1=tot[:, oc:oc + 1],
                                                                               scalar2=None, op0=ALU.mult), r=[('FB', 1, o, ct), 'tot'], w=[('stg', b)])
                    S.dma('pool', rows[:, 0:L - 1], stg[b][:, 0:L - 1], r=[('stg', b)], w=[('G', l, o, ct, 1)])
            S.barrier()


def pass_hyena(c, l):
    nc, S, L, NSEQ, NT, NB, NLAG = c.nc, c.S, c.L, c.NSEQ, c.NT, c.NB, c.NLAG
    SN = NSEQ * NB
    gsz = max(1, min(128, 512 // SN))
    ngrp = (128 + gsz - 1) // gsz
    for ct in range(2):
        with ExitStack() as st:
            def alloc(name, shape, dt):
                return st.enter_context(nc.sbuf_tensor(uname(name), list(shape), dt))
            cw = alloc("hy_cw", [128, 6, 3], F32)
            S.dma('sp', cw[:], c.hy_conv.ap()[l], w=['cw'])
            X = [alloc("hy_X%d" % i, [128, L + 2], F32) for i in range(2)]
            acc1 = alloc("hy_acc1", [128, L], F32)
            acc2 = alloc("hy_acc2", [128, L], F32)
            ub = [alloc("hy_ub%d" % i, [128, L], BF16) for i in range(2)]
            UR = alloc("hy_UR", [128, 128, NSEQ, NB], BF16)
            HX1 = alloc("hy_HX1", [128, 128, NSEQ, NB], BF16)
            HX2 = alloc("hy_HX2", [128, 128, NSEQ, NB], BF16)
            KS = [alloc("hy_KS%d" % i, [128, NLAG * 128], BF16) for i in range(2)]
            TM = [UR, HX1, HX2]
            for b in range(2):
                S.op('pool', lambda e, b=b: e.memset(X[b][:, 0:1], 0.0), w=[('X', b)])
                S.op('pool', lambda e, b=b: e.memset(X[b][:, L + 1:L + 2], 0.0), w=[('X', b)])
            it = 0
            tg = 0
            for s in range(NSEQ):
                for r_ in range(3):
                    b = it % 2
                    it += 1
                    tile = r_ * 2 + ct
                    rows = r_ * 256 + ct * 128
                    S.dma('sp', X[b][:, 1:L + 1], c.HYT.ap()[s, rows:rows + 128, :], r=[('HYT', s, i) for i in range(NT)], w=[('X', b)])
                    S.op('act', lambda e, b=b, tile=tile: e.activation(out=acc1[:], in_=X[b][:, 1:L + 1], func=AF.Copy,
                                                                     scale=cw[:, tile, 1:2]), r=[('X', b), 'cw'], w=['acc1'])
                    S.op('dve', lambda e, b=b, tile=tile: e.scalar_tensor_tensor(out=acc2[:], in0=X[b][:, 0:L], scalar=cw[:, tile, 0:1],
                                                                               in1=acc1[:], op0=ALU.mult, op1=ALU.add),
                         r=[('X', b), 'cw', 'acc1'], w=['acc2'])
                    if r_ == 0:
                        o_ap = sb_ap(ub[b], L - 1, [[-1, L]])
                    else:
                        o_ap = ub[b][:]
                    S.op('dve', lambda e, b=b, tile=tile, o_ap=o_ap: e.scalar_tensor_tensor(
                        out=o_ap, in0=X[b][:, 2:L + 2], scalar=cw[:, tile, 2:3], in1=acc2[:], op0=ALU.mult, op1=ALU.add),
                        r=[('X', b), 'cw', 'acc2'], w=[('ub', b)])
                    for g8 in range(NB // 8):
                        bank = 6 + (tg % 2)
                        tg += 1
                        psb = c.ps[bank][:].bitcast(BF16)
                        for q in range(8):
                            blk = g8 * 8 + q
                            S.op('pe', lambda e, psb=psb, q=q, blk=blk, b=b: e.transpose(psb[:, q * 128:(q + 1) * 128],
                                                                                      ub[b][:, blk * 128:(blk + 1) * 128], c.ident_b[:]),
                                 r=[('ub', b), 'ident_b'], w=[('ps', bank)])
                        if r_ == 0:
                            a_first = NB - 1 - g8 * 8
                            dst = sb_ap(TM[0], s * NB + a_first, [[-1, 8], [SN, 128]])
                        else:
                            dst = sb_ap(TM[r_], s * NB + g8 * 8, [[1, 8], [SN, 128]])
                        eng = 'act' if (tg % 2) else 'dve'
                        rkeys = [('ps', bank)]
                        wkeys = [('TM', r_, gi) for gi in range(ngrp)]
                        if eng == 'act':
                            S.op('act', lambda e, dst=dst, psb=psb: e.activation(out=dst, in_=psb.rearrange("p (q t) -> p q t", q=8), func=AF.Copy),
                                 r=rkeys, w=wkeys)
                        else:
                            S.op('dve', lambda e, dst=dst, psb=psb: e.tensor_copy(out=dst, in_=psb.rearrange("p (q t) -> p q t", q=8)),
                                 r=rkeys, w=wkeys)
            kc_i = 0
            for o in range(2):
                GB = HX1 if o == 0 else HX2
                gb_i = 1 if o == 0 else 2
                for gi in range(ngrp):
                    c0 = gi * gsz
                    n_c = min(gsz, 128 - c0)
                    bank = gi % 2
                    first = True
                    for ci in range(n_c):
                        ch = c0 + ci
                        kb = kc_i % 2
                        kc_i += 1
                        src = AP(c.G, ((l * 512 + o * 256 + ct * 128 + ch) * 2 * L), [[1, 128], [1, NLAG * 128]])
                        S.dma('sp', KS[kb][:], src, r=[('G', l, o, ct, 0), ('G', l, o, ct, 1)], w=[('KS', kb)])
                        for d in range(-(NB - 1), NB):
                            a0 = max(0, -d)
                            a1 = min(NB, NB - d)
                            n = a1 - a0
                            o_ap = sb_ap(c.ps[bank], ci * SN + a0 + d, [[NB, NSEQ], [1, n]])
                            r_ap = sb_ap(UR, ch * SN + a0, [[NB, NSEQ], [1, n]])
                            S.op('pe', lambda e, o_ap=o_ap, r_ap=r_ap, kb=kb, d=d, first=first: e.matmul(
                                o_ap, lhsT=KS[kb][:, (d + NB - 1) * 128:(d + NB) * 128], rhs=r_ap,
                                start=first, stop=False, skip_group_check=True),
                                r=[('KS', kb), ('TM', 0, gi)], w=[('ps', bank)])
                            first = False
                    ncol = n_c * SN
                    gflat = sb_ap(GB, c0 * SN, [[1, ncol]])
                    S.op('dve', lambda e, gflat=gflat, bank=bank, ncol=ncol: e.tensor_tensor(
                        out=gflat, in0=c.ps[bank][:, 0:ncol], in1=gflat, op=ALU.mult),
                        r=[('ps', bank), ('TM', gb_i, gi)], w=[('TM', gb_i, gi)])
                    if o == 0:
                        zb = 2 + (gi % 2)
                        S.op('pe', lambda e, zb=zb, gflat=gflat, ncol=ncol: e.matmul(c.ps[zb][:, 0:ncol], lhsT=c.J_b[:], rhs=gflat,
                                                                                    start=True, stop=True),
                             r=[('TM', 1, gi), 'J_b'], w=[('ps', zb)])
                        uflat = sb_ap(UR, c0 * SN, [[1, ncol]])
                        S.op('act', lambda e, zb=zb, uflat=uflat, ncol=ncol: e.activation(out=uflat, in_=c.ps[zb][:, 0:ncol], func=AF.Copy),
                             r=[('ps', zb)], w=[('TM', 0, gi)])
            ost = ub
            it = 0
            for s in range(NSEQ):
                b = it % 2
                it += 1
                for g8 in range(NB // 8):
                    bank = 6 + (tg % 2)
                    tg += 1
                    psb = c.ps[bank][:].bitcast(BF16)
                    for q in range(8):
                        a = g8 * 8 + q
                        i_ap = sb_ap(HX2, s * NB + a, [[SN, 128]])
                        S.op('pe', lambda e, psb=psb, q=q, i_ap=i_ap: e.transpose(psb[:, q * 128:(q + 1) * 128], i_ap, c.ident_b[:]),
                             r=[('TM', 2, gi) for gi in range(ngrp)] + ['ident_b'], w=[('ps', bank)])
                    S.op('act', lambda e, psb=psb, g8=g8, b=b: e.activation(out=ost[b][:, g8 * 1024:(g8 + 1) * 1024], in_=psb, func=AF.Copy),
                         r=[('ps', bank)], w=[('ub', b)])
                S.dma('pool', c.MIXT.ap()[s, 512 + ct * 128:512 + (ct + 1) * 128, :], ost[b][:], r=[('ub', b)], w=[('MIXT_h', s, ct)])
            S.barrier()


def pass_p3(c, l):
    nc, S, L, NSEQ = c.nc, c.S, c.L, c.NSEQ
    TW = 510
    tiles = [(a, min(a + TW, L)) for a in range(0, L, TW)]
    with ExitStack() as st:
        def alloc(name, shape, dt):
            return st.enter_context(nc.sbuf_tensor(uname(name), list(shape), dt))
        alloc_wstage(c, st)
        Wo = load_weight_bf16(c, st, "p3_Wo", c.w_out.ap()[l], KC, D, 'Wo')
        gmf = alloc("p3_gm", [128, KC], F32)
        S.dma('sp', gmf[:], c.norm_ffn.ap()[l], w=['gmf'])
        Wu = load_weight_bf16(c, st, "p3_Wu", c.w_up.ap()[l], KC, 2 * DFF, 'Wu', rowscale=gmf, rskey='gmf')
        fcw = alloc("p3_fcw", [128, NJ, 3], F32)
        S.dma('sp', fcw[:], c.ffn_conv.ap()[l], w=['fcw'])
        mx = alloc("p3_mx", [128, KC, 512], BF16)
        xt = alloc("p3_xt", [128, KC, 512], F32)
        xsq = alloc("p3_xsq", [128, KC, 512], BF16)
        rs = alloc("p3_rs", [128, 512], F32)
        h2 = alloc("p3_h2", [128, KC, 512], BF16)
        hid = alloc("p3_hid", [128, NJ, 512], BF16)
        acc = [alloc("p3_acc%d" % i, [128, 512], F32) for i in range(2)]
        gl = [alloc("p3_gl%d" % i, [128, 512], F32) for i in range(2)]
        rr = 0
        jj = 0
        for s in range(NSEQ):
            for (ta, tb_) in tiles:
                ntok = tb_ - ta
                ncol = ntok + 2
                lo = 1 if ta == 0 else 0
                hi = ncol - 1 if tb_ == L else ncol
                fm = lambda T: T.ap()[s].rearrange("(k p) t -> p k t", p=128)
                S.dma('sp', mx[:, :, lo:hi], fm(c.MIXT)[:, :, ta - 1 + lo:ta - 1 + hi], w=['mx'])
                S.dma('sp', xt[:, :, lo:hi], fm(c.XT)[:, :, ta - 1 + lo:ta - 1 + hi], w=['xt'])
                if lo == 1:
                    S.op('pool', lambda e: e.memset(mx[:, :, 0:1], 0.0), w=['mx'])
                    S.op('pool', lambda e: e.memset(xt[:, :, 0:1], 0.0), w=['xt'])
                if hi == ncol - 1:
                    S.op('pool', lambda e, ncol=ncol: e.memset(mx[:, :, ncol - 1:ncol], 0.0), w=['mx'])
                    S.op('pool', lambda e, ncol=ncol: e.memset(xt[:, :, ncol - 1:ncol], 0.0), w=['xt'])
                for m in range(KC):
                    bank = 1 + (rr % 4)
                    rr += 1
                    for k in range(KC):
                        S.op('pe', lambda e, k=k, m=m, bank=bank, ncol=ncol: e.matmul(
                            c.ps[bank][:, 0:ncol], lhsT=Wo[:, k, m * 128:(m + 1) * 128], rhs=mx[:, k, 0:ncol],
                            start=(k == 0), stop=(k == KC - 1)), r=['Wo', 'mx'], w=[('ps', bank)])
                    S.op('dve', lambda e, m=m, bank=bank, ncol=ncol: e.tensor_tensor(
                        out=xt[:, m, 0:ncol], in0=c.ps[bank][:, 0:ncol], in1=xt[:, m, 0:ncol], op=ALU.add),
                        r=[('ps', bank), 'xt'], w=['xt'])
                S.dma('pool', fm(c.X1T)[:, :, ta:tb_], xt[:, :, 1:1 + ntok], r=['xt'], w=[('X1T', s, ta)])
                S.op('act', lambda e, ncol=ncol: e.activation(out=xsq[:, :, 0:ncol], in_=xt[:, :, 0:ncol], func=AF.Square),
                     r=['xt'], w=['xsq'])
                for k in range(KC):
                    S.op('pe', lambda e, k=k, ncol=ncol: e.matmul(c.ps[0][:, 0:ncol], lhsT=c.ones_b[:], rhs=xsq[:, k, 0:ncol],
                                                                  start=(k == 0), stop=(k == KC - 1)),
                         r=['xsq', 'ones_b'], w=[('ps', 0)])
                rms_rstd(c, 0, rs, ncol, ('ps', 0), 'rs', D)
                for k in range(KC):
                    eng = 'dve' if k % 2 == 0 else 'pool'
                    S.op(eng, lambda e, k=k, ncol=ncol: e.tensor_tensor(
                        out=h2[:, k, 0:ncol], in0=xt[:, k, 0:ncol], in1=rs[:, 0:ncol], op=ALU.mult), r=['xt', 'rs'], w=['h2'])
                for j in range(NJ):
                    ab = jj % 2
                    jj += 1
                    bg = 1 + (rr % 4)
                    rr += 1
                    bu = 1 + (rr % 4)
                    rr += 1
                    for k in range(KC):
                        S.op('pe', lambda e, k=k, j=j, bg=bg, ncol=ncol: e.matmul(
                            c.ps[bg][:, 0:ncol], lhsT=Wu[:, k, j * 128:(j + 1) * 128], rhs=h2[:, k, 0:ncol],
                            start=(k == 0), stop=(k == KC - 1)), r=['Wu', 'h2'], w=[('ps', bg)])
                    for k in range(KC):
                        S.op('pe', lambda e, k=k, j=j, bu=bu, ncol=ncol: e.matmul(
                            c.ps[bu][:, 0:ncol], lhsT=Wu[:, k, DFF + j * 128:DFF + (j + 1) * 128], rhs=h2[:, k, 0:ncol],
                            start=(k == 0), stop=(k == KC - 1)), r=['Wu', 'h2'], w=[('ps', bu)])
                    S.op('act', lambda e, j=j, bg=bg, ab=ab, ntok=ntok: e.activation(
                        out=acc[ab][:, 0:ntok], in_=c.ps[bg][:, 1:1 + ntok], func=AF.Copy, scale=fcw[:, j, 1:2]),
                        r=[('ps', bg), 'fcw'], w=[('acc', ab)])
                    S.op('dve', lambda e, j=j, bg=bg, ab=ab, ntok=ntok: e.scalar_tensor_tensor(
                        out=acc[ab][:, 0:ntok], in0=c.ps[bg][:, 0:ntok], scalar=fcw[:, j, 0:1], in1=acc[ab][:, 0:ntok],
                        op0=ALU.mult, op1=ALU.add), r=[('ps', bg), 'fcw', ('acc', ab)], w=[('acc', ab)])
                    S.op('dve', lambda e, j=j, bg=bg, ab=ab, ntok=ntok: e.scalar_tensor_tensor(
                        out=acc[ab][:, 0:ntok], in0=c.ps[bg][:, 2:2 + ntok], scalar=fcw[:, j, 2:3], in1=acc[ab][:, 0:ntok],
                        op0=ALU.mult, op1=ALU.add), r=[('ps', bg), 'fcw', ('acc', ab)], w=[('acc', ab)])
                    S.op('act', lambda e, ab=ab, ntok=ntok: e.activation(out=gl[ab][:, 0:ntok], in_=acc[ab][:, 0:ntok],
                                                                        func=AF.Gelu_apprx_tanh), r=[('acc', ab)], w=[('gl', ab)])
                    S.op('dve', lambda e, j=j, bu=bu, ab=ab, ntok=ntok: e.tensor_tensor(
                        out=hid[:, j, 0:ntok], in0=c.ps[bu][:, 1:1 + ntok], in1=gl[ab][:, 0:ntok], op=ALU.mult),
                        r=[('ps', bu), ('gl', ab)], w=['hid'])
                S.dma('pool', c.HID.ap()[s].rearrange("(j p) t -> p j t", p=128)[:, :, ta:tb_], hid[:, :, 0:ntok],
                      r=['hid'], w=[('HID', s, ta)])
        S.barrier()


def pass_p4(c, l):
    nc, S, L, NSEQ, NT = c.nc, c.S, c.L, c.NSEQ, c.NT
    with ExitStack() as st:
        def alloc(name, shape, dt):
            return st.enter_context(nc.sbuf_tensor(uname(name), list(shape), dt))
        alloc_wstage(c, st)
        Wd = load_weight_bf16(c, st, "p4_Wd", c.w_down.ap()[l], NJ, D, 'Wd')
        Wg = load_weight_bf16(c, st, "p4_Wg", c.ple_gate.ap()[l], KC, D, 'Wg')
        Wp = load_weight_bf16(c, st, "p4_Wp", c.ple_w.ap()[l], 2, D, 'Wp')
        pn = alloc("p4_pn", [128, KC], F32)
        S.dma('sp', pn[:], c.ple_norm.ap()[l], w=['pn'])
        hid = alloc("p4_hid", [128, NJ, 512], BF16)
        x1 = alloc("p4_x1", [128, KC, 512], F32)
        pT = alloc("p4_pT", [128, 2, 512], BF16)
        x2b = alloc("p4_x2b", [128, KC, 512], BF16)
        er = alloc("p4_er", [128, KC, 512], F32)
        esq = alloc("p4_esq", [128, KC, 512], BF16)
        sg = alloc("p4_sg", [128, KC, 512], BF16)
        rs = alloc("p4_rs", [128, 512], F32)
        tmp = [alloc("p4_tmp%d" % i, [128, 512], F32) for i in range(2)]
        rr = 0
        for s in range(NSEQ):
            for i in range(NT):
                t0 = i * 512
                fm = lambda T: T.ap()[s].rearrange("(k p) t -> p k t", p=128)[:, :, t0:t0 + 512]
                S.dma('sp', hid[:], c.HID.ap()[s].rearrange("(j p) t -> p j t", p=128)[:, :, t0:t0 + 512], w=['hid'])
                S.dma('sp', x1[:], fm(c.X1T), w=['x1'])
                S.dma('sp', pT[:], c.PT.ap()[l, s].rearrange("(k p) t -> p k t", p=128)[:, :, t0:t0 + 512], w=['pT'])
                for m in range(KC):
                    bank = 1 + (rr % 4)
                    rr += 1
                    for j in range(NJ):
                        S.op('pe', lambda e, j=j, m=m, bank=bank: e.matmul(
                            c.ps[bank][:], lhsT=Wd[:, j, m * 128:(m + 1) * 128], rhs=hid[:, j, :],
                            start=(j == 0), stop=(j == NJ - 1)), r=['Wd', 'hid'], w=[('ps', bank)])
                    S.op('dve', lambda e, m=m, bank=bank: e.tensor_tensor(out=x1[:, m, :], in0=c.ps[bank][:], in1=x1[:, m, :], op=ALU.add),
                         r=[('ps', bank), 'x1'], w=['x1'])
                    S.op('pool', lambda e, m=m: e.tensor_copy(out=x2b[:, m, :], in_=x1[:, m, :]), r=['x1'], w=['x2b'])
                for m in range(KC):
                    bank = 1 + (rr % 4)
                    rr += 1
                    for k in range(2):
                        S.op('pe', lambda e, k=k, m=m, bank=bank: e.matmul(
                            c.ps[bank][:], lhsT=Wp[:, k, m * 128:(m + 1) * 128], rhs=pT[:, k, :],
                            start=(k == 0), stop=(k == 1)), r=['Wp', 'pT'], w=[('ps', bank)])
                    S.op('act', lambda e, m=m, bank=bank: e.activation(out=er[:, m, :], in_=c.ps[bank][:], func=AF.Copy),
                         r=[('ps', bank)], w=['er'])
                    S.op('act', lambda e, m=m, bank=bank: e.activation(out=esq[:, m, :], in_=c.ps[bank][:], func=AF.Square),
                         r=[('ps', bank)], w=['esq'])
                for k in range(KC):
                    S.op('pe', lambda e, k=k: e.matmul(c.ps[0][:], lhsT=c.ones_b[:], rhs=esq[:, k, :],
                                                       start=(k == 0), stop=(k == KC - 1)), r=['esq', 'ones_b'], w=[('ps', 0)])
                rms_rstd(c, 0, rs, 512, ('ps', 0), 'rs', D)
                for m in range(KC):
                    bank = 1 + (rr % 4)
                    rr += 1
                    for k in range(KC):
                        S.op('pe', lambda e, k=k, m=m, bank=bank: e.matmul(
                            c.ps[bank][:], lhsT=Wg[:, k, m * 128:(m + 1) * 128], rhs=x2b[:, k, :],
                            start=(k == 0), stop=(k == KC - 1)), r=['Wg', 'x2b'], w=[('ps', bank)])
                    S.op('act', lambda e, m=m, bank=bank: e.activation(out=sg[:, m, :], in_=c.ps[bank][:], func=AF.Sigmoid),
                         r=[('ps', bank)], w=['sg'])
                    tb_ = m % 2
                    S.op('dve', lambda e, m=m, tb_=tb_: e.scalar_tensor_tensor(
                        out=tmp[tb_][:], in0=er[:, m, :], scalar=pn[:, m:m + 1], in1=rs[:], op0=ALU.mult, op1=ALU.mult),
                        r=['er', 'pn', 'rs'], w=[('tmp', tb_)])
                    S.op('pool', lambda e, m=m, tb_=tb_: e.tensor_tensor(out=tmp[tb_][:], in0=tmp[tb_][:], in1=sg[:, m, :], op=ALU.mult),
                         r=[('tmp', tb_), 'sg'], w=[('tmp', tb_)])
                    S.op('pool', lambda e, m=m, tb_=tb_: e.tensor_tensor(out=x1[:, m, :], in0=x1[:, m, :], in1=tmp[tb_][:], op=ALU.add),
                         r=[('tmp', tb_), 'x1'], w=['x1'])
                S.dma('pool', fm(c.XT), x1[:], r=['x1'], w=[('XT', s, i)])
        S.barrier()


def epilogue(c):
    nc, S, L, NSEQ, NT = c.nc, c.S, c.L, c.NSEQ, c.NT
    with ExitStack() as st:
        def alloc(name, shape, dt):
            return st.enter_context(nc.sbuf_tensor(uname(name), list(shape), dt))
        nf = alloc("ep_nf", [128, D], F32)
        S.dma('sp', nf[:], c.norm_final.ap().partition_broadcast(128), w=['nf'])
        xt = [alloc("ep_xt%d" % i, [128, KC, 512], F32) for i in range(2)]
        yt = [alloc("ep_yt%d" % i, [128, D], F32) for i in range(2)]
        sq = alloc("ep_sq", [128, D], F32)
        ss = alloc("ep_ss", [128, 4], F32)
        yo = [alloc("ep_yo%d" % i, [128, 4, D], F32) for i in range(2)]
        it = 0
        yb = 0
        for s in range(NSEQ):
            for i in range(NT):
                b = it % 2
                it += 1
                S.dma('sp', xt[b][:], c.XT.ap()[s].rearrange("(k p) t -> p k t", p=128)[:, :, i * 512:(i + 1) * 512],
                      r=[('XT', s, i)], w=[('xt', b)])
                for jb in range(4):
                    y = yb % 2
                    yb += 1
                    for half in range(2):
                        bank = 1 + ((yb * 2 + half) % 4)
                        for q in range(4):
                            k = half * 4 + q
                            S.op('pe', lambda e, bank=bank, q=q, k=k, jb=jb, b=b: e.transpose(
                                c.ps[bank][:, q * 128:(q + 1) * 128], xt[b][:, k, jb * 128:(jb + 1) * 128], c.ident_f[:]),
                                r=[('xt', b), 'ident_f'], w=[('ps', bank)])
                        if half == 0:
                            S.op('act', lambda e, bank=bank, y=y: e.activation(out=yt[y][:, 0:512], in_=c.ps[bank][:], func=AF.Copy),
                                 r=[('ps', bank)], w=[('yt', y)])
                        else:
                            S.op('dve', lambda e, bank=bank, y=y: e.tensor_copy(out=yt[y][:, 512:1024], in_=c.ps[bank][:]),
                                 r=[('ps', bank)], w=[('yt', y)])
                    S.op('pool', lambda e, y=y: e.tensor_tensor(out=sq[:], in0=yt[y][:], in1=yt[y][:], op=ALU.mult), r=[('yt', y)], w=['sq'])
                    S.op('dve', lambda e: e.tensor_reduce(out=ss[:, 0:1], in_=sq[:], axis=AX.X, op=ALU.add), r=['sq'], w=['ss'])
                    S.op('act', lambda e: e.activation(out=ss[:, 1:2], in_=ss[:, 0:1], func=AF.Sqrt, bias=EPS, scale=1.0 / D), r=['ss'], w=['ss'])
                    S.op('dve', lambda e: e.reciprocal(out=ss[:, 2:3], in_=ss[:, 1:2]), r=['ss'], w=['ss'])
                    S.op('dve', lambda e, y=y, jb=jb, b=b: e.scalar_tensor_tensor(out=yo[b][:, jb, :], in0=yt[y][:], scalar=ss[:, 2:3], in1=nf[:],
                                                                               op0=ALU.mult, op1=ALU.mult), r=[('yt', y), 'ss', 'nf'], w=[('yo', b)])
                S.dma('pool', c.y.ap()[s].rearrange("(j p) f -> p j f", p=128)[:, 4 * i:4 * i + 4, :], yo[b][:], r=[('yo', b)], w=[('y', s, i)])
        S.barrier()


def make_consts(L):
    f32 = np.float32
    k = {}
    k["k_ident"] = np.eye(128, dtype=f32)
    k["k_J"] = np.eye(128, dtype=f32)[::-1].copy()
    rot = np.zeros((128, 128), f32)
    for m in range(64):
        rot[m + 64, m] = -1.0
    for m in range(64, 128):
        rot[m - 64, m] = 1.0
    k["k_rot"] = rot
    half = 64
    inv = (np.float32(10000.0) ** (-np.arange(half, dtype=f32) / f32(half))).astype(f32)
    ang = (np.arange(L, dtype=f32)[None, :] * inv[:, None]).astype(f32)
    k["k_cos"] = np.concatenate([np.cos(ang), np.cos(ang)], 0).astype(f32)
    k["k_sin"] = np.concatenate([np.sin(ang), np.sin(ang)], 0).astype(f32)
    t = np.linspace(0.0, 1.0, L, dtype=f32)[:, None]
    bands = np.linspace(1e-4, 15, 16, dtype=f32)
    w = (f32(2.0 * math.pi / L) * np.arange(L, dtype=f32)[:, None] * bands[None, :]).astype(f32)
    feats = np.concatenate([t, np.cos(w), -np.sin(w)], -1).astype(f32).T
    k["k_feats"] = np.stack([feats, feats[:, ::-1]], 0).copy()
    deltas = np.abs(np.linspace(HY_MIN_DECAY, HY_MAX_DECAY, HYW, dtype=f32))
    k["k_negdelta"] = (-deltas).reshape(2, 128).T.copy().astype(f32)
    pos = np.arange(128, dtype=f32)
    k["k_df"] = np.maximum(pos[None, :] - pos[:, None], 0).astype(f32)
    k["k_db"] = np.maximum(pos[:, None] - pos[None, :], 0).astype(f32)
    k["k_kvec"] = np.stack([127.0 - pos, pos], 1).astype(f32)
    k["k_qrow"] = np.stack([pos + 1.0, 128.0 - pos], 0).astype(f32)
    ic = np.zeros((4, 16), f32)
    tt = np.arange(L)
    for g, win in enumerate(POOL_WINDOWS):
        lo = np.clip(tt - win // 2, 0, L - 1)
        hi = np.clip(tt + win // 2 - 1, 0, L - 1)
        cnt = (hi - lo + 1).astype(f32)
        ic[g, 0:8] = 1.0 / cnt[0:8]
        ic[g, 8:16] = 1.0 / cnt[L - 8:L]
    k["k_invcnt"] = ic.reshape(1, 64)
    return k


def layout_weights(W, DEPTH):
    f = lambda a: np.ascontiguousarray(np.asarray(a, dtype=np.float32))
    o = {}
    vec8 = lambda a: f(np.asarray(a).reshape(DEPTH, KC, 128).transpose(0, 2, 1))
    o["norm_mix"] = vec8(W["norm_mix"])
    o["norm_ffn"] = vec8(W["norm_ffn"])
    o["ple_norm"] = vec8(W["ple_norm"])
    o["w_in"] = f(W["w_in"])
    o["ret_decay_fwd"] = f(W["ret_decay_fwd"])
    o["ret_decay_bwd"] = f(W["ret_decay_bwd"])
    o["ret_gn"] = f(W["ret_gn"])
    o["hy_short_conv"] = f(np.asarray(W["hy_short_conv"]).reshape(DEPTH, 3, 6, 128).transpose(0, 3, 2, 1))
    o["hy_w1"] = f(W["hy_w1"])
    o["hy_b1"] = f(np.asarray(W["hy_b1"]).reshape(DEPTH, 64, 1))
    o["hy_freq"] = f(np.asarray(W["hy_freq"]).reshape(DEPTH, 64, 1))
    o["hy_w2"] = f(W["hy_w2"])
    o["hy_b2"] = f(np.asarray(W["hy_b2"]).reshape(DEPTH, 64, 1))
    o["hy_w3"] = f(W["hy_w3"])
    o["hy_bias"] = f(np.asarray(W["hy_bias"]).reshape(DEPTH, 2, 2, 128).transpose(0, 3, 1, 2).reshape(DEPTH, 128, 4))
    o["pool_w"] = f(W["pool_w"])
    o["pool_scale"] = f(np.asarray(W["pool_scale"]).reshape(DEPTH, 2, 128).transpose(0, 2, 1))
    o["w_out"] = f(W["w_out"])
    o["ffn_w_up"] = f(W["ffn_w_up"])
    o["ffn_conv"] = f(np.asarray(W["ffn_conv"]).reshape(DEPTH, 3, NJ, 128).transpose(0, 3, 2, 1))
    o["ffn_w_down"] = f(W["ffn_w_down"])
    o["ple_w"] = f(W["ple_w"])
    o["ple_gate_w"] = f(W["ple_gate_w"])
    o["norm_final"] = f(np.asarray(W["norm_final"]).reshape(1, D))
    return o


_CACHE = {}


def kernel(**inputs):
    L, DEPTH = L_FULL, DEPTH_FULL
    xp = np.asarray(inputs["x_prompt"], dtype=np.float32)
    xs = np.asarray(inputs["x_sample"], dtype=np.float32)
    pp = np.asarray(inputs["p_prompt"], dtype=np.float32)
    psm = np.asarray(inputs["p_sample"], dtype=np.float32)
    nP, nS = xp.shape[0], xs.shape[0]
    def seq_x(g):
        return xp[g] if g < nP else xs[g - nP]
    def seq_p(g):
        return pp[:, g] if g < nP else psm[:, g - nP]
    slots = []
    for cid in range(8):
        if cid < 4:
            slots.append([3 * cid, 3 * cid + 1, 3 * cid + 2])
        else:
            a = 12 + 2 * (cid - 4)
            slots.append([a, a + 1, a + 1])
    if "nc" not in _CACHE:
        _CACHE["nc"] = build(L, NSLOT, DEPTH)[0]
        _CACHE["consts"] = make_consts(L)
    nc = _CACHE["nc"]
    shared = dict(_CACHE["consts"])
    shared.update(layout_weights(inputs, DEPTH))
    in_maps = []
    for cid in range(8):
        m = dict(shared)
        m["x"] = np.ascontiguousarray(np.stack([seq_x(g) for g in slots[cid]], 0))
        m["p"] = np.ascontiguousarray(np.stack([seq_p(g) for g in slots[cid]], 1))
        in_maps.append(m)
    res = run_bass_kernel_spmd(nc, in_maps, core_ids=list(range(8)))
    y_all = np.zeros((nP + nS, L, D), np.float32)
    for cid in range(8):
        y = np.asarray(res.results[cid]["y"])
        n_real = 3 if cid < 4 else 2
        for j in range(n_real):
            y_all[slots[cid][j]] = y[j]
    return (y_all[:nP].copy(), y_all[nP:].copy())
```

```python
import math
from contextlib import ExitStack
import numpy as np
import concourse.bass as bass
import concourse.mybir as mybir
from concourse.bass_utils import run_bass_kernel_spmd
from concourse.ap import AP

F32 = mybir.dt.float32
BF16 = mybir.dt.bfloat16
I32 = mybir.dt.int32
AF = mybir.ActivationFunctionType
ALU = mybir.AluOpType
AX = mybir.AxisListType

D = 1024
KC = 8
DEPTH_FULL = 4
L_FULL = 4096
NSLOT = 3
RW = 512
HYW = 256
INW = 3072
DFF = 2816
NJ = 22
EPS = 1e-6
PI = math.pi
HY_MIN_DECAY = math.log(1e-2) / 1.5
HY_MAX_DECAY = math.log(1e-2) / 0.3


class Sched:
    ENG = ('pe', 'act', 'dve', 'pool', 'sp')

    def __init__(s, nc, n_dma=40):
        s.nc = nc
        s.e = dict(pe=nc.tensor, act=nc.scalar, dve=nc.vector, pool=nc.gpsimd, sp=nc.sync)
        s.sem = {k: nc.alloc_semaphore("s_" + k) for k in ('pe', 'act', 'dve', 'pool')}
        s.cnt = {k: 0 for k in s.sem}
        s.dsem = [nc.alloc_semaphore("d%d" % i) for i in range(n_dma)]
        s.dcnt = [0] * n_dma
        s.drr = 0
        s.known = {k: {} for k in s.ENG}
        s.res = {}
        s.n_wait = 0
        s.n_ops = 0
        s.excl_ps = True

    def _semobj(s, name):
        return s.sem[name] if name in s.sem else s.dsem[name]

    def _wait(s, eng, name, val):
        if val <= 0 or s.known[eng].get(name, 0) >= val:
            return
        s.e[eng].wait_ge(s._semobj(name), val)
        s.known[eng][name] = val
        s.n_wait += 1

    def _deps(s, eng, reads, writes):
        best = {}
        for k in reads:
            r = s.res.get(k)
            if r and r[0]:
                n, v = r[0]
                if best.get(n, 0) < v:
                    best[n] = v
        for k in writes:
            r = s.res.get(k)
            if r:
                if r[0]:
                    n, v = r[0]
                    if best.get(n, 0) < v:
                        best[n] = v
                for n, v in r[1].items():
                    if best.get(n, 0) < v:
                        best[n] = v
        for n, v in best.items():
            if n == 'pe' and eng == 'pe':
                continue
            s._wait(eng, n, v)

    def _commit(s, ev, reads, writes):
        n, v = ev
        for k in reads:
            r = s.res.get(k)
            if r is None:
                r = s.res[k] = [None, {}]
            r[1][n] = v
        for k in writes:
            s.res[k] = [ev, {}]

    def op(s, eng, fn, r=(), w=()):
        if s.excl_ps:
            pr = [k for k in r if isinstance(k, tuple) and k[0] == 'ps']
            if pr:
                r = [k for k in r if k not in pr]
                w = list(w) + pr
        s._deps(eng, r, w)
        ins = fn(s.e[eng])
        s.cnt[eng] += 1
        ins.then_inc(s.sem[eng], 1)
        s._commit((eng, s.cnt[eng]), r, w)
        s.n_ops += 1

    def dma(s, q, out, in_, r=(), w=()):
        j = s.drr
        s.drr = (s.drr + 1) % len(s.dsem)
        s._wait(q, j, s.dcnt[j])
        s._deps(q, r, w)
        ins = s.e[q].dma_start(out=out, in_=in_)
        s.dcnt[j] += 16
        ins.then_inc(s.dsem[j], 16)
        s._commit((j, s.dcnt[j]), r, w)
        s.n_ops += 1

    def barrier(s, engs=None):
        for eng in (engs or s.ENG):
            for k in s.sem:
                s._wait(eng, k, s.cnt[k])
            for j in range(len(s.dsem)):
                s._wait(eng, j, s.dcnt[j])
        s.res = {}


def sb_ap(t, off, dims):
    pst = t[:].ap[0][0]
    return AP(t, off, [[pst, 128]] + [list(d) for d in dims])


class Ctx:
    pass


_UID = [0]


def uname(n):
    _UID[0] += 1
    return "%s_u%d" % (n, _UID[0])


ALL_STAGES = ('filt', 'tr', 'p1', 'ret', 'pool', 'hy', 'p3', 'p4', 'epi')


def build(L=L_FULL, NSEQ=NSLOT, DEPTH=DEPTH_FULL, taps=(), stages=ALL_STAGES):
    NT = L // 512
    NB = L // 128
    NLAG = 2 * NB - 1
    nc = bass.Bass("TRN2", target_bir_lowering=False)
    S = Sched(nc)
    c = Ctx()
    c.nc, c.S, c.L, c.NSEQ, c.DEPTH, c.NT, c.NB, c.NLAG = nc, S, L, NSEQ, DEPTH, NT, NB, NLAG
    c.taps = taps
    c.stages = stages

    def din(name, shape, dt=F32):
        return nc.dram_tensor(name, list(shape), dt, kind="ExternalInput")

    c.x = din("x", [NSEQ, L, D])
    c.p = din("p", [DEPTH, NSEQ, L, 256])
    c.norm_mix = din("norm_mix", [DEPTH, 128, KC])
    c.w_in = din("w_in", [DEPTH * D + 1, INW])
    c.dec_f = din("ret_decay_fwd", [DEPTH, 4])
    c.dec_b = din("ret_decay_bwd", [DEPTH, 4])
    c.ret_gn = din("ret_gn", [DEPTH, RW])
    c.hy_conv = din("hy_short_conv", [DEPTH, 128, 6, 3])
    c.hy_w1 = din("hy_w1", [DEPTH, 33, 64])
    c.hy_b1 = din("hy_b1", [DEPTH, 64, 1])
    c.hy_freq = din("hy_freq", [DEPTH, 64, 1])
    c.hy_w2 = din("hy_w2", [DEPTH, 64, 64])
    c.hy_b2 = din("hy_b2", [DEPTH, 64, 1])
    c.hy_w3 = din("hy_w3", [DEPTH * 64 + 1, 1024])
    c.hy_bias = din("hy_bias", [DEPTH, 128, 4])
    c.pool_w = din("pool_w", [DEPTH, 4, 64, 64])
    c.pool_scale = din("pool_scale", [DEPTH, 128, 2])
    c.w_out = din("w_out", [DEPTH * D + 1, D])
    c.norm_ffn = din("norm_ffn", [DEPTH, 128, KC])
    c.w_up = din("ffn_w_up", [DEPTH * D + 1, 2 * DFF])
    c.ffn_conv = din("ffn_conv", [DEPTH, 128, NJ, 3])
    c.w_down = din("ffn_w_down", [DEPTH * DFF + 1, D])
    c.ple_w = din("ple_w", [DEPTH * 256 + 1, D])
    c.ple_gate = din("ple_gate_w", [DEPTH * D + 1, D])
    c.ple_norm = din("ple_norm", [DEPTH, 128, KC])
    c.norm_final = din("norm_final", [1, D])
    c.k_ident = din("k_ident", [128, 128])
    c.k_J = din("k_J", [128, 128])
    c.k_rot = din("k_rot", [128, 128])
    c.k_cos = din("k_cos", [129, L])
    c.k_sin = din("k_sin", [129, L])
    c.k_feats = din("k_feats", [67, L])
    c.k_negdelta = din("k_negdelta", [128, 2])
    c.k_df = din("k_df", [128, 128])
    c.k_db = din("k_db", [128, 128])
    c.k_kvec = din("k_kvec", [128, 2])
    c.k_qrow = din("k_qrow", [2, 128])
    c.k_invcnt = din("k_invcnt", [1, 64])
    c.y = nc.dram_tensor("y", [NSEQ, L, D], F32, kind="ExternalOutput")

    def dscr(name, shape, dt):
        return nc.dram_tensor(name, list(shape), dt)

    c.XT = dscr("XT", [NSEQ, D, L], F32)
    c.PT = dscr("PT", [DEPTH, NSEQ, 256, L], BF16)
    c.QT = dscr("QT", [NSEQ, RW, L], BF16)
    c.KT = dscr("KT", [NSEQ, RW, L], BF16)
    c.V = dscr("V", [NSEQ, L, RW], BF16)
    c.GS = dscr("GS", [NSEQ, L, RW], BF16)
    c.HYT = dscr("HYT", [NSEQ, 768, L], F32)
    c.PLT = dscr("PLT", [NSEQ, 256, L], F32)
    c.MIXT = dscr("MIXT", [NSEQ, D, L], BF16)
    c.HID = dscr("HID", [NSEQ, DFF, L], BF16)
    c.X1T = dscr("X1T", [NSEQ, D, L], F32)
    c.G = dscr("G", [DEPTH, 512, 2 * L], BF16)
    c.dbg = {}
    for nm in taps:
        t_ = getattr(c, nm)
        c.dbg[nm] = nc.dram_tensor("dbg_" + nm, list(t_.shape), t_.dtype, kind="ExternalOutput")

    c.ps = [nc.alloc_psum_tensor("ps%d" % i, [128, 512], F32) for i in range(8)]

    with ExitStack() as gst:
        def galloc(name, shape, dt):
            return gst.enter_context(nc.sbuf_tensor(uname(name), list(shape), dt))
        c.ident_f = galloc("ident_f", [128, 128], F32)
        c.ident_b = galloc("ident_b", [128, 128], BF16)
        c.J_b = galloc("J_b", [128, 128], BF16)
        c.rot_b = galloc("rot_b", [128, 128], BF16)
        c.ones_b = galloc("ones_b", [128, 128], BF16)
        stg = galloc("cstage", [128, 128], F32)
        S.dma('sp', c.ident_f[:], c.k_ident.ap(), w=['ident_f'])
        S.op('dve', lambda e: e.tensor_copy(out=c.ident_b[:], in_=c.ident_f[:]), r=['ident_f'], w=['ident_b'])
        S.dma('sp', stg[:], c.k_J.ap(), w=['cstage'])
        S.op('dve', lambda e: e.tensor_copy(out=c.J_b[:], in_=stg[:]), r=['cstage'], w=['J_b'])
        S.dma('sp', stg[:], c.k_rot.ap(), w=['cstage'])
        S.op('dve', lambda e: e.tensor_copy(out=c.rot_b[:], in_=stg[:]), r=['cstage'], w=['rot_b'])
        S.op('dve', lambda e: e.memset(c.ones_b[:], 1.0), w=['ones_b'])
        S.barrier()

        st_ = c.stages
        if 'filt' in st_:
            prologue_filters(c)
        if 'tr' in st_:
            prologue_transposes(c)
        for l in range(DEPTH):
            if 'p1' in st_:
                pass_p1(c, l)
            if 'ret' in st_:
                pass_ret(c, l)
            if 'pool' in st_:
                pass_pool(c, l)
            if 'hy' in st_:
                pass_hyena(c, l)
            if 'p3' in st_:
                pass_p3(c, l)
            if 'p4' in st_:
                pass_p4(c, l)
        if 'epi' in st_:
            epilogue(c)
        S.barrier()
        for nm in taps:
            S.dma('sp', c.dbg[nm].ap(), getattr(c, nm).ap())
        S.barrier(['sp', 'pool'])
    return nc, c


def prologue_transposes(c):
    nc, S, L, NSEQ, DEPTH, NT = c.nc, c.S, c.L, c.NSEQ, c.DEPTH, c.NT
    with ExitStack() as st:
        def alloc(name, shape, dt):
            return st.enter_context(nc.sbuf_tensor(uname(name), list(shape), dt))
        xin = [alloc("pt_xin%d" % i, [128, 4, D], F32) for i in range(2)]
        xst = [alloc("pt_xst%d" % i, [128, KC, 512], F32) for i in range(2)]
        pin = [alloc("pt_pin%d" % i, [128, 4, 256], F32) for i in range(2)]
        pbf = [alloc("pt_pbf%d" % i, [128, 4, 256], BF16) for i in range(2)]
        pst = [alloc("pt_pst%d" % i, [128, 2, 512], BF16) for i in range(2)]
        it = 0
        for s in range(NSEQ):
            for i in range(NT):
                b = it % 2
                it += 1
                src = c.x.ap()[s].rearrange("(j p) f -> p j f", p=128)[:, 4 * i:4 * i + 4, :]
                S.dma('sp', xin[b][:], src, w=[('xin', b)])
                g = 0
                for jb in range(4):
                    for half in range(2):
                        bank = 1 + (g % 4)
                        g += 1
                        for q in range(4):
                            kc = half * 4 + q
                            S.op('pe', lambda e, bank=bank, q=q, kc=kc, jb=jb, b=b: e.transpose(
                                c.ps[bank][:, q * 128:(q + 1) * 128], xin[b][:, jb, kc * 128:(kc + 1) * 128], c.ident_f[:]),
                                r=[('xin', b)], w=[('ps', bank)])
                        eng = 'dve' if (g % 2) else 'act'
                        if eng == 'dve':
                            S.op('dve', lambda e, bank=bank, half=half, jb=jb, b=b: e.tensor_copy(
                                out=xst[b][:, half * 4:half * 4 + 4, jb * 128:(jb + 1) * 128],
                                in_=c.ps[bank][:].rearrange("p (q t) -> p q t", q=4)),
                                r=[('ps', bank)], w=[('xst', b)])
                        else:
                            S.op('act', lambda e, bank=bank, half=half, jb=jb, b=b: e.activation(
                                out=xst[b][:, half * 4:half * 4 + 4, jb * 128:(jb + 1) * 128],
                                in_=c.ps[bank][:].rearrange("p (q t) -> p q t", q=4), func=AF.Copy),
                                r=[('ps', bank)], w=[('xst', b)])
                dst = c.XT.ap()[s].rearrange("(k p) t -> p k t", p=128)[:, :, i * 512:(i + 1) * 512]
                S.dma('pool', dst, xst[b][:], r=[('xst', b)], w=[('XT', s, i)])
        it = 0
        for l in range(DEPTH):
            for s in range(NSEQ):
                for i in range(NT):
                    b = it % 2
                    it += 1
                    src = c.p.ap()[l, s].rearrange("(j p) f -> p j f", p=128)[:, 4 * i:4 * i + 4, :]
                    S.dma('sp', pin[b][:], src, w=[('pin', b)])
                    S.op('dve', lambda e, b=b: e.tensor_copy(out=pbf[b][:], in_=pin[b][:]), r=[('pin', b)], w=[('pbf', b)])
                    for kc in range(2):
                        bank = 5 + kc
                        psb = c.ps[bank][:].bitcast(BF16)
                        for jb in range(4):
                            S.op('pe', lambda e, psb=psb, jb=jb, kc=kc, b=b: e.transpose(
                                psb[:, jb * 128:(jb + 1) * 128], pbf[b][:, jb, kc * 128:(kc + 1) * 128], c.ident_b[:]),
                                r=[('pbf', b)], w=[('ps', bank)])
                        S.op('act', lambda e, psb=psb, kc=kc, b=b: e.activation(
                            out=pst[b][:, kc, :], in_=psb[:, 0:512], func=AF.Copy),
                            r=[('ps', bank)], w=[('pst', b)])
                    dst = c.PT.ap()[l, s].rearrange("(k p) t -> p k t", p=128)[:, :, i * 512:(i + 1) * 512]
                    S.dma('pool', dst, pst[b][:], r=[('pst', b)], w=[('PT', l, s, i)])
        S.barrier()


def load_weight_bf16(c, st, name, src_rows_ap, nk, ncols, key, stage_cols=1024, rowscale=None, rskey=None):
    nc, S = c.nc, c.S
    wb = st.enter_context(nc.sbuf_tensor(uname(name), [128, nk, ncols], BF16))
    if not hasattr(c, 'wstage'):
        raise RuntimeError("wstage missing")
    src = src_rows_ap.rearrange("(k p) n -> p k n", p=128)
    idx = 0
    for k in range(nk):
        for c0 in range(0, ncols, stage_cols):
            cw = min(stage_cols, ncols - c0)
            b = c.wstage_i % 2
            c.wstage_i += 1
            S.dma('sp', c.wstage[b][:, 0:cw], src[:, k, c0:c0 + cw], w=[('wstage', b)])
            eng = ('dve', 'pool', 'act')[idx % 3]
            idx += 1
            rk = [('wstage', b)] + ([rskey] if rskey else [])
            if rowscale is None:
                if eng == 'act':
                    S.op('act', lambda e, b=b, k=k, c0=c0, cw=cw: e.activation(
                        out=wb[:, k, c0:c0 + cw], in_=c.wstage[b][:, 0:cw], func=AF.Copy), r=rk, w=[key])
                else:
                    S.op(eng, lambda e, b=b, k=k, c0=c0, cw=cw: e.tensor_copy(
                        out=wb[:, k, c0:c0 + cw], in_=c.wstage[b][:, 0:cw]), r=rk, w=[key])
            else:
                if eng == 'act':
                    S.op('act', lambda e, b=b, k=k, c0=c0, cw=cw: e.activation(
                        out=wb[:, k, c0:c0 + cw], in_=c.wstage[b][:, 0:cw], func=AF.Copy, scale=rowscale[:, k:k + 1]), r=rk, w=[key])
                else:
                    S.op(eng, lambda e, b=b, k=k, c0=c0, cw=cw: e.tensor_scalar(
                        out=wb[:, k, c0:c0 + cw], in0=c.wstage[b][:, 0:cw], scalar1=rowscale[:, k:k + 1], scalar2=None, op0=ALU.mult),
                        r=rk, w=[key])
    return wb


def alloc_wstage(c, st, cols=1024):
    c.wstage = [st.enter_context(c.nc.sbuf_tensor(uname("wstage%d" % i), [128, cols], F32)) for i in range(2)]
    c.wstage_i = 0


def rms_rstd(c, ps_bank, rs, n, key_ps, key_rs, dim):
    S = c.S
    S.op('act', lambda e: e.activation(out=rs[:, 0:n], in_=c.ps[ps_bank][:, 0:n], func=AF.Sqrt,
                                       bias=EPS, scale=1.0 / dim), r=[key_ps], w=[key_rs])
    S.op('dve', lambda e: e.reciprocal(out=rs[:, 0:n], in_=rs[:, 0:n]), r=[key_rs], w=[key_rs])


def pass_p1(c, l):
    nc, S, L, NSEQ, NT = c.nc, c.S, c.L, c.NSEQ, c.NT
    with ExitStack() as st:
        def alloc(name, shape, dt):
            return st.enter_context(nc.sbuf_tensor(uname(name), list(shape), dt))
        alloc_wstage(c, st)
        gm = alloc("p1_gm", [128, KC], F32)
        S.dma('sp', gm[:], c.norm_mix.ap()[l], w=['gm'])
        Wb = load_weight_bf16(c, st, "p1_W", c.w_in.ap()[l * D:(l + 1) * D, :], KC, INW, 'Wb', rowscale=gm, rskey='gm')
        cos = alloc("p1_cos", [128, L], F32)
        sin = alloc("p1_sin", [128, L], F32)
        gn = alloc("p1_gn", [128, RW], F32)
        S.dma('sp', cos[:], c.k_cos.ap()[0:128, :], w=['cos'])
        S.dma('sp', sin[:], c.k_sin.ap()[0:128, :], w=['sin'])
        S.dma('sp', gn[:], c.ret_gn.ap()[l:l + 1, :].partition_broadcast(128), w=['gn'])
        xt = alloc("p1_xt", [128, KC, 512], F32)
        xsq = alloc("p1_xsq", [128, KC, 512], BF16)
        rs = alloc("p1_rs", [128, 512], F32)
        h = [alloc("p1_h%d" % i, [128, KC, 512], BF16) for i in range(2)]
        qb = [alloc("p1_qb%d" % i, [128, 512], BF16) for i in range(2)]
        t1 = [alloc("p1_t1%d" % i, [128, 512], F32) for i in range(2)]
        t2 = [alloc("p1_t2%d" % i, [128, 512], F32) for i in range(2)]
        sl = [alloc("p1_sl%d" % i, [128, 512], F32) for i in range(2)]
        qk_out = alloc("p1_qko", [128, 8, 512], BF16)
        hy_out = alloc("p1_hyo", [128, 6, 512], F32)
        pl_out = alloc("p1_plo", [128, 2, 512], F32)
        v_out = alloc("p1_vo", [128, 4, 512], BF16)
        gs_out = alloc("p1_gso", [128, 4, 512], BF16)
        scl_q = 128.0 ** -0.5
        it = 0
        rr = 0
        for s in range(NSEQ):
            for i in range(NT):
                t0 = i * 512
                hb = it % 2
                it += 1
                S.dma('sp', xt[:], c.XT.ap()[s].rearrange("(k p) t -> p k t", p=128)[:, :, t0:t0 + 512],
                      r=[('XT', s, i)], w=['xt'])
                S.op('act', lambda e: e.activation(out=xsq[:], in_=xt[:], func=AF.Square), r=['xt'], w=['xsq'])
                for k in range(KC):
                    S.op('pe', lambda e, k=k: e.matmul(c.ps[0][:], lhsT=c.ones_b[:], rhs=xsq[:, k, :],
                                                       start=(k == 0), stop=(k == KC - 1)),
                         r=['xsq', 'ones_b'], w=[('ps', 0)])
                rms_rstd(c, 0, rs, 512, ('ps', 0), 'rs', D)
                for k in range(KC):
                    eng = 'dve' if k % 2 == 0 else 'pool'
                    S.op(eng, lambda e, k=k, hb=hb: e.tensor_tensor(
                        out=h[hb][:, k, :], in0=xt[:, k, :], in1=rs[:], op=ALU.mult), r=['xt', 'rs'], w=[('h', hb)])
                mt = [('q', j, j * 128) for j in range(4)] + [('k', j, 512 + j * 128) for j in range(4)] + \
                     [('hy', j, 2048 + j * 128) for j in range(6)] + [('pl', j, 2816 + j * 128) for j in range(2)]
                for kind, j, col in mt:
                    bank = 1 + (rr % 4)
                    rr += 1
                    for k in range(KC):
                        S.op('pe', lambda e, k=k, bank=bank, col=col, hb=hb: e.matmul(
                            c.ps[bank][:], lhsT=Wb[:, k, col:col + 128], rhs=h[hb][:, k, :],
                            start=(k == 0), stop=(k == KC - 1)), r=['Wb', ('h', hb)], w=[('ps', bank)])
                    if kind in ('q', 'k'):
                        b2 = rr % 2
                        rb = 5 + b2
                        oi = j if kind == 'q' else 4 + j
                        sc = scl_q if kind == 'q' else 1.0
                        S.op('act', lambda e, bank=bank, b2=b2: e.activation(out=qb[b2][:], in_=c.ps[bank][:], func=AF.Copy),
                             r=[('ps', bank)], w=[('qb', b2)])
                        S.op('pe', lambda e, rb=rb, b2=b2: e.matmul(c.ps[rb][:], lhsT=c.rot_b[:], rhs=qb[b2][:],
                                                                   start=True, stop=True),
                             r=[('qb', b2), 'rot_b'], w=[('ps', rb)])
                        S.op('dve', lambda e, bank=bank, b2=b2, sc=sc: e.scalar_tensor_tensor(
                            out=t1[b2][:], in0=c.ps[bank][:], scalar=sc, in1=cos[:, t0:t0 + 512],
                            op0=ALU.mult, op1=ALU.mult), r=[('ps', bank), 'cos'], w=[('t1', b2)])
                        S.op('dve', lambda e, rb=rb, b2=b2, sc=sc: e.scalar_tensor_tensor(
                            out=t2[b2][:], in0=c.ps[rb][:], scalar=sc, in1=sin[:, t0:t0 + 512],
                            op0=ALU.mult, op1=ALU.mult), r=[('ps', rb), 'sin'], w=[('t2', b2)])
                        S.op('pool', lambda e, b2=b2, oi=oi: e.tensor_tensor(
                            out=qk_out[:, oi, :], in0=t1[b2][:], in1=t2[b2][:], op=ALU.add),
                            r=[('t1', b2), ('t2', b2)], w=['qk_out'])
                    elif kind == 'hy':
                        S.op('act', lambda e, bank=bank, j=j: e.activation(out=hy_out[:, j, :], in_=c.ps[bank][:], func=AF.Copy),
                             r=[('ps', bank)], w=['hy_out'])
                    else:
                        S.op('dve', lambda e, bank=bank, j=j: e.tensor_copy(out=pl_out[:, j, :], in_=c.ps[bank][:]),
                             r=[('ps', bank)], w=['pl_out'])
                for j in range(4):
                    bank = 1 + (rr % 4)
                    rr += 1
                    for k in range(KC):
                        S.op('pe', lambda e, k=k, bank=bank, j=j, hb=hb: e.matmul(
                            c.ps[bank][:], lhsT=h[hb][:, k, j * 128:(j + 1) * 128], rhs=Wb[:, k, 1024:1536],
                            start=(k == 0), stop=(k == KC - 1)), r=['Wb', ('h', hb)], w=[('ps', bank)])
                    S.op('dve', lambda e, bank=bank, j=j: e.tensor_copy(out=v_out[:, j, :], in_=c.ps[bank][:]),
                         r=[('ps', bank)], w=['v_out'])
                    bank = 1 + (rr % 4)
                    rr += 1
                    b2 = rr % 2
                    for k in range(KC):
                        S.op('pe', lambda e, k=k, bank=bank, j=j, hb=hb: e.matmul(
                            c.ps[bank][:], lhsT=h[hb][:, k, j * 128:(j + 1) * 128], rhs=Wb[:, k, 1536:2048],
                            start=(k == 0), stop=(k == KC - 1)), r=['Wb', ('h', hb)], w=[('ps', bank)])
                    S.op('act', lambda e, bank=bank, b2=b2: e.activation(out=sl[b2][:], in_=c.ps[bank][:], func=AF.Silu),
                         r=[('ps', bank)], w=[('sl', b2)])
                    S.op('pool', lambda e, b2=b2, j=j: e.tensor_tensor(out=gs_out[:, j, :], in0=sl[b2][:], in1=gn[:], op=ALU.mult),
                         r=[('sl', b2), 'gn'], w=['gs_out'])
                fm = lambda T, n: T.ap()[s].rearrange("(k p) t -> p k t", p=128)[:, 0:n, t0:t0 + 512]
                tm = lambda T: T.ap()[s].rearrange("(j p) f -> p j f", p=128)[:, 4 * i:4 * i + 4, :]
                S.dma('pool', fm(c.QT, 4), qk_out[:, 0:4, :], r=['qk_out'], w=[('QT', s, i)])
                S.dma('pool', fm(c.KT, 4), qk_out[:, 4:8, :], r=['qk_out'], w=[('KT', s, i)])
                S.dma('pool', fm(c.HYT, 6), hy_out[:], r=['hy_out'], w=[('HYT', s, i)])
                S.dma('pool', fm(c.PLT, 2), pl_out[:], r=['pl_out'], w=[('PLT', s, i)])
                S.dma('pool', tm(c.V), v_out[:], r=['v_out'], w=[('V', s, i)])
                S.dma('pool', tm(c.GS), gs_out[:], r=['gs_out'], w=[('GS', s, i)])
        S.barrier()


def pass_ret(c, l):
    nc, S, L, NSEQ, NT, NB = c.nc, c.S, c.L, c.NSEQ, c.NT, c.NB
    with ExitStack() as st:
        def alloc(name, shape, dt):
            return st.enter_context(nc.sbuf_tensor(uname(name), list(shape), dt))
        lgr = alloc("rt_lgr", [128, 8], F32)
        lg = alloc("rt_lg", [128, 8], F32)
        df = alloc("rt_df", [128, 128], F32)
        db = alloc("rt_db", [128, 128], F32)
        kvec = alloc("rt_kvec", [128, 2], F32)
        qrow = alloc("rt_qrow", [128, 2, 128], F32)
        mask = alloc("rt_mask", [128, 4, 128], F32)
        mtmp = alloc("rt_mtmp", [128, 128], F32)
        wk = alloc("rt_wk", [128, 2, 4], F32)
        wq = alloc("rt_wq", [128, 2, 4, 128], F32)
        decv = alloc("rt_decv", [128, 8], F32)
        S.dma('sp', lgr[:, 0:4], c.dec_f.ap()[l:l + 1, :].partition_broadcast(128), w=['lgr'])
        S.dma('sp', lgr[:, 4:8], c.dec_b.ap()[l:l + 1, :].partition_broadcast(128), w=['lgr'])
        S.dma('sp', df[:], c.k_df.ap(), w=['df'])
        S.dma('sp', db[:], c.k_db.ap(), w=['db'])
        S.dma('sp', kvec[:], c.k_kvec.ap(), w=['kvec'])
        S.dma('sp', qrow[:, 0, :], c.k_qrow.ap()[0:1, :].partition_broadcast(128), w=['qrow'])
        S.dma('sp', qrow[:, 1, :], c.k_qrow.ap()[1:2, :].partition_broadcast(128), w=['qrow'])
        S.op('act', lambda e: e.activation(out=lg[:], in_=lgr[:], func=AF.Exp, scale=-1.0), r=['lgr'], w=['lg'])
        S.op('act', lambda e: e.activation(out=lg[:], in_=lg[:], func=AF.Ln, bias=1.0, scale=1.0), r=['lg'], w=['lg'])
        S.op('dve', lambda e: e.tensor_scalar(out=lg[:], in0=lg[:], scalar1=-1.0, scalar2=None, op0=ALU.mult), r=['lg'], w=['lg'])
        for hh in range(4):
            S.op('dve', lambda e, hh=hh: e.tensor_scalar(out=mtmp[:], in0=df[:], scalar1=lg[:, hh:hh + 1], scalar2=None,
                                                         op0=ALU.mult), r=['df', 'lg'], w=['mtmp'])
            S.op('dve', lambda e, hh=hh: e.scalar_tensor_tensor(out=mtmp[:], in0=db[:], scalar=lg[:, 4 + hh:5 + hh], in1=mtmp[:],
                                                                op0=ALU.mult, op1=ALU.add), r=['db', 'lg', 'mtmp'], w=['mtmp'])
            S.op('act', lambda e, hh=hh: e.activation(out=mask[:, hh, :], in_=mtmp[:], func=AF.Exp), r=['mtmp'], w=['mask'])
            for d in range(2):
                S.op('act', lambda e, hh=hh, d=d: e.activation(out=wq[:, d, hh, :], in_=qrow[:, d, :], func=AF.Exp,
                                                               scale=lg[:, 4 * d + hh:4 * d + hh + 1]),
                     r=['qrow', 'lg'], w=['wq'])
        for d in range(2):
            S.op('act', lambda e, d=d: e.activation(out=wk[:, d, :], in_=lg[:, 4 * d:4 * d + 4], func=AF.Exp,
                                                    scale=kvec[:, d:d + 1]), r=['kvec', 'lg'], w=['wk'])
        S.op('act', lambda e: e.activation(out=decv[:], in_=lg[:], func=AF.Exp, scale=128.0), r=['lg'], w=['decv'])

        QTs = alloc("rt_QT", [128, 4, L], BF16)
        KTs = alloc("rt_KT", [128, 4, L], BF16)
        Vs = alloc("rt_V", [128, NB, RW], BF16)
        SB = alloc("rt_SB", [128, NB, 4, 128], BF16)
        stt = alloc("rt_st", [128, 4, 128], F32)
        stb = alloc("rt_stb", [128, 4, 128], BF16)
        ktw = [alloc("rt_ktw%d" % i, [128, 4, 128], BF16) for i in range(2)]
        PT_ = [alloc("rt_PT%d" % i, [128, 4, 128], BF16) for i in range(2)]
        qf = [alloc("rt_qf%d" % i, [128, 4, 512], BF16) for i in range(2)]
        qbk = [alloc("rt_qbk%d" % i, [128, 4, 512], BF16) for i in range(2)]
        GSt = [alloc("rt_GS%d" % i, [128, 4, RW], BF16) for i in range(2)]
        osq = alloc("rt_osq", [128, 4, 128], F32)
        sm = alloc("rt_sm", [128, 8], F32)
        sm2 = alloc("rt_sm2", [128, 20], F32)
        tmpo = [alloc("rt_tmpo%d" % i, [128, 4, 128], F32) for i in range(2)]
        rtok = [alloc("rt_rtok%d" % i, [128, 4, 128], BF16) for i in range(2)]
        mixst = [alloc("rt_mix%d" % i, [128, 4, 512], BF16) for i in range(2)]
        psT = c.ps[0][:].bitcast(BF16)
        psR = c.ps[5][:].bitcast(BF16)

        def bc4(t, off):
            return sb_ap(t, off, [[1, 4], [0, 128]])

        def k_transposes(n, w_dir, kb):
            for hh in range(4):
                S.op('pe', lambda e, hh=hh: e.transpose(psT[:, hh * 128:(hh + 1) * 128], KTs[:, hh, n * 128:(n + 1) * 128],
                                                        c.ident_b[:]), r=['KTs', 'ident_b'], w=[('ps', 0)])
            S.op('dve', lambda e: e.tensor_tensor(out=ktw[kb][:], in0=psT[:, 0:512].rearrange("p (h t) -> p h t", h=4),
                                                  in1=bc4(wk, w_dir * 4), op=ALU.mult),
                 r=[('ps', 0), 'wk'], w=[('ktw', kb)])

        def kv_update(n, d, kb):
            for hh in range(4):
                S.op('pe', lambda e, hh=hh: e.matmul(c.ps[4][:, hh * 128:(hh + 1) * 128], lhsT=ktw[kb][:, hh, :],
                                                     rhs=Vs[:, n, hh * 128:(hh + 1) * 128], start=True, stop=True),
                     r=[('ktw', kb), 'Vs'], w=[('ps', 4)])
            S.op('pool', lambda e: e.tensor_tensor(out=stt[:], in0=stt[:], in1=bc4(decv, d * 4), op=ALU.mult),
                 r=['stt', 'decv'], w=['stt'])
            S.op('dve', lambda e: e.tensor_tensor(out=stt[:], in0=stt[:], in1=c.ps[4][:].rearrange("p (h t) -> p h t", h=4),
                                                  op=ALU.add), r=['stt', ('ps', 4)], w=['stt'])

        it = 0
        ck = 0
        for s in range(NSEQ):
            allt = [(nm, s, i) for nm in ('QT',) for i in range(NT)]
            S.dma('sp', QTs[:], c.QT.ap()[s].rearrange("(h p) t -> p h t", p=128), r=[('QT', s, i) for i in range(NT)], w=['QTs'])
            S.dma('sp', KTs[:], c.KT.ap()[s].rearrange("(h p) t -> p h t", p=128), r=[('KT', s, i) for i in range(NT)], w=['KTs'])
            S.dma('sp', Vs[:], c.V.ap()[s].rearrange("(n p) f -> p n f", p=128), r=[('V', s, i) for i in range(NT)], w=['Vs'])
            S.op('pool', lambda e: e.memset(stt[:], 0.0), w=['stt'])
            for n in range(NB - 1, -1, -1):
                S.op('act', lambda e, n=n: e.activation(out=SB[:, n, :, :], in_=stt[:], func=AF.Copy), r=['stt'], w=['SB'])
                if n > 0:
                    kb = ck % 2
                    ck += 1
                    k_transposes(n, 1, kb)
                    kv_update(n, 1, kb)
            S.op('pool', lambda e: e.memset(stt[:], 0.0), w=['stt'])
            S.op('pool', lambda e: e.memset(stb[:], 0.0), w=['stb'])
            for i in range(NT):
                t0 = i * 512
                tb = it % 2
                it += 1
                S.dma('sp', GSt[tb][:], c.GS.ap()[s].rearrange("(j p) f -> p j f", p=128)[:, 4 * i:4 * i + 4, :],
                      r=[('GS', s, i)], w=[('GSt', tb)])
                for d, dst in ((0, qf), (1, qbk)):
                    eng = 'dve' if d == 0 else 'pool'
                    S.op(eng, lambda e, d=d, dst=dst: e.tensor_tensor(
                        out=dst[tb][:].rearrange("p h (j t) -> p h j t", j=4),
                        in0=QTs[:, :, t0:t0 + 512].rearrange("p h (j t) -> p h j t", j=4),
                        in1=sb_ap(wq, d * 512, [[128, 4], [0, 4], [1, 128]]), op=ALU.mult),
                        r=['QTs', 'wq'], w=[(('qf', 'qbk')[d], tb)])
                for j in range(4):
                    n = 4 * i + j
                    kb = ck % 2
                    ck += 1
                    ob = 2 + (ck % 2)
                    k_transposes(n, 0, kb)
                    for hh in range(4):
                        S.op('pe', lambda e, hh=hh, n=n: e.matmul(c.ps[1][:, hh * 128:(hh + 1) * 128],
                                                                 lhsT=KTs[:, hh, n * 128:(n + 1) * 128],
                                                                 rhs=QTs[:, hh, n * 128:(n + 1) * 128], start=True, stop=True),
                             r=['KTs', 'QTs'], w=[('ps', 1)])
                    S.op('dve', lambda e, kb=kb: e.tensor_tensor(out=PT_[kb][:], in0=c.ps[1][:].rearrange("p (h t) -> p h t", h=4),
                                                                 in1=mask[:], op=ALU.mult), r=[('ps', 1), 'mask'], w=[('PT_', kb)])
                    for hh in range(4):
                        o_ap = c.ps[ob][:, hh * 128:(hh + 1) * 128]
                        S.op('pe', lambda e, hh=hh, n=n, o_ap=o_ap, kb=kb: e.matmul(o_ap, lhsT=PT_[kb][:, hh, :],
                                                                                  rhs=Vs[:, n, hh * 128:(hh + 1) * 128],
                                                                                  start=True, stop=False),
                             r=[('PT_', kb), 'Vs'], w=[('ps', ob)])
                        S.op('pe', lambda e, hh=hh, j=j, o_ap=o_ap: e.matmul(o_ap, lhsT=qf[tb][:, hh, j * 128:(j + 1) * 128],
                                                                           rhs=stb[:, hh, :], start=False, stop=False),
                             r=[('qf', tb), 'stb'], w=[('ps', ob)])
                        S.op('pe', lambda e, hh=hh, j=j, n=n, o_ap=o_ap: e.matmul(o_ap, lhsT=qbk[tb][:, hh, j * 128:(j + 1) * 128],
                                                                                rhs=SB[:, n, hh, :], start=False, stop=True),
                             r=[('qbk', tb), 'SB'], w=[('ps', ob)])
                    if n < NB - 1:
                        kv_update(n, 0, kb)
                        S.op('act', lambda e: e.activation(out=stb[:], in_=stt[:], func=AF.Copy), r=['stt'], w=['stb'])
                    o3 = c.ps[ob][:].rearrange("p (h t) -> p h t", h=4)
                    ub = ck % 2
                    S.op('dve', lambda e, o3=o3: e.tensor_reduce(out=sm[:, 0:4], in_=o3, axis=AX.X, op=ALU.add),
                         r=[('ps', ob)], w=['sm'])
                    S.op('act', lambda e, o3=o3: e.activation(out=osq[:], in_=o3, func=AF.Square), r=[('ps', ob)], w=['osq'])
                    S.op('dve', lambda e: e.tensor_reduce(out=sm[:, 4:8], in_=osq[:], axis=AX.X, op=ALU.add), r=['osq'], w=['sm'])
                    S.op('dve', lambda e: e.tensor_scalar(out=sm2[:, 0:4], in0=sm[:, 0:4], scalar1=-1.0 / 128, scalar2=None,
                                                          op0=ALU.mult), r=['sm'], w=['sm2'])
                    S.op('dve', lambda e: e.tensor_tensor(out=sm2[:, 4:8], in0=sm2[:, 0:4], in1=sm2[:, 0:4], op=ALU.mult),
                         r=['sm2'], w=['sm2'])
                    S.op('dve', lambda e: e.scalar_tensor_tensor(out=sm2[:, 8:12], in0=sm[:, 4:8], scalar=1.0 / 128, in1=sm2[:, 4:8],
                                                                 op0=ALU.mult, op1=ALU.subtract), r=['sm', 'sm2'], w=['sm2'])
                    S.op('act', lambda e: e.activation(out=sm2[:, 12:16], in_=sm2[:, 8:12], func=AF.Sqrt, bias=EPS, scale=1.0),
                         r=['sm2'], w=['sm2'])
                    S.op('dve', lambda e: e.reciprocal(out=sm2[:, 16:20], in_=sm2[:, 12:16]), r=['sm2'], w=['sm2'])
                    S.op('dve', lambda e, o3=o3, ub=ub: e.tensor_tensor(out=tmpo[ub][:], in0=o3, in1=bc4(sm2, 0), op=ALU.add),
                         r=[('ps', ob), 'sm2'], w=[('tmpo', ub)])
                    S.op('pool', lambda e, ub=ub: e.tensor_tensor(out=tmpo[ub][:], in0=tmpo[ub][:], in1=bc4(sm2, 16), op=ALU.mult),
                         r=[('tmpo', ub), 'sm2'], w=[('tmpo', ub)])
                    S.op('pool', lambda e, ub=ub, j=j: e.tensor_tensor(
                        out=rtok[ub][:], in0=tmpo[ub][:], in1=GSt[tb][:, j, :].rearrange("p (h t) -> p h t", h=4), op=ALU.mult),
                        r=[('tmpo', ub), ('GSt', tb)], w=[('rtok', ub)])
                    for hh in range(4):
                        S.op('pe', lambda e, hh=hh, ub=ub: e.transpose(psR[:, hh * 128:(hh + 1) * 128], rtok[ub][:, hh, :], c.ident_b[:]),
                             r=[('rtok', ub), 'ident_b'], w=[('ps', 5)])
                    S.op('act', lambda e, j=j: e.activation(out=mixst[tb][:, :, j * 128:(j + 1) * 128],
                                                            in_=psR[:, 0:512].rearrange("p (h t) -> p h t", h=4), func=AF.Copy),
                         r=[('ps', 5)], w=[('mixst', tb)])
                S.dma('pool', c.MIXT.ap()[s].rearrange("(k p) t -> p k t", p=128)[:, 0:4, t0:t0 + 512], mixst[tb][:],
                      r=[('mixst', tb)], w=[('MIXT_r', s, i)])
        S.barrier()


POOL_WINDOWS = (2, 4, 8, 16)


def pass_pool(c, l):
    nc, S, L, NSEQ, NT = c.nc, c.S, c.L, c.NSEQ, c.NT
    LP = L + 16
    with ExitStack() as st:
        def alloc(name, shape, dt):
            return st.enter_context(nc.sbuf_tensor(uname(name), list(shape), dt))
        U = alloc("pl_U", [128, LP], F32)
        W2 = alloc("pl_W2", [128, LP], F32)
        W4 = alloc("pl_W4", [128, LP], F32)
        W8 = alloc("pl_W8", [128, LP], F32)
        W16 = alloc("pl_W16", [128, LP], F32)
        Wn = {2: W2, 4: W4, 8: W8, 16: W16}
        M = alloc("pl_M", [128, L], F32)
        Mb = alloc("pl_Mb", [128, L], BF16)
        O = alloc("pl_O", [128, L], BF16)
        ic = alloc("pl_ic", [128, 4, 16], F32)
        wst = alloc("pl_wst", [128, 2, 128], F32)
        wpb = alloc("pl_wpb", [128, 2, 128], BF16)
        psc = alloc("pl_psc", [128, 2], F32)
        S.dma('sp', ic[:].rearrange("p a b -> p (a b)"), c.k_invcnt.ap().partition_broadcast(128), w=['ic'])
        S.dma('sp', psc[:], c.pool_scale.ap()[l], w=['psc'])
        S.op('dve', lambda e: e.memset(wst[:], 0.0), w=['wst'])
        for g in range(4):
            ct, hf = g // 2, g % 2
            S.dma('sp', wst[hf * 64:(hf + 1) * 64, ct, hf * 64:(hf + 1) * 64], c.pool_w.ap()[l, g], w=['wst'])
        S.op('dve', lambda e: e.tensor_copy(out=wpb[:], in_=wst[:]), r=['wst'], w=['wpb'])
        S.op('pool', lambda e: e.memset(U[:], 0.0), w=['U'])
        rr = 0
        for s in range(NSEQ):
            for ct in range(2):
                S.dma('sp', U[:, 8:8 + L], c.PLT.ap()[s, ct * 128:(ct + 1) * 128, :], r=[('PLT', s, i) for i in range(NT)], w=['U'])
                S.op('dve', lambda e: e.tensor_tensor(out=W2[:, 1:LP], in0=U[:, 0:LP - 1], in1=U[:, 1:LP], op=ALU.add),
                     r=['U'], w=['W2'])
                S.op('pool', lambda e: e.tensor_tensor(out=W4[:, 2:LP - 1], in0=W2[:, 1:LP - 2], in1=W2[:, 3:LP], op=ALU.add),
                     r=['W2'], w=['W4'])
                S.op('dve', lambda e: e.tensor_tensor(out=W8[:, 4:LP - 3], in0=W4[:, 2:LP - 5], in1=W4[:, 6:LP - 1], op=ALU.add),
                     r=['W4'], w=['W8'])
                S.op('pool', lambda e: e.tensor_tensor(out=W16[:, 8:LP - 7], in0=W8[:, 4:LP - 11], in1=W8[:, 12:LP - 3], op=ALU.add),
                     r=['W8'], w=['W16'])
                for hf in range(2):
                    g = ct * 2 + hf
                    w = POOL_WINDOWS[g]
                    Wt = Wn[w]
                    p0, p1 = hf * 64, (hf + 1) * 64
                    eng = 'dve' if hf == 0 else 'pool'
                    S.op('dve', lambda e, Wt=Wt, w=w, p0=p0, p1=p1: e.scalar_tensor_tensor(
                        out=M[p0:p1, :], in0=Wt[p0:p1, 8:8 + L], scalar=1.0 / w, in1=U[p0:p1, 8:8 + L],
                        op0=ALU.mult, op1=ALU.subtract), r=['W%d' % w, 'U'], w=['M'])
                    for (a0, io) in ((0, 0), (L - 8, 8)):
                        S.op(eng, lambda e, Wt=Wt, p0=p0, p1=p1, a0=a0, io=io, g=g: e.tensor_tensor(
                            out=M[p0:p1, a0:a0 + 8], in0=Wt[p0:p1, 8 + a0:16 + a0], in1=ic[p0:p1, g, io:io + 8], op=ALU.mult),
                            r=['W%d' % w, 'ic', 'M'], w=['M'])
                        S.op(eng, lambda e, p0=p0, p1=p1, a0=a0: e.tensor_tensor(
                            out=M[p0:p1, a0:a0 + 8], in0=M[p0:p1, a0:a0 + 8], in1=U[p0:p1, 8 + a0:16 + a0], op=ALU.subtract),
                            r=['U', 'M'], w=['M'])
                S.op('act', lambda e: e.activation(out=Mb[:], in_=M[:], func=AF.Copy), r=['M'], w=['Mb'])
                for i in range(NT):
                    bank = 1 + (rr % 4)
                    rr += 1
                    S.op('pe', lambda e, bank=bank, i=i, ct=ct: e.matmul(c.ps[bank][:], lhsT=wpb[:, ct, :], rhs=Mb[:, i * 512:(i + 1) * 512],
                                                                    start=True, stop=True), r=['wpb', 'Mb'], w=[('ps', bank)])
                    S.op('act', lambda e, bank=bank, i=i, ct=ct: e.activation(out=O[:, i * 512:(i + 1) * 512], in_=c.ps[bank][:],
                                                                         func=AF.Copy, scale=psc[:, ct:ct + 1]),
                         r=[('ps', bank), 'psc'], w=['O'])
                S.dma('pool', c.MIXT.ap()[s, 768 + ct * 128:768 + (ct + 1) * 128, :], O[:], r=['O'], w=[('MIXT_p', s, ct)])
        S.barrier()


def prologue_filters(c):
    nc, S, L, DEPTH, NT = c.nc, c.S, c.L, c.DEPTH, c.NT
    for l in range(DEPTH):
        with ExitStack() as st:
            def alloc(name, shape, dt):
                return st.enter_context(nc.sbuf_tensor(uname(name), list(shape), dt))
            w1 = alloc("hf_w1", [33, 64], F32)
            w2 = alloc("hf_w2", [64, 64], F32)
            w3 = alloc("hf_w3", [64, 1024], F32)
            b1 = alloc("hf_b1", [64, 1], F32)
            b2 = alloc("hf_b2", [64, 1], F32)
            fr = alloc("hf_fr", [64, 1], F32)
            fb = alloc("hf_fb", [64, 2], F32)
            negd = alloc("hf_negd", [128, 2], F32)
            bias = alloc("hf_bias", [128, 4], F32)
            S.dma('sp', w1[:], c.hy_w1.ap()[l], w=['w1'])
            S.dma('sp', w2[:], c.hy_w2.ap()[l], w=['w2'])
            S.dma('sp', w3[:], c.hy_w3.ap()[l * 64:(l + 1) * 64, :], w=['w3'])
            S.dma('sp', b1[:], c.hy_b1.ap()[l], w=['b1'])
            S.dma('sp', b2[:], c.hy_b2.ap()[l], w=['b2'])
            S.dma('sp', fr[:], c.hy_freq.ap()[l], w=['fr'])
            S.dma('sp', negd[:], c.k_negdelta.ap(), w=['negd'])
            S.dma('sp', bias[:], c.hy_bias.ap()[l], w=['bias'])
            S.op('dve', lambda e: e.tensor_tensor(out=fb[:, 0:1], in0=b1[:], in1=fr[:], op=ALU.mult), r=['b1', 'fr'], w=['fb'])
            S.op('dve', lambda e: e.tensor_tensor(out=fb[:, 1:2], in0=b2[:], in1=fr[:], op=ALU.mult), r=['b2', 'fr', 'fb'], w=['fb'])
            FB = [[[alloc("hf_FB%d%d%d" % (g, o, ct), [128, L], F32) for ct in range(2)] for o in range(2)] for g in range(2)]
            feats = alloc("hf_feats", [33, 512], F32)
            tb = alloc("hf_tb", [128, 512], F32)
            a_sb = alloc("hf_a", [64, 512], F32)
            ki = alloc("hf_ki", [64, 512], I32)
            rr_ = alloc("hf_r", [64, 512], F32)
            h1 = alloc("hf_h1", [64, 512], F32)
            h2 = alloc("hf_h2", [64, 512], F32)
            dec = [alloc("hf_dec%d" % i, [128, 512], F32) for i in range(2)]
            asum = alloc("hf_asum", [128, 8], F32)
            tot = alloc("hf_tot", [128, 4], F32)
            stg = [alloc("hf_stg%d" % i, [128, L], BF16) for i in range(2)]

            def sin_layer(psb, fcol, dst, dkey):
                S.op('dve', lambda e: e.tensor_scalar(out=a_sb[:], in0=c.ps[psb][0:64, :], scalar1=fr[:, 0:1], scalar2=fb[:, fcol:fcol + 1],
                                                      op0=ALU.mult, op1=ALU.add), r=[('ps', psb), 'fr', 'fb'], w=['a_sb'])
                S.op('dve', lambda e: e.tensor_scalar(out=ki[:], in0=a_sb[:], scalar1=float(1.0 / (2 * PI)), scalar2=None, op0=ALU.mult),
                     r=['a_sb'], w=['ki'])
                S.op('dve', lambda e: e.scalar_tensor_tensor(out=rr_[:], in0=ki[:], scalar=float(-2 * PI), in1=a_sb[:],
                                                             op0=ALU.mult, op1=ALU.add), r=['ki', 'a_sb'], w=['rr_'])
                S.op('dve', lambda e: e.tensor_scalar(out=rr_[:], in0=rr_[:], scalar1=-3.141592, scalar2=3.141592,
                                                      op0=ALU.max, op1=ALU.min), r=['rr_'], w=['rr_'])
                S.op('act', lambda e: e.activation(out=dst[:], in_=rr_[:], func=AF.Sin), r=['rr_'], w=[dkey])

            rb = 0
            for g in range(2):
                for i in range(NT):
                    S.dma('sp', feats[:], c.k_feats.ap()[g * 33:(g + 1) * 33, i * 512:(i + 1) * 512], w=['feats'])
                    S.dma('sp', tb[:], c.k_feats.ap()[g * 33:g * 33 + 1, i * 512:(i + 1) * 512].partition_broadcast(128), w=['tb'])
                    S.op('pe', lambda e: e.matmul(c.ps[0][0:64, :], lhsT=w1[:], rhs=feats[:], start=True, stop=True),
                         r=['w1', 'feats'], w=[('ps', 0)])
                    sin_layer(0, 0, h1, 'h1')
                    S.op('pe', lambda e: e.matmul(c.ps[1][0:64, :], lhsT=w2[:], rhs=h1[:], start=True, stop=True),
                         r=['w2', 'h1'], w=[('ps', 1)])
                    sin_layer(1, 1, h2, 'h2')
                    for ct in range(2):
                        S.op('act', lambda e, ct=ct: e.activation(out=dec[ct][:], in_=tb[:], func=AF.Exp, scale=negd[:, ct:ct + 1]),
                             r=['tb', 'negd'], w=[('dec', ct)])
                    for o in range(2):
                        for ct in range(2):
                            col = o * 512 + g * 256 + ct * 128
                            bank = 2 + (rb % 4)
                            rb += 1
                            S.op('pe', lambda e, bank=bank, col=col: e.matmul(c.ps[bank][:], lhsT=w3[:, col:col + 128], rhs=h2[:],
                                                                             start=True, stop=True), r=['w3', 'h2'], w=[('ps', bank)])
                            S.op('dve', lambda e, bank=bank, g=g, o=o, ct=ct, i=i: e.tensor_tensor(
                                out=FB[g][o][ct][:, i * 512:(i + 1) * 512], in0=c.ps[bank][:], in1=dec[ct][:], op=ALU.mult),
                                r=[('ps', bank), ('dec', ct)], w=[('FB', g, o, ct)])
            for g in range(2):
                n = L if g == 0 else L - 1
                for o in range(2):
                    for ct in range(2):
                        idx = g * 4 + o * 2 + ct
                        S.op('dve', lambda e, g=g, o=o, ct=ct, idx=idx, n=n: e.tensor_reduce(
                            out=asum[:, idx:idx + 1], in_=FB[g][o][ct][:, 0:n], axis=AX.X, op=ALU.add, apply_absolute_value=True),
                            r=[('FB', g, o, ct)], w=['asum'])
            S.op('dve', lambda e: e.tensor_tensor(out=tot[:], in0=asum[:, 0:4], in1=asum[:, 4:8], op=ALU.add), r=['asum'], w=['tot'])
            S.op('dve', lambda e: e.reciprocal(out=tot[:], in_=tot[:]), r=['tot'], w=['tot'])
            sb_i = 0
            for o in range(2):
                for ct in range(2):
                    oc = o * 2 + ct
                    S.op('dve', lambda e, o=o, ct=ct, oc=oc: e.tensor_scalar(out=FB[0][o][ct][:], in0=FB[0][o][ct][:], scalar1=tot[:, oc:oc + 1],
                                                                          scalar2=None, op0=ALU.mult), r=[('FB', 0, o, ct), 'tot'], w=[('FB', 0, o, ct)])
                    S.op('dve', lambda e, o=o, ct=ct, oc=oc: e.tensor_tensor(out=FB[0][o][ct][:, 0:1], in0=FB[0][o][ct][:, 0:1], in1=bias[:, oc:oc + 1],
                                                                          op=ALU.add), r=[('FB', 0, o, ct), 'bias'], w=[('FB', 0, o, ct)])
                    rows = c.G.ap()[l, o * 256 + ct * 128:o * 256 + (ct + 1) * 128, :]
                    b = sb_i % 2
                    sb_i += 1
                    S.op('act', lambda e, o=o, ct=ct, b=b: e.activation(out=stg[b][:], in_=FB[0][o][ct][:], func=AF.Copy),
                         r=[('FB', 0, o, ct)], w=[('stg', b)])
                    S.dma('pool', rows[:, L - 1:2 * L - 1], stg[b][:], r=[('stg', b)], w=[('G', l, o, ct, 0)])
                    b = sb_i % 2
                    sb_i += 1
                    S.op('pool', lambda e, o=o, ct=ct, oc=oc, b=b: e.tensor_scalar(out=stg[b][:], in0=FB[1][o][ct][:], scalar1=tot[:, oc:oc + 1],
                                                                               scalar2=None, op0=ALU.mult), r=[('FB', 1, o, ct), 'tot'], w=[('stg', b)])
                    S.dma('pool', rows[:, 0:L - 1], stg[b][:, 0:L - 1], r=[('stg', b)], w=[('G', l, o, ct, 1)])
            S.barrier()


def pass_hyena(c, l):
    nc, S, L, NSEQ, NT, NB, NLAG = c.nc, c.S, c.L, c.NSEQ, c.NT, c.NB, c.NLAG
    SN = NSEQ * NB
    gsz = max(1, min(128, 512 // SN))
    ngrp = (128 + gsz - 1) // gsz
    for ct in range(2):
        with ExitStack() as st:
            def alloc(name, shape, dt):
                return st.enter_context(nc.sbuf_tensor(uname(name), list(shape), dt))
            cw = alloc("hy_cw", [128, 6, 3], F32)
            S.dma('sp', cw[:], c.hy_conv.ap()[l], w=['cw'])
            X = [alloc("hy_X%d" % i, [128, L + 2], F32) for i in range(2)]
            acc1 = alloc("hy_acc1", [128, L], F32)
            acc2 = alloc("hy_acc2", [128, L], F32)
            ub = [alloc("hy_ub%d" % i, [128, L], BF16) for i in range(2)]
            UR = alloc("hy_UR", [128, 128, NSEQ, NB], BF16)
            HX1 = alloc("hy_HX1", [128, 128, NSEQ, NB], BF16)
            HX2 = alloc("hy_HX2", [128, 128, NSEQ, NB], BF16)
            KS = [alloc("hy_KS%d" % i, [128, NLAG * 128], BF16) for i in range(2)]
            TM = [UR, HX1, HX2]
            for b in range(2):
                S.op('pool', lambda e, b=b: e.memset(X[b][:, 0:1], 0.0), w=[('X', b)])
                S.op('pool', lambda e, b=b: e.memset(X[b][:, L + 1:L + 2], 0.0), w=[('X', b)])
            it = 0
            tg = 0
            for s in range(NSEQ):
                for r_ in range(3):
                    b = it % 2
                    it += 1
                    tile = r_ * 2 + ct
                    rows = r_ * 256 + ct * 128
                    S.dma('sp', X[b][:, 1:L + 1], c.HYT.ap()[s, rows:rows + 128, :], r=[('HYT', s, i) for i in range(NT)], w=[('X', b)])
                    S.op('act', lambda e, b=b, tile=tile: e.activation(out=acc1[:], in_=X[b][:, 1:L + 1], func=AF.Copy,
                                                                     scale=cw[:, tile, 1:2]), r=[('X', b), 'cw'], w=['acc1'])
                    S.op('dve', lambda e, b=b, tile=tile: e.scalar_tensor_tensor(out=acc2[:], in0=X[b][:, 0:L], scalar=cw[:, tile, 0:1],
                                                                               in1=acc1[:], op0=ALU.mult, op1=ALU.add),
                         r=[('X', b), 'cw', 'acc1'], w=['acc2'])
                    if r_ == 0:
                        o_ap = sb_ap(ub[b], L - 1, [[-1, L]])
                    else:
                        o_ap = ub[b][:]
                    S.op('dve', lambda e, b=b, tile=tile, o_ap=o_ap: e.scalar_tensor_tensor(
                        out=o_ap, in0=X[b][:, 2:L + 2], scalar=cw[:, tile, 2:3], in1=acc2[:], op0=ALU.mult, op1=ALU.add),
                        r=[('X', b), 'cw', 'acc2'], w=[('ub', b)])
                    for g8 in range(NB // 8):
                        bank = 6 + (tg % 2)
                        tg += 1
                        psb = c.ps[bank][:].bitcast(BF16)
                        for q in range(8):
                            blk = g8 * 8 + q
                            S.op('pe', lambda e, psb=psb, q=q, blk=blk, b=b: e.transpose(psb[:, q * 128:(q + 1) * 128],
                                                                                      ub[b][:, blk * 128:(blk + 1) * 128], c.ident_b[:]),
                                 r=[('ub', b), 'ident_b'], w=[('ps', bank)])
                        if r_ == 0:
                            a_first = NB - 1 - g8 * 8
                            dst = sb_ap(TM[0], s * NB + a_first, [[-1, 8], [SN, 128]])
                        else:
                            dst = sb_ap(TM[r_], s * NB + g8 * 8, [[1, 8], [SN, 128]])
                        eng = 'act' if (tg % 2) else 'dve'
                        rkeys = [('ps', bank)]
                        wkeys = [('TM', r_, gi) for gi in range(ngrp)]
                        if eng == 'act':
                            S.op('act', lambda e, dst=dst, psb=psb: e.activation(out=dst, in_=psb.rearrange("p (q t) -> p q t", q=8), func=AF.Copy),
                                 r=rkeys, w=wkeys)
                        else:
                            S.op('dve', lambda e, dst=dst, psb=psb: e.tensor_copy(out=dst, in_=psb.rearrange("p (q t) -> p q t", q=8)),
                                 r=rkeys, w=wkeys)
            kc_i = 0
            for o in range(2):
                GB = HX1 if o == 0 else HX2
                gb_i = 1 if o == 0 else 2
                for gi in range(ngrp):
                    c0 = gi * gsz
                    n_c = min(gsz, 128 - c0)
                    bank = gi % 2
                    first = True
                    for ci in range(n_c):
                        ch = c0 + ci
                        kb = kc_i % 2
                        kc_i += 1
                        src = AP(c.G, ((l * 512 + o * 256 + ct * 128 + ch) * 2 * L), [[1, 128], [1, NLAG * 128]])
                        S.dma('sp', KS[kb][:], src, r=[('G', l, o, ct, 0), ('G', l, o, ct, 1)], w=[('KS', kb)])
                        for d in range(-(NB - 1), NB):
                            a0 = max(0, -d)
                            a1 = min(NB, NB - d)
                            n = a1 - a0
                            o_ap = sb_ap(c.ps[bank], ci * SN + a0 + d, [[NB, NSEQ], [1, n]])
                            r_ap = sb_ap(UR, ch * SN + a0, [[NB, NSEQ], [1, n]])
                            S.op('pe', lambda e, o_ap=o_ap, r_ap=r_ap, kb=kb, d=d, first=first: e.matmul(
                                o_ap, lhsT=KS[kb][:, (d + NB - 1) * 128:(d + NB) * 128], rhs=r_ap,
                                start=first, stop=False, skip_group_check=True),
                                r=[('KS', kb), ('TM', 0, gi)], w=[('ps', bank)])
                            first = False
                    ncol = n_c * SN
                    gflat = sb_ap(GB, c0 * SN, [[1, ncol]])
                    S.op('dve', lambda e, gflat=gflat, bank=bank, ncol=ncol: e.tensor_tensor(
                        out=gflat, in0=c.ps[bank][:, 0:ncol], in1=gflat, op=ALU.mult),
                        r=[('ps', bank), ('TM', gb_i, gi)], w=[('TM', gb_i, gi)])
                    if o == 0:
                        zb = 2 + (gi % 2)
                        S.op('pe', lambda e, zb=zb, gflat=gflat, ncol=ncol: e.matmul(c.ps[zb][:, 0:ncol], lhsT=c.J_b[:], rhs=gflat,
                                                                                    start=True, stop=True),
                             r=[('TM', 1, gi), 'J_b'], w=[('ps', zb)])
                        uflat = sb_ap(UR, c0 * SN, [[1, ncol]])
                        S.op('act', lambda e, zb=zb, uflat=uflat, ncol=ncol: e.activation(out=uflat, in_=c.ps[zb][:, 0:ncol], func=AF.Copy),
                             r=[('ps', zb)], w=[('TM', 0, gi)])
            ost = ub
            it = 0
            for s in range(NSEQ):
                b = it % 2
                it += 1
                for g8 in range(NB // 8):
                    bank = 6 + (tg % 2)
                    tg += 1
                    psb = c.ps[bank][:].bitcast(BF16)
                    for q in range(8):
                        a = g8 * 8 + q
                        i_ap = sb_ap(HX2, s * NB + a, [[SN, 128]])
                        S.op('pe', lambda e, psb=psb, q=q, i_ap=i_ap: e.transpose(psb[:, q * 128:(q + 1) * 128], i_ap, c.ident_b[:]),
                             r=[('TM', 2, gi) for gi in range(ngrp)] + ['ident_b'], w=[('ps', bank)])
                    S.op('act', lambda e, psb=psb, g8=g8, b=b: e.activation(out=ost[b][:, g8 * 1024:(g8 + 1) * 1024], in_=psb, func=AF.Copy),
                         r=[('ps', bank)], w=[('ub', b)])
                S.dma('pool', c.MIXT.ap()[s, 512 + ct * 128:512 + (ct + 1) * 128, :], ost[b][:], r=[('ub', b)], w=[('MIXT_h', s, ct)])
            S.barrier()


def pass_p3(c, l):
    nc, S, L, NSEQ = c.nc, c.S, c.L, c.NSEQ
    TW = 510
    tiles = [(a, min(a + TW, L)) for a in range(0, L, TW)]
    with ExitStack() as st:
        def alloc(name, shape, dt):
            return st.enter_context(nc.sbuf_tensor(uname(name), list(shape), dt))
        alloc_wstage(c, st)
        Wo = load_weight_bf16(c, st, "p3_Wo", c.w_out.ap()[l * D:(l + 1) * D, :], KC, D, 'Wo')
        gmf = alloc("p3_gm", [128, KC], F32)
        S.dma('sp', gmf[:], c.norm_ffn.ap()[l], w=['gmf'])
        Wu = load_weight_bf16(c, st, "p3_Wu", c.w_up.ap()[l * D:(l + 1) * D, :], KC, 2 * DFF, 'Wu', rowscale=gmf, rskey='gmf')
        fcw = alloc("p3_fcw", [128, NJ, 3], F32)
        S.dma('sp', fcw[:], c.ffn_conv.ap()[l], w=['fcw'])
        mx = alloc("p3_mx", [128, KC, 512], BF16)
        xt = alloc("p3_xt", [128, KC, 512], F32)
        xsq = alloc("p3_xsq", [128, KC, 512], BF16)
        rs = alloc("p3_rs", [128, 512], F32)
        h2 = alloc("p3_h2", [128, KC, 512], BF16)
        hid = alloc("p3_hid", [128, NJ, 512], BF16)
        acc = [alloc("p3_acc%d" % i, [128, 512], F32) for i in range(2)]
        gl = [alloc("p3_gl%d" % i, [128, 512], F32) for i in range(2)]
        rr = 0
        jj = 0
        for s in range(NSEQ):
            for (ta, tb_) in tiles:
                ntok = tb_ - ta
                ncol = ntok + 2
                lo = 1 if ta == 0 else 0
                hi = ncol - 1 if tb_ == L else ncol
                fm = lambda T: T.ap()[s].rearrange("(k p) t -> p k t", p=128)
                S.dma('sp', mx[:, :, lo:hi], fm(c.MIXT)[:, :, ta - 1 + lo:ta - 1 + hi], w=['mx'])
                S.dma('sp', xt[:, :, lo:hi], fm(c.XT)[:, :, ta - 1 + lo:ta - 1 + hi], w=['xt'])
                if lo == 1:
                    S.op('pool', lambda e: e.memset(mx[:, :, 0:1], 0.0), w=['mx'])
                    S.op('pool', lambda e: e.memset(xt[:, :, 0:1], 0.0), w=['xt'])
                if hi == ncol - 1:
                    S.op('pool', lambda e, ncol=ncol: e.memset(mx[:, :, ncol - 1:ncol], 0.0), w=['mx'])
                    S.op('pool', lambda e, ncol=ncol: e.memset(xt[:, :, ncol - 1:ncol], 0.0), w=['xt'])
                for m in range(KC):
                    bank = 1 + (rr % 4)
                    rr += 1
                    for k in range(KC):
                        S.op('pe', lambda e, k=k, m=m, bank=bank, ncol=ncol: e.matmul(
                            c.ps[bank][:, 0:ncol], lhsT=Wo[:, k, m * 128:(m + 1) * 128], rhs=mx[:, k, 0:ncol],
                            start=(k == 0), stop=(k == KC - 1)), r=['Wo', 'mx'], w=[('ps', bank)])
                    S.op('dve', lambda e, m=m, bank=bank, ncol=ncol: e.tensor_tensor(
                        out=xt[:, m, 0:ncol], in0=c.ps[bank][:, 0:ncol], in1=xt[:, m, 0:ncol], op=ALU.add),
                        r=[('ps', bank), 'xt'], w=['xt'])
                S.dma('pool', fm(c.X1T)[:, :, ta:tb_], xt[:, :, 1:1 + ntok], r=['xt'], w=[('X1T', s, ta)])
                S.op('act', lambda e, ncol=ncol: e.activation(out=xsq[:, :, 0:ncol], in_=xt[:, :, 0:ncol], func=AF.Square),
                     r=['xt'], w=['xsq'])
                for k in range(KC):
                    S.op('pe', lambda e, k=k, ncol=ncol: e.matmul(c.ps[0][:, 0:ncol], lhsT=c.ones_b[:], rhs=xsq[:, k, 0:ncol],
                                                                  start=(k == 0), stop=(k == KC - 1)),
                         r=['xsq', 'ones_b'], w=[('ps', 0)])
                rms_rstd(c, 0, rs, ncol, ('ps', 0), 'rs', D)
                for k in range(KC):
                    eng = 'dve' if k % 2 == 0 else 'pool'
                    S.op(eng, lambda e, k=k, ncol=ncol: e.tensor_tensor(
                        out=h2[:, k, 0:ncol], in0=xt[:, k, 0:ncol], in1=rs[:, 0:ncol], op=ALU.mult), r=['xt', 'rs'], w=['h2'])
                for j in range(NJ):
                    ab = jj % 2
                    jj += 1
                    bg = 1 + (rr % 4)
                    rr += 1
                    bu = 1 + (rr % 4)
                    rr += 1
                    for k in range(KC):
                        S.op('pe', lambda e, k=k, j=j, bg=bg, ncol=ncol: e.matmul(
                            c.ps[bg][:, 0:ncol], lhsT=Wu[:, k, j * 128:(j + 1) * 128], rhs=h2[:, k, 0:ncol],
                            start=(k == 0), stop=(k == KC - 1)), r=['Wu', 'h2'], w=[('ps', bg)])
                    for k in range(KC):
                        S.op('pe', lambda e, k=k, j=j, bu=bu, ncol=ncol: e.matmul(
                            c.ps[bu][:, 0:ncol], lhsT=Wu[:, k, DFF + j * 128:DFF + (j + 1) * 128], rhs=h2[:, k, 0:ncol],
                            start=(k == 0), stop=(k == KC - 1)), r=['Wu', 'h2'], w=[('ps', bu)])
                    S.op('act', lambda e, j=j, bg=bg, ab=ab, ntok=ntok: e.activation(
                        out=acc[ab][:, 0:ntok], in_=c.ps[bg][:, 1:1 + ntok], func=AF.Copy, scale=fcw[:, j, 1:2]),
                        r=[('ps', bg), 'fcw'], w=[('acc', ab)])
                    S.op('dve', lambda e, j=j, bg=bg, ab=ab, ntok=ntok: e.scalar_tensor_tensor(
                        out=acc[ab][:, 0:ntok], in0=c.ps[bg][:, 0:ntok], scalar=fcw[:, j, 0:1], in1=acc[ab][:, 0:ntok],
                        op0=ALU.mult, op1=ALU.add), r=[('ps', bg), 'fcw', ('acc', ab)], w=[('acc', ab)])
                    S.op('dve', lambda e, j=j, bg=bg, ab=ab, ntok=ntok: e.scalar_tensor_tensor(
                        out=acc[ab][:, 0:ntok], in0=c.ps[bg][:, 2:2 + ntok], scalar=fcw[:, j, 2:3], in1=acc[ab][:, 0:ntok],
                        op0=ALU.mult, op1=ALU.add), r=[('ps', bg), 'fcw', ('acc', ab)], w=[('acc', ab)])
                    S.op('act', lambda e, ab=ab, ntok=ntok: e.activation(out=gl[ab][:, 0:ntok], in_=acc[ab][:, 0:ntok],
                                                                        func=AF.Gelu_apprx_tanh), r=[('acc', ab)], w=[('gl', ab)])
                    S.op('dve', lambda e, j=j, bu=bu, ab=ab, ntok=ntok: e.tensor_tensor(
                        out=hid[:, j, 0:ntok], in0=c.ps[bu][:, 1:1 + ntok], in1=gl[ab][:, 0:ntok], op=ALU.mult),
                        r=[('ps', bu), ('gl', ab)], w=['hid'])
                S.dma('pool', c.HID.ap()[s].rearrange("(j p) t -> p j t", p=128)[:, :, ta:tb_], hid[:, :, 0:ntok],
                      r=['hid'], w=[('HID', s, ta)])
        S.barrier()


def pass_p4(c, l):
    nc, S, L, NSEQ, NT = c.nc, c.S, c.L, c.NSEQ, c.NT
    with ExitStack() as st:
        def alloc(name, shape, dt):
            return st.enter_context(nc.sbuf_tensor(uname(name), list(shape), dt))
        alloc_wstage(c, st)
        Wd = load_weight_bf16(c, st, "p4_Wd", c.w_down.ap()[l * DFF:(l + 1) * DFF, :], NJ, D, 'Wd')
        Wg = load_weight_bf16(c, st, "p4_Wg", c.ple_gate.ap()[l * D:(l + 1) * D, :], KC, D, 'Wg')
        Wp = load_weight_bf16(c, st, "p4_Wp", c.ple_w.ap()[l * 256:(l + 1) * 256, :], 2, D, 'Wp')
        pn = alloc("p4_pn", [128, KC], F32)
        S.dma('sp', pn[:], c.ple_norm.ap()[l], w=['pn'])
        hid = alloc("p4_hid", [128, NJ, 512], BF16)
        x1 = alloc("p4_x1", [128, KC, 512], F32)
        pT = alloc("p4_pT", [128, 2, 512], BF16)
        x2b = alloc("p4_x2b", [128, KC, 512], BF16)
        er = alloc("p4_er", [128, KC, 512], F32)
        esq = alloc("p4_esq", [128, KC, 512], BF16)
        sg = alloc("p4_sg", [128, KC, 512], BF16)
        rs = alloc("p4_rs", [128, 512], F32)
        tmp = [alloc("p4_tmp%d" % i, [128, 512], F32) for i in range(2)]
        rr = 0
        for s in range(NSEQ):
            for i in range(NT):
                t0 = i * 512
                fm = lambda T: T.ap()[s].rearrange("(k p) t -> p k t", p=128)[:, :, t0:t0 + 512]
                S.dma('sp', hid[:], c.HID.ap()[s].rearrange("(j p) t -> p j t", p=128)[:, :, t0:t0 + 512], w=['hid'])
                S.dma('sp', x1[:], fm(c.X1T), w=['x1'])
                S.dma('sp', pT[:], c.PT.ap()[l, s].rearrange("(k p) t -> p k t", p=128)[:, :, t0:t0 + 512], w=['pT'])
                for m in range(KC):
                    bank = 1 + (rr % 4)
                    rr += 1
                    for j in range(NJ):
                        S.op('pe', lambda e, j=j, m=m, bank=bank: e.matmul(
                            c.ps[bank][:], lhsT=Wd[:, j, m * 128:(m + 1) * 128], rhs=hid[:, j, :],
                            start=(j == 0), stop=(j == NJ - 1)), r=['Wd', 'hid'], w=[('ps', bank)])
                    S.op('dve', lambda e, m=m, bank=bank: e.tensor_tensor(out=x1[:, m, :], in0=c.ps[bank][:], in1=x1[:, m, :], op=ALU.add),
                         r=[('ps', bank), 'x1'], w=['x1'])
                    S.op('pool', lambda e, m=m: e.tensor_copy(out=x2b[:, m, :], in_=x1[:, m, :]), r=['x1'], w=['x2b'])
                for m in range(KC):
                    bank = 1 + (rr % 4)
                    rr += 1
                    for k in range(2):
                        S.op('pe', lambda e, k=k, m=m, bank=bank: e.matmul(
                            c.ps[bank][:], lhsT=Wp[:, k, m * 128:(m + 1) * 128], rhs=pT[:, k, :],
                            start=(k == 0), stop=(k == 1)), r=['Wp', 'pT'], w=[('ps', bank)])
                    S.op('act', lambda e, m=m, bank=bank: e.activation(out=er[:, m, :], in_=c.ps[bank][:], func=AF.Copy),
                         r=[('ps', bank)], w=['er'])
                    S.op('act', lambda e, m=m, bank=bank: e.activation(out=esq[:, m, :], in_=c.ps[bank][:], func=AF.Square),
                         r=[('ps', bank)], w=['esq'])
                for k in range(KC):
                    S.op('pe', lambda e, k=k: e.matmul(c.ps[0][:], lhsT=c.ones_b[:], rhs=esq[:, k, :],
                                                       start=(k == 0), stop=(k == KC - 1)), r=['esq', 'ones_b'], w=[('ps', 0)])
                rms_rstd(c, 0, rs, 512, ('ps', 0), 'rs', D)
                for m in range(KC):
                    bank = 1 + (rr % 4)
                    rr += 1
                    for k in range(KC):
                        S.op('pe', lambda e, k=k, m=m, bank=bank: e.matmul(
                            c.ps[bank][:], lhsT=Wg[:, k, m * 128:(m + 1) * 128], rhs=x2b[:, k, :],
                            start=(k == 0), stop=(k == KC - 1)), r=['Wg', 'x2b'], w=[('ps', bank)])
                    S.op('act', lambda e, m=m, bank=bank: e.activation(out=sg[:, m, :], in_=c.ps[bank][:], func=AF.Sigmoid),
                         r=[('ps', bank)], w=['sg'])
                    tb_ = m % 2
                    S.op('dve', lambda e, m=m, tb_=tb_: e.scalar_tensor_tensor(
                        out=tmp[tb_][:], in0=er[:, m, :], scalar=pn[:, m:m + 1], in1=rs[:], op0=ALU.mult, op1=ALU.mult),
                        r=['er', 'pn', 'rs'], w=[('tmp', tb_)])
                    S.op('pool', lambda e, m=m, tb_=tb_: e.tensor_tensor(out=tmp[tb_][:], in0=tmp[tb_][:], in1=sg[:, m, :], op=ALU.mult),
                         r=[('tmp', tb_), 'sg'], w=[('tmp', tb_)])
                    S.op('pool', lambda e, m=m, tb_=tb_: e.tensor_tensor(out=x1[:, m, :], in0=x1[:, m, :], in1=tmp[tb_][:], op=ALU.add),
                         r=[('tmp', tb_), 'x1'], w=['x1'])
                S.dma('pool', fm(c.XT), x1[:], r=['x1'], w=[('XT', s, i)])
        S.barrier()


def epilogue(c):
    nc, S, L, NSEQ, NT = c.nc, c.S, c.L, c.NSEQ, c.NT
    with ExitStack() as st:
        def alloc(name, shape, dt):
            return st.enter_context(nc.sbuf_tensor(uname(name), list(shape), dt))
        nf = alloc("ep_nf", [128, D], F32)
        S.dma('sp', nf[:], c.norm_final.ap().partition_broadcast(128), w=['nf'])
        xt = [alloc("ep_xt%d" % i, [128, KC, 512], F32) for i in range(2)]
        yt = [alloc("ep_yt%d" % i, [128, D], F32) for i in range(2)]
        sq = alloc("ep_sq", [128, D], F32)
        ss = alloc("ep_ss", [128, 4], F32)
        yo = [alloc("ep_yo%d" % i, [128, 4, D], F32) for i in range(2)]
        it = 0
        yb = 0
        for s in range(NSEQ):
            for i in range(NT):
                b = it % 2
                it += 1
                S.dma('sp', xt[b][:], c.XT.ap()[s].rearrange("(k p) t -> p k t", p=128)[:, :, i * 512:(i + 1) * 512],
                      r=[('XT', s, i)], w=[('xt', b)])
                for jb in range(4):
                    y = yb % 2
                    yb += 1
                    for half in range(2):
                        bank = 1 + ((yb * 2 + half) % 4)
                        for q in range(4):
                            k = half * 4 + q
                            S.op('pe', lambda e, bank=bank, q=q, k=k, jb=jb, b=b: e.transpose(
                                c.ps[bank][:, q * 128:(q + 1) * 128], xt[b][:, k, jb * 128:(jb + 1) * 128], c.ident_f[:]),
                                r=[('xt', b), 'ident_f'], w=[('ps', bank)])
                        if half == 0:
                            S.op('act', lambda e, bank=bank, y=y: e.activation(out=yt[y][:, 0:512], in_=c.ps[bank][:], func=AF.Copy),
                                 r=[('ps', bank)], w=[('yt', y)])
                        else:
                            S.op('dve', lambda e, bank=bank, y=y: e.tensor_copy(out=yt[y][:, 512:1024], in_=c.ps[bank][:]),
                                 r=[('ps', bank)], w=[('yt', y)])
                    S.op('pool', lambda e, y=y: e.tensor_tensor(out=sq[:], in0=yt[y][:], in1=yt[y][:], op=ALU.mult), r=[('yt', y)], w=['sq'])
                    S.op('dve', lambda e: e.tensor_reduce(out=ss[:, 0:1], in_=sq[:], axis=AX.X, op=ALU.add), r=['sq'], w=['ss'])
                    S.op('act', lambda e: e.activation(out=ss[:, 1:2], in_=ss[:, 0:1], func=AF.Sqrt, bias=EPS, scale=1.0 / D), r=['ss'], w=['ss'])
                    S.op('dve', lambda e: e.reciprocal(out=ss[:, 2:3], in_=ss[:, 1:2]), r=['ss'], w=['ss'])
                    S.op('dve', lambda e, y=y, jb=jb, b=b: e.scalar_tensor_tensor(out=yo[b][:, jb, :], in0=yt[y][:], scalar=ss[:, 2:3], in1=nf[:],
                                                                               op0=ALU.mult, op1=ALU.mult), r=[('yt', y), 'ss', 'nf'], w=[('yo', b)])
                S.dma('pool', c.y.ap()[s].rearrange("(j p) f -> p j f", p=128)[:, 4 * i:4 * i + 4, :], yo[b][:], r=[('yo', b)], w=[('y', s, i)])
        S.barrier()


def make_consts(L):
    f32 = np.float32
    k = {}
    k["k_ident"] = np.eye(128, dtype=f32)
    k["k_J"] = np.eye(128, dtype=f32)[::-1].copy()
    rot = np.zeros((128, 128), f32)
    for m in range(64):
        rot[m + 64, m] = -1.0
    for m in range(64, 128):
        rot[m - 64, m] = 1.0
    k["k_rot"] = rot
    half = 64
    inv = (np.float32(10000.0) ** (-np.arange(half, dtype=f32) / f32(half))).astype(f32)
    ang = (np.arange(L, dtype=f32)[None, :] * inv[:, None]).astype(f32)
    k["k_cos"] = np.concatenate([np.cos(ang), np.cos(ang)], 0).astype(f32)
    k["k_sin"] = np.concatenate([np.sin(ang), np.sin(ang)], 0).astype(f32)
    t = np.linspace(0.0, 1.0, L, dtype=f32)[:, None]
    bands = np.linspace(1e-4, 15, 16, dtype=f32)
    w = (f32(2.0 * math.pi / L) * np.arange(L, dtype=f32)[:, None] * bands[None, :]).astype(f32)
    feats = np.concatenate([t, np.cos(w), -np.sin(w)], -1).astype(f32).T
    k["k_feats"] = np.stack([feats, feats[:, ::-1]], 0).copy()
    deltas = np.abs(np.linspace(HY_MIN_DECAY, HY_MAX_DECAY, HYW, dtype=f32))
    k["k_negdelta"] = (-deltas).reshape(2, 128).T.copy().astype(f32)
    pos = np.arange(128, dtype=f32)
    k["k_df"] = np.maximum(pos[None, :] - pos[:, None], 0).astype(f32)
    k["k_db"] = np.maximum(pos[:, None] - pos[None, :], 0).astype(f32)
    k["k_kvec"] = np.stack([127.0 - pos, pos], 1).astype(f32)
    k["k_qrow"] = np.stack([pos + 1.0, 128.0 - pos], 0).astype(f32)
    ic = np.zeros((4, 16), f32)
    tt = np.arange(L)
    for g, win in enumerate(POOL_WINDOWS):
        lo = np.clip(tt - win // 2, 0, L - 1)
        hi = np.clip(tt + win // 2 - 1, 0, L - 1)
        cnt = (hi - lo + 1).astype(f32)
        ic[g, 0:8] = 1.0 / cnt[0:8]
        ic[g, 8:16] = 1.0 / cnt[L - 8:L]
    k["k_invcnt"] = ic.reshape(1, 64)
    return k


def layout_weights(W, DEPTH):
    f = lambda a: np.ascontiguousarray(np.asarray(a, dtype=np.float32))
    o = {}
    vec8 = lambda a: f(np.asarray(a).reshape(DEPTH, KC, 128).transpose(0, 2, 1))
    o["norm_mix"] = vec8(W["norm_mix"])
    o["norm_ffn"] = vec8(W["norm_ffn"])
    o["ple_norm"] = vec8(W["ple_norm"])
    o["w_in"] = f(W["w_in"])
    o["ret_decay_fwd"] = f(W["ret_decay_fwd"])
    o["ret_decay_bwd"] = f(W["ret_decay_bwd"])
    o["ret_gn"] = f(W["ret_gn"])
    o["hy_short_conv"] = f(np.asarray(W["hy_short_conv"]).reshape(DEPTH, 3, 6, 128).transpose(0, 3, 2, 1))
    o["hy_w1"] = f(W["hy_w1"])
    o["hy_b1"] = f(np.asarray(W["hy_b1"]).reshape(DEPTH, 64, 1))
    o["hy_freq"] = f(np.asarray(W["hy_freq"]).reshape(DEPTH, 64, 1))
    o["hy_w2"] = f(W["hy_w2"])
    o["hy_b2"] = f(np.asarray(W["hy_b2"]).reshape(DEPTH, 64, 1))
    o["hy_w3"] = f(W["hy_w3"])
    o["hy_bias"] = f(np.asarray(W["hy_bias"]).reshape(DEPTH, 2, 2, 128).transpose(0, 3, 1, 2).reshape(DEPTH, 128, 4))
    o["pool_w"] = f(W["pool_w"])
    o["pool_scale"] = f(np.asarray(W["pool_scale"]).reshape(DEPTH, 2, 128).transpose(0, 2, 1))
    o["w_out"] = f(W["w_out"])
    o["ffn_w_up"] = f(W["ffn_w_up"])
    o["ffn_conv"] = f(np.asarray(W["ffn_conv"]).reshape(DEPTH, 3, NJ, 128).transpose(0, 3, 2, 1))
    o["ffn_w_down"] = f(W["ffn_w_down"])
    o["ple_w"] = f(W["ple_w"])
    o["ple_gate_w"] = f(W["ple_gate_w"])
    o["norm_final"] = f(np.asarray(W["norm_final"]).reshape(1, D))
    return o


PADDED = ("w_in", "hy_w3", "w_out", "ffn_w_up", "ffn_w_down", "ple_w", "ple_gate_w", "k_cos", "k_sin", "k_feats")


def add_core_rows(m, cid):
    o = dict(m)
    for k in PADDED:
        a = np.asarray(o[k], dtype=np.float32)
        a2 = a.reshape(-1, a.shape[-1])
        o[k] = np.concatenate([a2, np.full((1, a2.shape[1]), float(cid), np.float32)], 0)
    return o


_CACHE = {}


def kernel(**inputs):
    L, DEPTH = L_FULL, DEPTH_FULL
    xp = np.asarray(inputs["x_prompt"], dtype=np.float32)
    xs = np.asarray(inputs["x_sample"], dtype=np.float32)
    pp = np.asarray(inputs["p_prompt"], dtype=np.float32)
    psm = np.asarray(inputs["p_sample"], dtype=np.float32)
    nP, nS = xp.shape[0], xs.shape[0]
    def seq_x(g):
        return xp[g] if g < nP else xs[g - nP]
    def seq_p(g):
        return pp[:, g] if g < nP else psm[:, g - nP]
    slots = []
    for cid in range(8):
        if cid < 4:
            slots.append([3 * cid, 3 * cid + 1, 3 * cid + 2])
        else:
            a = 12 + 2 * (cid - 4)
            slots.append([a, a + 1, a + 1])
    if "nc" not in _CACHE:
        _CACHE["nc"] = build(L, NSLOT, DEPTH)[0]
        _CACHE["consts"] = make_consts(L)
    nc = _CACHE["nc"]
    shared = dict(_CACHE["consts"])
    shared.update(layout_weights(inputs, DEPTH))
    in_maps = []
    for cid in range(8):
        m = add_core_rows(shared, cid)
        m["x"] = np.ascontiguousarray(np.stack([seq_x(g) for g in slots[cid]], 0))
        m["p"] = np.ascontiguousarray(np.stack([seq_p(g) for g in slots[cid]], 1))
        in_maps.append(m)
    res = run_bass_kernel_spmd(nc, in_maps, core_ids=list(range(8)))
    y_all = np.zeros((nP + nS, L, D), np.float32)
    for cid in range(8):
        y = np.asarray(res.results[cid]["y"])
        n_real = 3 if cid < 4 else 2
        for j in range(n_real):
            y_all[slots[cid][j]] = y[j]
    return (y_all[:nP].copy(), y_all[nP:].copy())
```

```python
import math
from contextlib import ExitStack
import numpy as np
import concourse.bass as bass
import concourse.mybir as mybir
from concourse.bass_utils import run_bass_kernel_spmd
from concourse.ap import AP

F32 = mybir.dt.float32
BF16 = mybir.dt.bfloat16
I32 = mybir.dt.int32
AF = mybir.ActivationFunctionType
ALU = mybir.AluOpType
AX = mybir.AxisListType

D = 1024
KC = 8
DEPTH_FULL = 4
L_FULL = 4096
NSLOT = 3
RW = 512
HYW = 256
INW = 3072
DFF = 2816
NJ = 22
EPS = 1e-6
PI = math.pi
HY_MIN_DECAY = math.log(1e-2) / 1.5
HY_MAX_DECAY = math.log(1e-2) / 0.3


class Sched:
    ENG = ('pe', 'act', 'dve', 'pool', 'sp')

    def __init__(s, nc, n_dma=40):
        s.nc = nc
        s.e = dict(pe=nc.tensor, act=nc.scalar, dve=nc.vector, pool=nc.gpsimd, sp=nc.sync)
        s.sem = {k: nc.alloc_semaphore("s_" + k) for k in ('pe', 'act', 'dve', 'pool')}
        s.cnt = {k: 0 for k in s.sem}
        s.dsem = [nc.alloc_semaphore("d%d" % i) for i in range(n_dma)]
        s.dcnt = [0] * n_dma
        s.drr = 0
        s.known = {k: {} for k in s.ENG}
        s.res = {}
        s.n_wait = 0
        s.n_ops = 0
        s.excl_ps = True

    def _semobj(s, name):
        return s.sem[name] if name in s.sem else s.dsem[name]

    def _wait(s, eng, name, val):
        if val <= 0 or s.known[eng].get(name, 0) >= val:
            return
        s.e[eng].wait_ge(s._semobj(name), val)
        s.known[eng][name] = val
        s.n_wait += 1

    def _deps(s, eng, reads, writes):
        best = {}
        for k in reads:
            r = s.res.get(k)
            if r and r[0]:
                n, v = r[0]
                if best.get(n, 0) < v:
                    best[n] = v
        for k in writes:
            r = s.res.get(k)
            if r:
                if r[0]:
                    n, v = r[0]
                    if best.get(n, 0) < v:
                        best[n] = v
                for n, v in r[1].items():
                    if best.get(n, 0) < v:
                        best[n] = v
        for n, v in best.items():
            if n == 'pe' and eng == 'pe':
                continue
            s._wait(eng, n, v)

    def _commit(s, ev, reads, writes):
        n, v = ev
        for k in reads:
            r = s.res.get(k)
            if r is None:
                r = s.res[k] = [None, {}]
            r[1][n] = v
        for k in writes:
            s.res[k] = [ev, {}]

    def op(s, eng, fn, r=(), w=()):
        if s.excl_ps:
            pr = [k for k in r if isinstance(k, tuple) and k[0] == 'ps']
            if pr:
                r = [k for k in r if k not in pr]
                w = list(w) + pr
        s._deps(eng, r, w)
        ins = fn(s.e[eng])
        s.cnt[eng] += 1
        ins.then_inc(s.sem[eng], 1)
        s._commit((eng, s.cnt[eng]), r, w)
        s.n_ops += 1

    def dma(s, q, out, in_, r=(), w=()):
        j = s.drr
        s.drr = (s.drr + 1) % len(s.dsem)
        s._wait(q, j, s.dcnt[j])
        s._deps(q, r, w)
        ins = s.e[q].dma_start(out=out, in_=in_)
        s.dcnt[j] += 16
        ins.then_inc(s.dsem[j], 16)
        s._commit((j, s.dcnt[j]), r, w)
        s.n_ops += 1

    def barrier(s, engs=None):
        for eng in (engs or s.ENG):
            for k in s.sem:
                s._wait(eng, k, s.cnt[k])
            for j in range(len(s.dsem)):
                s._wait(eng, j, s.dcnt[j])
        s.res = {}


def sb_ap(t, off, dims):
    pst = t[:].ap[0][0]
    return AP(t, off, [[pst, 128]] + [list(d) for d in dims])


class Ctx:
    pass


_UID = [0]


def uname(n):
    _UID[0] += 1
    return "%s_u%d" % (n, _UID[0])


ALL_STAGES = ('filt', 'tr', 'p1', 'ret', 'pool', 'hy', 'p3', 'p4', 'epi')


def build(L=L_FULL, NSEQ=NSLOT, DEPTH=DEPTH_FULL, taps=(), stages=ALL_STAGES):
    NT = L // 512
    NB = L // 128
    NLAG = 2 * NB - 1
    nc = bass.Bass("TRN2", target_bir_lowering=False)
    S = Sched(nc)
    c = Ctx()
    c.nc, c.S, c.L, c.NSEQ, c.DEPTH, c.NT, c.NB, c.NLAG = nc, S, L, NSEQ, DEPTH, NT, NB, NLAG
    c.taps = taps
    c.stages = stages

    def din(name, shape, dt=F32):
        return nc.dram_tensor(name, list(shape), dt, kind="ExternalInput")

    c.x = din("x", [NSEQ, L, D])
    c.p = din("p", [DEPTH, NSEQ, L, 256])
    c.norm_mix = din("norm_mix", [DEPTH, 128, KC])
    c.w_in = din("w_in", [DEPTH * D + 1, INW])
    c.dec_f = din("ret_decay_fwd", [DEPTH, 4])
    c.dec_b = din("ret_decay_bwd", [DEPTH, 4])
    c.ret_gn = din("ret_gn", [DEPTH, RW])
    c.hy_conv = din("hy_short_conv", [DEPTH, 128, 6, 3])
    c.hy_w1 = din("hy_w1", [DEPTH, 33, 64])
    c.hy_b1 = din("hy_b1", [DEPTH, 64, 1])
    c.hy_freq = din("hy_freq", [DEPTH, 64, 1])
    c.hy_w2 = din("hy_w2", [DEPTH, 64, 64])
    c.hy_b2 = din("hy_b2", [DEPTH, 64, 1])
    c.hy_w3 = din("hy_w3", [DEPTH * 64 + 1, 1024])
    c.hy_bias = din("hy_bias", [DEPTH, 128, 4])
    c.pool_w = din("pool_w", [DEPTH, 4, 64, 64])
    c.pool_scale = din("pool_scale", [DEPTH, 128, 2])
    c.w_out = din("w_out", [DEPTH * D + 1, D])
    c.norm_ffn = din("norm_ffn", [DEPTH, 128, KC])
    c.w_up = din("ffn_w_up", [DEPTH * D + 1, 2 * DFF])
    c.ffn_conv = din("ffn_conv", [DEPTH, 128, NJ, 3])
    c.w_down = din("ffn_w_down", [DEPTH * DFF + 1, D])
    c.ple_w = din("ple_w", [DEPTH * 256 + 1, D])
    c.ple_gate = din("ple_gate_w", [DEPTH * D + 1, D])
    c.ple_norm = din("ple_norm", [DEPTH, 128, KC])
    c.norm_final = din("norm_final", [1, D])
    c.k_ident = din("k_ident", [128, 128])
    c.k_J = din("k_J", [128, 128])
    c.k_rot = din("k_rot", [128, 128])
    c.k_cos = din("k_cos", [129, L])
    c.k_sin = din("k_sin", [129, L])
    c.k_feats = din("k_feats", [67, L])
    c.k_negdelta = din("k_negdelta", [128, 2])
    c.k_df = din("k_df", [128, 128])
    c.k_db = din("k_db", [128, 128])
    c.k_kvec = din("k_kvec", [128, 2])
    c.k_qrow = din("k_qrow", [2, 128])
    c.k_invcnt = din("k_invcnt", [1, 64])
    c.y = nc.dram_tensor("y", [NSEQ, L, D], F32, kind="ExternalOutput")

    def dscr(name, shape, dt):
        return nc.dram_tensor(name, list(shape), dt)

    c.XT = dscr("XT", [NSEQ, D, L], F32)
    c.PT = dscr("PT", [DEPTH, NSEQ, 256, L], BF16)
    c.QT = dscr("QT", [NSEQ, RW, L], BF16)
    c.KT = dscr("KT", [NSEQ, RW, L], BF16)
    c.V = dscr("V", [NSEQ, L, RW], BF16)
    c.GS = dscr("GS", [NSEQ, L, RW], BF16)
    c.HYT = dscr("HYT", [NSEQ, 768, L], F32)
    c.PLT = dscr("PLT", [NSEQ, 256, L], F32)
    c.MIXT = dscr("MIXT", [NSEQ, D, L], BF16)
    c.HID = dscr("HID", [NSEQ, DFF, L], BF16)
    c.X1T = dscr("X1T", [NSEQ, D, L], F32)
    c.G = dscr("G", [DEPTH, 512, 2 * L], BF16)
    c.dbg = {}
    for nm in taps:
        t_ = getattr(c, nm)
        c.dbg[nm] = nc.dram_tensor("dbg_" + nm, list(t_.shape), t_.dtype, kind="ExternalOutput")

    c.ps = [nc.alloc_psum_tensor("ps%d" % i, [128, 512], F32) for i in range(8)]

    with ExitStack() as gst:
        def galloc(name, shape, dt):
            return gst.enter_context(nc.sbuf_tensor(uname(name), list(shape), dt))
        c.ident_f = galloc("ident_f", [128, 128], F32)
        c.ident_b = galloc("ident_b", [128, 128], BF16)
        c.J_b = galloc("J_b", [128, 128], BF16)
        c.rot_b = galloc("rot_b", [128, 128], BF16)
        c.ones_b = galloc("ones_b", [128, 128], BF16)
        stg = galloc("cstage", [128, 128], F32)
        S.dma('sp', c.ident_f[:], c.k_ident.ap(), w=['ident_f'])
        S.op('dve', lambda e: e.tensor_copy(out=c.ident_b[:], in_=c.ident_f[:]), r=['ident_f'], w=['ident_b'])
        S.dma('sp', stg[:], c.k_J.ap(), w=['cstage'])
        S.op('dve', lambda e: e.tensor_copy(out=c.J_b[:], in_=stg[:]), r=['cstage'], w=['J_b'])
        S.dma('sp', stg[:], c.k_rot.ap(), w=['cstage'])
        S.op('dve', lambda e: e.tensor_copy(out=c.rot_b[:], in_=stg[:]), r=['cstage'], w=['rot_b'])
        S.op('dve', lambda e: e.memset(c.ones_b[:], 1.0), w=['ones_b'])
        S.barrier()

        st_ = c.stages
        if 'filt' in st_:
            prologue_filters(c)
        if 'tr' in st_:
            prologue_transposes(c)
        for l in range(DEPTH):
            if 'p1' in st_:
                pass_p1(c, l)
            if 'ret' in st_:
                pass_ret(c, l)
            if 'pool' in st_:
                pass_pool(c, l)
            if 'hy' in st_:
                pass_hyena(c, l)
            if 'p3' in st_:
                pass_p3(c, l)
            if 'p4' in st_:
                pass_p4(c, l)
        if 'epi' in st_:
            epilogue(c)
        S.barrier()
        for nm in taps:
            S.dma('sp', c.dbg[nm].ap(), getattr(c, nm).ap())
        S.barrier(['sp', 'pool'])
    return nc, c


def prologue_transposes(c):
    nc, S, L, NSEQ, DEPTH, NT = c.nc, c.S, c.L, c.NSEQ, c.DEPTH, c.NT
    with ExitStack() as st:
        def alloc(name, shape, dt):
            return st.enter_context(nc.sbuf_tensor(uname(name), list(shape), dt))
        xin = [alloc("pt_xin%d" % i, [128, 4, D], F32) for i in range(2)]
        xst = [alloc("pt_xst%d" % i, [128, KC, 512], F32) for i in range(2)]
        pin = [alloc("pt_pin%d" % i, [128, 4, 256], F32) for i in range(2)]
        pbf = [alloc("pt_pbf%d" % i, [128, 4, 256], BF16) for i in range(2)]
        pst = [alloc("pt_pst%d" % i, [128, 2, 512], BF16) for i in range(2)]
        it = 0
        for s in range(NSEQ):
            for i in range(NT):
                b = it % 2
                it += 1
                src = c.x.ap()[s].rearrange("(j p) f -> p j f", p=128)[:, 4 * i:4 * i + 4, :]
                S.dma('sp', xin[b][:], src, w=[('xin', b)])
                g = 0
                for jb in range(4):
                    for half in range(2):
                        bank = 1 + (g % 4)
                        g += 1
                        for q in range(4):
                            kc = half * 4 + q
                            S.op('pe', lambda e, bank=bank, q=q, kc=kc, jb=jb, b=b: e.transpose(
                                c.ps[bank][:, q * 128:(q + 1) * 128], xin[b][:, jb, kc * 128:(kc + 1) * 128], c.ident_f[:]),
                                r=[('xin', b)], w=[('ps', bank)])
                        eng = 'dve' if (g % 2) else 'act'
                        if eng == 'dve':
                            S.op('dve', lambda e, bank=bank, half=half, jb=jb, b=b: e.tensor_copy(
                                out=xst[b][:, half * 4:half * 4 + 4, jb * 128:(jb + 1) * 128],
                                in_=c.ps[bank][:].rearrange("p (q t) -> p q t", q=4)),
                                r=[('ps', bank)], w=[('xst', b)])
                        else:
                            S.op('act', lambda e, bank=bank, half=half, jb=jb, b=b: e.activation(
                                out=xst[b][:, half * 4:half * 4 + 4, jb * 128:(jb + 1) * 128],
                                in_=c.ps[bank][:].rearrange("p (q t) -> p q t", q=4), func=AF.Copy),
                                r=[('ps', bank)], w=[('xst', b)])
                dst = c.XT.ap()[s].rearrange("(k p) t -> p k t", p=128)[:, :, i * 512:(i + 1) * 512]
                S.dma('pool', dst, xst[b][:], r=[('xst', b)], w=[('XT', s, i)])
        it = 0
        for l in range(DEPTH):
            for s in range(NSEQ):
                for i in range(NT):
                    b = it % 2
                    it += 1
                    src = c.p.ap()[l, s].rearrange("(j p) f -> p j f", p=128)[:, 4 * i:4 * i + 4, :]
                    S.dma('sp', pin[b][:], src, w=[('pin', b)])
                    S.op('dve', lambda e, b=b: e.tensor_copy(out=pbf[b][:], in_=pin[b][:]), r=[('pin', b)], w=[('pbf', b)])
                    for kc in range(2):
                        bank = 5 + kc
                        psb = c.ps[bank][:].bitcast(BF16)
                        for jb in range(4):
                            S.op('pe', lambda e, psb=psb, jb=jb, kc=kc, b=b: e.transpose(
                                psb[:, jb * 128:(jb + 1) * 128], pbf[b][:, jb, kc * 128:(kc + 1) * 128], c.ident_b[:]),
                                r=[('pbf', b)], w=[('ps', bank)])
                        S.op('act', lambda e, psb=psb, kc=kc, b=b: e.activation(
                            out=pst[b][:, kc, :], in_=psb[:, 0:512], func=AF.Copy),
                            r=[('ps', bank)], w=[('pst', b)])
                    dst = c.PT.ap()[l, s].rearrange("(k p) t -> p k t", p=128)[:, :, i * 512:(i + 1) * 512]
                    S.dma('pool', dst, pst[b][:], r=[('pst', b)], w=[('PT', l, s, i)])
        S.barrier()


def load_weight_bf16(c, st, name, src_rows_ap, nk, ncols, key, stage_cols=1024, rowscale=None, rskey=None):
    nc, S = c.nc, c.S
    wb = st.enter_context(nc.sbuf_tensor(uname(name), [128, nk, ncols], BF16))
    if not hasattr(c, 'wstage'):
        raise RuntimeError("wstage missing")
    src = src_rows_ap.rearrange("(k p) n -> p k n", p=128)
    idx = 0
    for k in range(nk):
        for c0 in range(0, ncols, stage_cols):
            cw = min(stage_cols, ncols - c0)
            b = c.wstage_i % 2
            c.wstage_i += 1
            S.dma('sp', c.wstage[b][:, 0:cw], src[:, k, c0:c0 + cw], w=[('wstage', b)])
            eng = ('dve', 'pool', 'act')[idx % 3]
            idx += 1
            rk = [('wstage', b)] + ([rskey] if rskey else [])
            if rowscale is None:
                if eng == 'act':
                    S.op('act', lambda e, b=b, k=k, c0=c0, cw=cw: e.activation(
                        out=wb[:, k, c0:c0 + cw], in_=c.wstage[b][:, 0:cw], func=AF.Copy), r=rk, w=[key])
                else:
                    S.op(eng, lambda e, b=b, k=k, c0=c0, cw=cw: e.tensor_copy(
                        out=wb[:, k, c0:c0 + cw], in_=c.wstage[b][:, 0:cw]), r=rk, w=[key])
            else:
                if eng == 'act':
                    S.op('act', lambda e, b=b, k=k, c0=c0, cw=cw: e.activation(
                        out=wb[:, k, c0:c0 + cw], in_=c.wstage[b][:, 0:cw], func=AF.Copy, scale=rowscale[:, k:k + 1]), r=rk, w=[key])
                else:
                    S.op(eng, lambda e, b=b, k=k, c0=c0, cw=cw: e.tensor_scalar(
                        out=wb[:, k, c0:c0 + cw], in0=c.wstage[b][:, 0:cw], scalar1=rowscale[:, k:k + 1], scalar2=None, op0=ALU.mult),
                        r=rk, w=[key])
    return wb


def alloc_wstage(c, st, cols=1024):
    c.wstage = [st.enter_context(c.nc.sbuf_tensor(uname("wstage%d" % i), [128, cols], F32)) for i in range(2)]
    c.wstage_i = 0


def rms_rstd(c, ps_bank, rs, n, key_ps, key_rs, dim):
    S = c.S
    S.op('act', lambda e: e.activation(out=rs[:, 0:n], in_=c.ps[ps_bank][:, 0:n], func=AF.Sqrt,
                                       bias=EPS, scale=1.0 / dim), r=[key_ps], w=[key_rs])
    S.op('dve', lambda e: e.reciprocal(out=rs[:, 0:n], in_=rs[:, 0:n]), r=[key_rs], w=[key_rs])


def pass_p1(c, l):
    nc, S, L, NSEQ, NT = c.nc, c.S, c.L, c.NSEQ, c.NT
    with ExitStack() as st:
        def alloc(name, shape, dt):
            return st.enter_context(nc.sbuf_tensor(uname(name), list(shape), dt))
        alloc_wstage(c, st)
        gm = alloc("p1_gm", [128, KC], F32)
        S.dma('sp', gm[:], c.norm_mix.ap()[l], w=['gm'])
        Wb = load_weight_bf16(c, st, "p1_W", c.w_in.ap()[l * D:(l + 1) * D, :], KC, INW, 'Wb', rowscale=gm, rskey='gm')
        cos = alloc("p1_cos", [128, L], F32)
        sin = alloc("p1_sin", [128, L], F32)
        gn = alloc("p1_gn", [128, RW], F32)
        S.dma('sp', cos[:], c.k_cos.ap()[0:128, :], w=['cos'])
        S.dma('sp', sin[:], c.k_sin.ap()[0:128, :], w=['sin'])
        S.dma('sp', gn[:], c.ret_gn.ap()[l:l + 1, :].partition_broadcast(128), w=['gn'])
        xt_ = [alloc("p1_xt%d" % i_, [128, KC, 512], F32) for i_ in range(2)]
        xsq = alloc("p1_xsq", [128, KC, 512], BF16)
        rs = alloc("p1_rs", [128, 512], F32)
        h = [alloc("p1_h%d" % i, [128, KC, 512], BF16) for i in range(2)]
        qb = [alloc("p1_qb%d" % i, [128, 512], BF16) for i in range(2)]
        t1 = [alloc("p1_t1%d" % i, [128, 512], F32) for i in range(2)]
        t2 = [alloc("p1_t2%d" % i, [128, 512], F32) for i in range(2)]
        sl = [alloc("p1_sl%d" % i, [128, 512], F32) for i in range(2)]
        qk_out = alloc("p1_qko", [128, 8, 512], BF16)
        hy_out = alloc("p1_hyo", [128, 6, 512], F32)
        pl_out = alloc("p1_plo", [128, 2, 512], F32)
        v_out = alloc("p1_vo", [128, 4, 512], BF16)
        gs_out = alloc("p1_gso", [128, 4, 512], BF16)
        scl_q = 128.0 ** -0.5
        tiles = [(s_, i_) for s_ in range(NSEQ) for i_ in range(NT)]

        def prep(ti):
            s_, i_ = tiles[ti]
            hb = ti % 2
            xt = xt_[hb]
            t0 = i_ * 512
            S.dma('sp', xt[:], c.XT.ap()[s_].rearrange("(k p) t -> p k t", p=128)[:, :, t0:t0 + 512],
                  r=[('XT', s_, i_)], w=[('xt', hb)])
            S.op('act', lambda e: e.activation(out=xsq[:], in_=xt[:], func=AF.Square), r=[('xt', hb)], w=['xsq'])
            for k in range(KC):
                S.op('pe', lambda e, k=k: e.matmul(c.ps[0][:], lhsT=c.ones_b[:], rhs=xsq[:, k, :],
                                                   start=(k == 0), stop=(k == KC - 1)),
                     r=['xsq', 'ones_b'], w=[('ps', 0)])
            rms_rstd(c, 0, rs, 512, ('ps', 0), 'rs', D)
            for k in range(KC):
                eng = 'dve' if k % 2 == 0 else 'pool'
                S.op(eng, lambda e, k=k: e.tensor_tensor(
                    out=h[hb][:, k, :], in0=xt[:, k, :], in1=rs[:], op=ALU.mult), r=[('xt', hb), 'rs'], w=[('h', hb)])

        rr = 0
        prep(0)
        for ti, (s, i) in enumerate(tiles):
            t0 = i * 512
            hb = ti % 2
            mt = [('q', j, j * 128) for j in range(4)] + [('k', j, 512 + j * 128) for j in range(4)] + \
                 [('hy', j, 2048 + j * 128) for j in range(6)] + [('pl', j, 2816 + j * 128) for j in range(2)]
            pending = []
            for kind, j, col in mt:
                bank = 1 + (rr % 4)
                rr += 1
                for k in range(KC):
                    S.op('pe', lambda e, k=k, bank=bank, col=col: e.matmul(
                        c.ps[bank][:], lhsT=Wb[:, k, col:col + 128], rhs=h[hb][:, k, :],
                        start=(k == 0), stop=(k == KC - 1)), r=['Wb', ('h', hb)], w=[('ps', bank)])
                for fn_ in pending:
                    fn_()
                pending = []
                if kind in ('q', 'k'):
                    b2 = rr % 2
                    rb = 5 + b2
                    oi = j if kind == 'q' else 4 + j
                    sc = scl_q if kind == 'q' else 1.0
                    S.op('act', lambda e, bank=bank, b2=b2: e.activation(out=qb[b2][:], in_=c.ps[bank][:], func=AF.Copy),
                         r=[('ps', bank)], w=[('qb', b2)])

                    def rope_rest(bank=bank, b2=b2, rb=rb, oi=oi, sc=sc):
                        S.op('pe', lambda e: e.matmul(c.ps[rb][:], lhsT=c.rot_b[:], rhs=qb[b2][:], start=True, stop=True),
                             r=[('qb', b2), 'rot_b'], w=[('ps', rb)])
                        S.op('dve', lambda e: e.scalar_tensor_tensor(
                            out=t1[b2][:], in0=c.ps[bank][:], scalar=sc, in1=cos[:, t0:t0 + 512],
                            op0=ALU.mult, op1=ALU.mult), r=[('ps', bank), 'cos'], w=[('t1', b2)])
                        S.op('dve', lambda e: e.scalar_tensor_tensor(
                            out=t2[b2][:], in0=c.ps[rb][:], scalar=sc, in1=sin[:, t0:t0 + 512],
                            op0=ALU.mult, op1=ALU.mult), r=[('ps', rb), 'sin'], w=[('t2', b2)])
                        S.op('pool', lambda e: e.tensor_tensor(
                            out=qk_out[:, oi, :], in0=t1[b2][:], in1=t2[b2][:], op=ALU.add),
                            r=[('t1', b2), ('t2', b2)], w=['qk_out'])
                    pending.append(rope_rest)
                elif kind == 'hy':
                    S.op('act', lambda e, bank=bank, j=j: e.activation(out=hy_out[:, j, :], in_=c.ps[bank][:], func=AF.Copy),
                         r=[('ps', bank)], w=['hy_out'])
                else:
                    S.op('dve', lambda e, bank=bank, j=j: e.tensor_copy(out=pl_out[:, j, :], in_=c.ps[bank][:]),
                         r=[('ps', bank)], w=['pl_out'])
            for fn_ in pending:
                fn_()
            if ti + 1 < len(tiles):
                prep(ti + 1)
            for j in range(4):
                bank = 1 + (rr % 4)
                rr += 1
                for k in range(KC):
                    S.op('pe', lambda e, k=k, bank=bank, j=j: e.matmul(
                        c.ps[bank][:], lhsT=h[hb][:, k, j * 128:(j + 1) * 128], rhs=Wb[:, k, 1024:1536],
                        start=(k == 0), stop=(k == KC - 1)), r=['Wb', ('h', hb)], w=[('ps', bank)])
                S.op('dve', lambda e, bank=bank, j=j: e.tensor_copy(out=v_out[:, j, :], in_=c.ps[bank][:]),
                     r=[('ps', bank)], w=['v_out'])
                bank = 1 + (rr % 4)
                rr += 1
                b2 = rr % 2
                for k in range(KC):
                    S.op('pe', lambda e, k=k, bank=bank, j=j: e.matmul(
                        c.ps[bank][:], lhsT=h[hb][:, k, j * 128:(j + 1) * 128], rhs=Wb[:, k, 1536:2048],
                        start=(k == 0), stop=(k == KC - 1)), r=['Wb', ('h', hb)], w=[('ps', bank)])
                S.op('act', lambda e, bank=bank, b2=b2: e.activation(out=sl[b2][:], in_=c.ps[bank][:], func=AF.Silu),
                     r=[('ps', bank)], w=[('sl', b2)])
                S.op('pool', lambda e, b2=b2, j=j: e.tensor_tensor(out=gs_out[:, j, :], in0=sl[b2][:], in1=gn[:], op=ALU.mult),
                     r=[('sl', b2), 'gn'], w=['gs_out'])
            fm = lambda T, n: T.ap()[s].rearrange("(k p) t -> p k t", p=128)[:, 0:n, t0:t0 + 512]
            tm = lambda T: T.ap()[s].rearrange("(j p) f -> p j f", p=128)[:, 4 * i:4 * i + 4, :]
            S.dma('pool', fm(c.QT, 4), qk_out[:, 0:4, :], r=['qk_out'], w=[('QT', s, i)])
            S.dma('pool', fm(c.KT, 4), qk_out[:, 4:8, :], r=['qk_out'], w=[('KT', s, i)])
            S.dma('pool', fm(c.HYT, 6), hy_out[:], r=['hy_out'], w=[('HYT', s, i)])
            S.dma('pool', fm(c.PLT, 2), pl_out[:], r=['pl_out'], w=[('PLT', s, i)])
            S.dma('pool', tm(c.V), v_out[:], r=['v_out'], w=[('V', s, i)])
            S.dma('pool', tm(c.GS), gs_out[:], r=['gs_out'], w=[('GS', s, i)])
        S.barrier()


def pass_ret(c, l):
    nc, S, L, NSEQ, NT, NB = c.nc, c.S, c.L, c.NSEQ, c.NT, c.NB
    with ExitStack() as st:
        def alloc(name, shape, dt):
            return st.enter_context(nc.sbuf_tensor(uname(name), list(shape), dt))
        lgr = alloc("rt_lgr", [128, 8], F32)
        lg = alloc("rt_lg", [128, 8], F32)
        df = alloc("rt_df", [128, 128], F32)
        db = alloc("rt_db", [128, 128], F32)
        kvec = alloc("rt_kvec", [128, 2], F32)
        qrow = alloc("rt_qrow", [128, 2, 128], F32)
        mask = alloc("rt_mask", [128, 4, 128], F32)
        mtmp = alloc("rt_mtmp", [128, 128], F32)
        wk = alloc("rt_wk", [128, 2, 4], F32)
        wq = alloc("rt_wq", [128, 2, 4, 128], F32)
        decv = alloc("rt_decv", [128, 8], F32)
        S.dma('sp', lgr[:, 0:4], c.dec_f.ap()[l:l + 1, :].partition_broadcast(128), w=['lgr'])
        S.dma('sp', lgr[:, 4:8], c.dec_b.ap()[l:l + 1, :].partition_broadcast(128), w=['lgr'])
        S.dma('sp', df[:], c.k_df.ap(), w=['df'])
        S.dma('sp', db[:], c.k_db.ap(), w=['db'])
        S.dma('sp', kvec[:], c.k_kvec.ap(), w=['kvec'])
        S.dma('sp', qrow[:, 0, :], c.k_qrow.ap()[0:1, :].partition_broadcast(128), w=['qrow'])
        S.dma('sp', qrow[:, 1, :], c.k_qrow.ap()[1:2, :].partition_broadcast(128), w=['qrow'])
        S.op('act', lambda e: e.activation(out=lg[:], in_=lgr[:], func=AF.Exp, scale=-1.0), r=['lgr'], w=['lg'])
        S.op('act', lambda e: e.activation(out=lg[:], in_=lg[:], func=AF.Ln, bias=1.0, scale=1.0), r=['lg'], w=['lg'])
        S.op('dve', lambda e: e.tensor_scalar(out=lg[:], in0=lg[:], scalar1=-1.0, scalar2=None, op0=ALU.mult), r=['lg'], w=['lg'])
        for hh in range(4):
            S.op('dve', lambda e, hh=hh: e.tensor_scalar(out=mtmp[:], in0=df[:], scalar1=lg[:, hh:hh + 1], scalar2=None,
                                                         op0=ALU.mult), r=['df', 'lg'], w=['mtmp'])
            S.op('dve', lambda e, hh=hh: e.scalar_tensor_tensor(out=mtmp[:], in0=db[:], scalar=lg[:, 4 + hh:5 + hh], in1=mtmp[:],
                                                                op0=ALU.mult, op1=ALU.add), r=['db', 'lg', 'mtmp'], w=['mtmp'])
            S.op('act', lambda e, hh=hh: e.activation(out=mask[:, hh, :], in_=mtmp[:], func=AF.Exp), r=['mtmp'], w=['mask'])
            for d in range(2):
                S.op('act', lambda e, hh=hh, d=d: e.activation(out=wq[:, d, hh, :], in_=qrow[:, d, :], func=AF.Exp,
                                                               scale=lg[:, 4 * d + hh:4 * d + hh + 1]),
                     r=['qrow', 'lg'], w=['wq'])
        for d in range(2):
            S.op('act', lambda e, d=d: e.activation(out=wk[:, d, :], in_=lg[:, 4 * d:4 * d + 4], func=AF.Exp,
                                                    scale=kvec[:, d:d + 1]), r=['kvec', 'lg'], w=['wk'])
        S.op('act', lambda e: e.activation(out=decv[:], in_=lg[:], func=AF.Exp, scale=128.0), r=['lg'], w=['decv'])

        QTs = alloc("rt_QT", [128, 4, L], BF16)
        KTs = alloc("rt_KT", [128, 4, L], BF16)
        Vs = alloc("rt_V", [128, NB, RW], BF16)
        SB = alloc("rt_SB", [128, NB, 4, 128], BF16)
        stt = alloc("rt_st", [128, 4, 128], F32)
        stb = alloc("rt_stb", [128, 4, 128], BF16)
        ktw = [alloc("rt_ktw%d" % i, [128, 4, 128], BF16) for i in range(2)]
        PT_ = [alloc("rt_PT%d" % i, [128, 4, 128], BF16) for i in range(2)]
        qf = [alloc("rt_qf%d" % i, [128, 4, 512], BF16) for i in range(2)]
        qbk = [alloc("rt_qbk%d" % i, [128, 4, 512], BF16) for i in range(2)]
        GSt = [alloc("rt_GS%d" % i, [128, 4, RW], BF16) for i in range(2)]
        osq = alloc("rt_osq", [128, 4, 128], F32)
        sm = alloc("rt_sm", [128, 8], F32)
        sm2 = alloc("rt_sm2", [128, 20], F32)
        tmpo = [alloc("rt_tmpo%d" % i, [128, 4, 128], F32) for i in range(2)]
        rtok = [alloc("rt_rtok%d" % i, [128, 4, 128], BF16) for i in range(2)]
        mixst = [alloc("rt_mix%d" % i, [128, 4, 512], BF16) for i in range(2)]
        psT = c.ps[0][:].bitcast(BF16)
        psR = c.ps[5][:].bitcast(BF16)

        def bc4(t, off):
            return sb_ap(t, off, [[1, 4], [0, 128]])

        def k_transposes(n, w_dir, kb):
            for hh in range(4):
                S.op('pe', lambda e, hh=hh: e.transpose(psT[:, hh * 128:(hh + 1) * 128], KTs[:, hh, n * 128:(n + 1) * 128],
                                                        c.ident_b[:]), r=['KTs', 'ident_b'], w=[('ps', 0)])
            S.op('dve', lambda e: e.tensor_tensor(out=ktw[kb][:], in0=psT[:, 0:512].rearrange("p (h t) -> p h t", h=4),
                                                  in1=bc4(wk, w_dir * 4), op=ALU.mult),
                 r=[('ps', 0), 'wk'], w=[('ktw', kb)])

        def kv_update(n, d, kb):
            for hh in range(4):
                S.op('pe', lambda e, hh=hh: e.matmul(c.ps[4][:, hh * 128:(hh + 1) * 128], lhsT=ktw[kb][:, hh, :],
                                                     rhs=Vs[:, n, hh * 128:(hh + 1) * 128], start=True, stop=True),
                     r=[('ktw', kb), 'Vs'], w=[('ps', 4)])
            S.op('pool', lambda e: e.tensor_tensor(out=stt[:], in0=stt[:], in1=bc4(decv, d * 4), op=ALU.mult),
                 r=['stt', 'decv'], w=['stt'])
            S.op('dve', lambda e: e.tensor_tensor(out=stt[:], in0=stt[:], in1=c.ps[4][:].rearrange("p (h t) -> p h t", h=4),
                                                  op=ALU.add), r=['stt', ('ps', 4)], w=['stt'])

        def sT_bank(n):
            return (1, 6)[n % 2]

        def emit_scores(n):
            kb = n % 2
            k_transposes(n, 0, kb)
            sb_ = sT_bank(n)
            for hh in range(4):
                S.op('pe', lambda e, hh=hh: e.matmul(c.ps[sb_][:, hh * 128:(hh + 1) * 128],
                                                     lhsT=KTs[:, hh, n * 128:(n + 1) * 128],
                                                     rhs=QTs[:, hh, n * 128:(n + 1) * 128], start=True, stop=True),
                     r=['KTs', 'QTs'], w=[('ps', sb_)])
            S.op('dve', lambda e: e.tensor_tensor(out=PT_[kb][:], in0=c.ps[sb_][:].rearrange("p (h t) -> p h t", h=4),
                                                  in1=mask[:], op=ALU.mult), r=[('ps', sb_), 'mask'], w=[('PT_', kb)])

        it = 0
        for s in range(NSEQ):
            S.dma('sp', QTs[:], c.QT.ap()[s].rearrange("(h p) t -> p h t", p=128), r=[('QT', s, i) for i in range(NT)], w=['QTs'])
            S.dma('sp', KTs[:], c.KT.ap()[s].rearrange("(h p) t -> p h t", p=128), r=[('KT', s, i) for i in range(NT)], w=['KTs'])
            S.dma('sp', Vs[:], c.V.ap()[s].rearrange("(n p) f -> p n f", p=128), r=[('V', s, i) for i in range(NT)], w=['Vs'])
            S.op('pool', lambda e: e.memset(stt[:], 0.0), w=['stt'])
            if NB > 1:
                k_transposes(NB - 1, 1, (NB - 1) % 2)
            for n in range(NB - 1, -1, -1):
                S.op('act', lambda e, n=n: e.activation(out=SB[:, n, :, :], in_=stt[:], func=AF.Copy), r=['stt'], w=['SB'])
                if n > 0:
                    if n - 1 > 0:
                        k_transposes(n - 1, 1, (n - 1) % 2)
                    kv_update(n, 1, n % 2)
            S.op('pool', lambda e: e.memset(stt[:], 0.0), w=['stt'])
            S.op('pool', lambda e: e.memset(stb[:], 0.0), w=['stb'])
            pending = []
            emit_scores(0)
            for i in range(NT):
                t0 = i * 512
                tb = it % 2
                it += 1
                S.dma('sp', GSt[tb][:], c.GS.ap()[s].rearrange("(j p) f -> p j f", p=128)[:, 4 * i:4 * i + 4, :],
                      r=[('GS', s, i)], w=[('GSt', tb)])
                for d, dst in ((0, qf), (1, qbk)):
                    eng = 'dve' if d == 0 else 'pool'
                    S.op(eng, lambda e, d=d, dst=dst: e.tensor_tensor(
                        out=dst[tb][:].rearrange("p h (j t) -> p h j t", j=4),
                        in0=QTs[:, :, t0:t0 + 512].rearrange("p h (j t) -> p h j t", j=4),
                        in1=sb_ap(wq, d * 512, [[128, 4], [0, 4], [1, 128]]), op=ALU.mult),
                        r=['QTs', 'wq'], w=[(('qf', 'qbk')[d], tb)])
                for j in range(4):
                    n = 4 * i + j
                    kb = n % 2
                    ob = 2 + (n % 2)
                    ub = n % 2
                    for hh in range(4):
                        o_ap = c.ps[ob][:, hh * 128:(hh + 1) * 128]
                        S.op('pe', lambda e, hh=hh, o_ap=o_ap: e.matmul(o_ap, lhsT=PT_[kb][:, hh, :],
                                                                      rhs=Vs[:, n, hh * 128:(hh + 1) * 128],
                                                                      start=True, stop=False),
                             r=[('PT_', kb), 'Vs'], w=[('ps', ob)])
                        S.op('pe', lambda e, hh=hh, o_ap=o_ap: e.matmul(o_ap, lhsT=qf[tb][:, hh, j * 128:(j + 1) * 128],
                                                                      rhs=stb[:, hh, :], start=False, stop=False),
                             r=[('qf', tb), 'stb'], w=[('ps', ob)])
                        S.op('pe', lambda e, hh=hh, o_ap=o_ap: e.matmul(o_ap, lhsT=qbk[tb][:, hh, j * 128:(j + 1) * 128],
                                                                      rhs=SB[:, n, hh, :], start=False, stop=True),
                             r=[('qbk', tb), 'SB'], w=[('ps', ob)])
                    if n < NB - 1:
                        kv_update(n, 0, kb)
                        S.op('act', lambda e: e.activation(out=stb[:], in_=stt[:], func=AF.Copy), r=['stt'], w=['stb'])
                        emit_scores(n + 1)
                    for fn_ in pending:
                        fn_()
                    pending = []
                    o3 = c.ps[ob][:].rearrange("p (h t) -> p h t", h=4)
                    S.op('dve', lambda e, o3=o3: e.tensor_reduce(out=sm[:, 0:4], in_=o3, axis=AX.X, op=ALU.add),
                         r=[('ps', ob)], w=['sm'])
                    S.op('act', lambda e, o3=o3: e.activation(out=osq[:], in_=o3, func=AF.Square), r=[('ps', ob)], w=['osq'])
                    S.op('dve', lambda e: e.tensor_reduce(out=sm[:, 4:8], in_=osq[:], axis=AX.X, op=ALU.add), r=['osq'], w=['sm'])
                    S.op('dve', lambda e: e.tensor_scalar(out=sm2[:, 0:4], in0=sm[:, 0:4], scalar1=-1.0 / 128, scalar2=None,
                                                          op0=ALU.mult), r=['sm'], w=['sm2'])
                    S.op('dve', lambda e: e.tensor_tensor(out=sm2[:, 4:8], in0=sm2[:, 0:4], in1=sm2[:, 0:4], op=ALU.mult),
                         r=['sm2'], w=['sm2'])
                    S.op('dve', lambda e: e.scalar_tensor_tensor(out=sm2[:, 8:12], in0=sm[:, 4:8], scalar=1.0 / 128, in1=sm2[:, 4:8],
                                                                 op0=ALU.mult, op1=ALU.subtract), r=['sm', 'sm2'], w=['sm2'])
                    S.op('act', lambda e: e.activation(out=sm2[:, 12:16], in_=sm2[:, 8:12], func=AF.Sqrt, bias=EPS, scale=1.0),
                         r=['sm2'], w=['sm2'])
                    S.op('dve', lambda e: e.reciprocal(out=sm2[:, 16:20], in_=sm2[:, 12:16]), r=['sm2'], w=['sm2'])
                    S.op('dve', lambda e, o3=o3, ub=ub: e.tensor_tensor(out=tmpo[ub][:], in0=o3, in1=bc4(sm2, 0), op=ALU.add),
                         r=[('ps', ob), 'sm2'], w=[('tmpo', ub)])
                    S.op('pool', lambda e, ub=ub: e.tensor_tensor(out=tmpo[ub][:], in0=tmpo[ub][:], in1=bc4(sm2, 16), op=ALU.mult),
                         r=[('tmpo', ub), 'sm2'], w=[('tmpo', ub)])
                    S.op('pool', lambda e, ub=ub, j=j, tb=tb: e.tensor_tensor(
                        out=rtok[ub][:], in0=tmpo[ub][:], in1=GSt[tb][:, j, :].rearrange("p (h t) -> p h t", h=4), op=ALU.mult),
                        r=[('tmpo', ub), ('GSt', tb)], w=[('rtok', ub)])

                    def finish(ub=ub, j=j, tb=tb, s=s, i=i, t0=t0):
                        for hh in range(4):
                            S.op('pe', lambda e, hh=hh: e.transpose(psR[:, hh * 128:(hh + 1) * 128], rtok[ub][:, hh, :], c.ident_b[:]),
                                 r=[('rtok', ub), 'ident_b'], w=[('ps', 5)])
                        S.op('act', lambda e: e.activation(out=mixst[tb][:, :, j * 128:(j + 1) * 128],
                                                           in_=psR[:, 0:512].rearrange("p (h t) -> p h t", h=4), func=AF.Copy),
                             r=[('ps', 5)], w=[('mixst', tb)])
                        if j == 3:
                            S.dma('pool', c.MIXT.ap()[s].rearrange("(k p) t -> p k t", p=128)[:, 0:4, t0:t0 + 512], mixst[tb][:],
                                  r=[('mixst', tb)], w=[('MIXT_r', s, i)])
                    pending.append(finish)
            for fn_ in pending:
                fn_()
            pending = []
        S.barrier()


POOL_WINDOWS = (2, 4, 8, 16)


def pass_pool(c, l):
    nc, S, L, NSEQ, NT = c.nc, c.S, c.L, c.NSEQ, c.NT
    LP = L + 16
    with ExitStack() as st:
        def alloc(name, shape, dt):
            return st.enter_context(nc.sbuf_tensor(uname(name), list(shape), dt))
        U = alloc("pl_U", [128, LP], F32)
        W2 = alloc("pl_W2", [128, LP], F32)
        W4 = alloc("pl_W4", [128, LP], F32)
        W8 = alloc("pl_W8", [128, LP], F32)
        W16 = alloc("pl_W16", [128, LP], F32)
        Wn = {2: W2, 4: W4, 8: W8, 16: W16}
        M = alloc("pl_M", [128, L], F32)
        Mb = alloc("pl_Mb", [128, L], BF16)
        O = alloc("pl_O", [128, L], BF16)
        ic = alloc("pl_ic", [128, 4, 16], F32)
        wst = alloc("pl_wst", [128, 2, 128], F32)
        wpb = alloc("pl_wpb", [128, 2, 128], BF16)
        psc = alloc("pl_psc", [128, 2], F32)
        S.dma('sp', ic[:].rearrange("p a b -> p (a b)"), c.k_invcnt.ap().partition_broadcast(128), w=['ic'])
        S.dma('sp', psc[:], c.pool_scale.ap()[l], w=['psc'])
        S.op('dve', lambda e: e.memset(wst[:], 0.0), w=['wst'])
        for g in range(4):
            ct, hf = g // 2, g % 2
            S.dma('sp', wst[hf * 64:(hf + 1) * 64, ct, hf * 64:(hf + 1) * 64], c.pool_w.ap()[l, g], w=['wst'])
        S.op('dve', lambda e: e.tensor_copy(out=wpb[:], in_=wst[:]), r=['wst'], w=['wpb'])
        S.op('pool', lambda e: e.memset(U[:], 0.0), w=['U'])
        rr = 0
        for s in range(NSEQ):
            for ct in range(2):
                S.dma('sp', U[:, 8:8 + L], c.PLT.ap()[s, ct * 128:(ct + 1) * 128, :], r=[('PLT', s, i) for i in range(NT)], w=['U'])
                S.op('dve', lambda e: e.tensor_tensor(out=W2[:, 1:LP], in0=U[:, 0:LP - 1], in1=U[:, 1:LP], op=ALU.add),
                     r=['U'], w=['W2'])
                S.op('pool', lambda e: e.tensor_tensor(out=W4[:, 2:LP - 1], in0=W2[:, 1:LP - 2], in1=W2[:, 3:LP], op=ALU.add),
                     r=['W2'], w=['W4'])
                S.op('dve', lambda e: e.tensor_tensor(out=W8[:, 4:LP - 3], in0=W4[:, 2:LP - 5], in1=W4[:, 6:LP - 1], op=ALU.add),
                     r=['W4'], w=['W8'])
                S.op('pool', lambda e: e.tensor_tensor(out=W16[:, 8:LP - 7], in0=W8[:, 4:LP - 11], in1=W8[:, 12:LP - 3], op=ALU.add),
                     r=['W8'], w=['W16'])
                for hf in range(2):
                    g = ct * 2 + hf
                    w = POOL_WINDOWS[g]
                    Wt = Wn[w]
                    p0, p1 = hf * 64, (hf + 1) * 64
                    eng = 'dve' if hf == 0 else 'pool'
                    S.op('dve', lambda e, Wt=Wt, w=w, p0=p0, p1=p1: e.scalar_tensor_tensor(
                        out=M[p0:p1, :], in0=Wt[p0:p1, 8:8 + L], scalar=1.0 / w, in1=U[p0:p1, 8:8 + L],
                        op0=ALU.mult, op1=ALU.subtract), r=['W%d' % w, 'U'], w=['M'])
                    for (a0, io) in ((0, 0), (L - 8, 8)):
                        S.op(eng, lambda e, Wt=Wt, p0=p0, p1=p1, a0=a0, io=io, g=g: e.tensor_tensor(
                            out=M[p0:p1, a0:a0 + 8], in0=Wt[p0:p1, 8 + a0:16 + a0], in1=ic[p0:p1, g, io:io + 8], op=ALU.mult),
                            r=['W%d' % w, 'ic', 'M'], w=['M'])
                        S.op(eng, lambda e, p0=p0, p1=p1, a0=a0: e.tensor_tensor(
                            out=M[p0:p1, a0:a0 + 8], in0=M[p0:p1, a0:a0 + 8], in1=U[p0:p1, 8 + a0:16 + a0], op=ALU.subtract),
                            r=['U', 'M'], w=['M'])
                S.op('act', lambda e: e.activation(out=Mb[:], in_=M[:], func=AF.Copy), r=['M'], w=['Mb'])
                for i in range(NT):
                    bank = 1 + (rr % 4)
                    rr += 1
                    S.op('pe', lambda e, bank=bank, i=i, ct=ct: e.matmul(c.ps[bank][:], lhsT=wpb[:, ct, :], rhs=Mb[:, i * 512:(i + 1) * 512],
                                                                    start=True, stop=True), r=['wpb', 'Mb'], w=[('ps', bank)])
                    S.op('act', lambda e, bank=bank, i=i, ct=ct: e.activation(out=O[:, i * 512:(i + 1) * 512], in_=c.ps[bank][:],
                                                                         func=AF.Copy, scale=psc[:, ct:ct + 1]),
                         r=[('ps', bank), 'psc'], w=['O'])
                S.dma('pool', c.MIXT.ap()[s, 768 + ct * 128:768 + (ct + 1) * 128, :], O[:], r=['O'], w=[('MIXT_p', s, ct)])
        S.barrier()


def prologue_filters(c):
    nc, S, L, DEPTH, NT = c.nc, c.S, c.L, c.DEPTH, c.NT
    for l in range(DEPTH):
        with ExitStack() as st:
            def alloc(name, shape, dt):
                return st.enter_context(nc.sbuf_tensor(uname(name), list(shape), dt))
            w1 = alloc("hf_w1", [33, 64], F32)
            w2 = alloc("hf_w2", [64, 64], F32)
            w3 = alloc("hf_w3", [64, 1024], F32)
            b1 = alloc("hf_b1", [64, 1], F32)
            b2 = alloc("hf_b2", [64, 1], F32)
            fr = alloc("hf_fr", [64, 1], F32)
            fb = alloc("hf_fb", [64, 2], F32)
            negd = alloc("hf_negd", [128, 2], F32)
            bias = alloc("hf_bias", [128, 4], F32)
            S.dma('sp', w1[:], c.hy_w1.ap()[l], w=['w1'])
            S.dma('sp', w2[:], c.hy_w2.ap()[l], w=['w2'])
            S.dma('sp', w3[:], c.hy_w3.ap()[l * 64:(l + 1) * 64, :], w=['w3'])
            S.dma('sp', b1[:], c.hy_b1.ap()[l], w=['b1'])
            S.dma('sp', b2[:], c.hy_b2.ap()[l], w=['b2'])
            S.dma('sp', fr[:], c.hy_freq.ap()[l], w=['fr'])
            S.dma('sp', negd[:], c.k_negdelta.ap(), w=['negd'])
            S.dma('sp', bias[:], c.hy_bias.ap()[l], w=['bias'])
            S.op('dve', lambda e: e.tensor_tensor(out=fb[:, 0:1], in0=b1[:], in1=fr[:], op=ALU.mult), r=['b1', 'fr'], w=['fb'])
            S.op('dve', lambda e: e.tensor_tensor(out=fb[:, 1:2], in0=b2[:], in1=fr[:], op=ALU.mult), r=['b2', 'fr', 'fb'], w=['fb'])
            FB = [[[alloc("hf_FB%d%d%d" % (g, o, ct), [128, L], F32) for ct in range(2)] for o in range(2)] for g in range(2)]
            feats = alloc("hf_feats", [33, 512], F32)
            tb = alloc("hf_tb", [128, 512], F32)
            a_sb = alloc("hf_a", [64, 512], F32)
            ki = alloc("hf_ki", [64, 512], I32)
            rr_ = alloc("hf_r", [64, 512], F32)
            h1 = alloc("hf_h1", [64, 512], F32)
            h2 = alloc("hf_h2", [64, 512], F32)
            dec = [alloc("hf_dec%d" % i, [128, 512], F32) for i in range(2)]
            asum = alloc("hf_asum", [128, 8], F32)
            tot = alloc("hf_tot", [128, 4], F32)
            stg = [alloc("hf_stg%d" % i, [128, L], BF16) for i in range(2)]

            def sin_layer(psb, fcol, dst, dkey):
                S.op('dve', lambda e: e.tensor_scalar(out=a_sb[:], in0=c.ps[psb][0:64, :], scalar1=fr[:, 0:1], scalar2=fb[:, fcol:fcol + 1],
                                                      op0=ALU.mult, op1=ALU.add), r=[('ps', psb), 'fr', 'fb'], w=['a_sb'])
                S.op('dve', lambda e: e.tensor_scalar(out=ki[:], in0=a_sb[:], scalar1=float(1.0 / (2 * PI)), scalar2=None, op0=ALU.mult),
                     r=['a_sb'], w=['ki'])
                S.op('dve', lambda e: e.scalar_tensor_tensor(out=rr_[:], in0=ki[:], scalar=float(-2 * PI), in1=a_sb[:],
                                                             op0=ALU.mult, op1=ALU.add), r=['ki', 'a_sb'], w=['rr_'])
                S.op('dve', lambda e: e.tensor_scalar(out=rr_[:], in0=rr_[:], scalar1=-3.141592, scalar2=3.141592,
                                                      op0=ALU.max, op1=ALU.min), r=['rr_'], w=['rr_'])
                S.op('act', lambda e: e.activation(out=dst[:], in_=rr_[:], func=AF.Sin), r=['rr_'], w=[dkey])

            rb = 0
            for g in range(2):
                for i in range(NT):
                    S.dma('sp', feats[:], c.k_feats.ap()[g * 33:(g + 1) * 33, i * 512:(i + 1) * 512], w=['feats'])
                    S.dma('sp', tb[:], c.k_feats.ap()[g * 33:g * 33 + 1, i * 512:(i + 1) * 512].partition_broadcast(128), w=['tb'])
                    S.op('pe', lambda e: e.matmul(c.ps[0][0:64, :], lhsT=w1[:], rhs=feats[:], start=True, stop=True),
                         r=['w1', 'feats'], w=[('ps', 0)])
                    sin_layer(0, 0, h1, 'h1')
                    S.op('pe', lambda e: e.matmul(c.ps[1][0:64, :], lhsT=w2[:], rhs=h1[:], start=True, stop=True),
                         r=['w2', 'h1'], w=[('ps', 1)])
                    sin_layer(1, 1, h2, 'h2')
                    for ct in range(2):
                        S.op('act', lambda e, ct=ct: e.activation(out=dec[ct][:], in_=tb[:], func=AF.Exp, scale=negd[:, ct:ct + 1]),
                             r=['tb', 'negd'], w=[('dec', ct)])
                    for o in range(2):
                        for ct in range(2):
                            col = o * 512 + g * 256 + ct * 128
                            bank = 2 + (rb % 4)
                            rb += 1
                            S.op('pe', lambda e, bank=bank, col=col: e.matmul(c.ps[bank][:], lhsT=w3[:, col:col + 128], rhs=h2[:],
                                                                             start=True, stop=True), r=['w3', 'h2'], w=[('ps', bank)])
                            S.op('dve', lambda e, bank=bank, g=g, o=o, ct=ct, i=i: e.tensor_tensor(
                                out=FB[g][o][ct][:, i * 512:(i + 1) * 512], in0=c.ps[bank][:], in1=dec[ct][:], op=ALU.mult),
                                r=[('ps', bank), ('dec', ct)], w=[('FB', g, o, ct)])
            for g in range(2):
                n = L if g == 0 else L - 1
                for o in range(2):
                    for ct in range(2):
                        idx = g * 4 + o * 2 + ct
                        S.op('dve', lambda e, g=g, o=o, ct=ct, idx=idx, n=n: e.tensor_reduce(
                            out=asum[:, idx:idx + 1], in_=FB[g][o][ct][:, 0:n], axis=AX.X, op=ALU.add, apply_absolute_value=True),
                            r=[('FB', g, o, ct)], w=['asum'])
            S.op('dve', lambda e: e.tensor_tensor(out=tot[:], in0=asum[:, 0:4], in1=asum[:, 4:8], op=ALU.add), r=['asum'], w=['tot'])
            S.op('dve', lambda e: e.reciprocal(out=tot[:], in_=tot[:]), r=['tot'], w=['tot'])
            sb_i = 0
            for o in range(2):
                for ct in range(2):
                    oc = o * 2 + ct
                    S.op('dve', lambda e, o=o, ct=ct, oc=oc: e.tensor_scalar(out=FB[0][o][ct][:], in0=FB[0][o][ct][:], scalar1=tot[:, oc:oc + 1],
                                                                          scalar2=None, op0=ALU.mult), r=[('FB', 0, o, ct), 'tot'], w=[('FB', 0, o, ct)])
                    S.op('dve', lambda e, o=o, ct=ct, oc=oc: e.tensor_tensor(out=FB[0][o][ct][:, 0:1], in0=FB[0][o][ct][:, 0:1], in1=bias[:, oc:oc + 1],
                                                                          op=ALU.add), r=[('FB', 0, o, ct), 'bias'], w=[('FB', 0, o, ct)])
                    rows = c.G.ap()[l, o * 256 + ct * 128:o * 256 + (ct + 1) * 128, :]
                    b = sb_i % 2
                    sb_i += 1
                    S.op('act', lambda e, o=o, ct=ct, b=b: e.activation(out=stg[b][:], in_=FB[0][o][ct][:], func=AF.Copy),
                         r=[('FB', 0, o, ct)], w=[('stg', b)])
                    S.dma('pool', rows[:, L - 1:2 * L - 1], stg[b][:], r=[('stg', b)], w=[('G', l, o, ct, 0)])
                    b = sb_i % 2
                    sb_i += 1
                    S.op('pool', lambda e, o=o, ct=ct, oc=oc, b=b: e.tensor_scalar(out=stg[b][:], in0=FB[1][o][ct][:], scalar1=tot[:, oc:oc + 1],
                                                                               scalar2=None, op0=ALU.mult), r=[('FB', 1, o, ct), 'tot'], w=[('stg', b)])
                    S.dma('pool', rows[:, 0:L - 1], stg[b][:, 0:L - 1], r=[('stg', b)], w=[('G', l, o, ct, 1)])
            S.barrier()


def pass_hyena(c, l):
    nc, S, L, NSEQ, NT, NB, NLAG = c.nc, c.S, c.L, c.NSEQ, c.NT, c.NB, c.NLAG
    SN = NSEQ * NB
    gsz = max(1, min(128, 512 // SN))
    ngrp = (128 + gsz - 1) // gsz
    for ct in range(2):
        with ExitStack() as st:
            def alloc(name, shape, dt):
                return st.enter_context(nc.sbuf_tensor(uname(name), list(shape), dt))
            cw = alloc("hy_cw", [128, 6, 3], F32)
            S.dma('sp', cw[:], c.hy_conv.ap()[l], w=['cw'])
            X = [alloc("hy_X%d" % i, [128, L + 2], F32) for i in range(2)]
            acc1 = alloc("hy_acc1", [128, L], F32)
            acc2 = alloc("hy_acc2", [128, L], F32)
            ub = [alloc("hy_ub%d" % i, [128, L], BF16) for i in range(2)]
            UR = alloc("hy_UR", [128, 128, NSEQ, NB], BF16)
            HX1 = alloc("hy_HX1", [128, 128, NSEQ, NB], BF16)
            HX2 = alloc("hy_HX2", [128, 128, NSEQ, NB], BF16)
            KS = [alloc("hy_KS%d" % i, [128, NLAG * 128], BF16) for i in range(2)]
            TM = [UR, HX1, HX2]
            for b in range(2):
                S.op('pool', lambda e, b=b: e.memset(X[b][:, 0:1], 0.0), w=[('X', b)])
                S.op('pool', lambda e, b=b: e.memset(X[b][:, L + 1:L + 2], 0.0), w=[('X', b)])
            it = 0
            tg = 0
            for s in range(NSEQ):
                for r_ in range(3):
                    b = it % 2
                    it += 1
                    tile = r_ * 2 + ct
                    rows = r_ * 256 + ct * 128
                    S.dma('sp', X[b][:, 1:L + 1], c.HYT.ap()[s, rows:rows + 128, :], r=[('HYT', s, i) for i in range(NT)], w=[('X', b)])
                    S.op('act', lambda e, b=b, tile=tile: e.activation(out=acc1[:], in_=X[b][:, 1:L + 1], func=AF.Copy,
                                                                     scale=cw[:, tile, 1:2]), r=[('X', b), 'cw'], w=['acc1'])
                    S.op('dve', lambda e, b=b, tile=tile: e.scalar_tensor_tensor(out=acc2[:], in0=X[b][:, 0:L], scalar=cw[:, tile, 0:1],
                                                                               in1=acc1[:], op0=ALU.mult, op1=ALU.add),
                         r=[('X', b), 'cw', 'acc1'], w=['acc2'])
                    if r_ == 0:
                        o_ap = sb_ap(ub[b], L - 1, [[-1, L]])
                    else:
                        o_ap = ub[b][:]
                    S.op('dve', lambda e, b=b, tile=tile, o_ap=o_ap: e.scalar_tensor_tensor(
                        out=o_ap, in0=X[b][:, 2:L + 2], scalar=cw[:, tile, 2:3], in1=acc2[:], op0=ALU.mult, op1=ALU.add),
                        r=[('X', b), 'cw', 'acc2'], w=[('ub', b)])
                    for g8 in range(NB // 8):
                        bank = 6 + (tg % 2)
                        tg += 1
                        psb = c.ps[bank][:].bitcast(BF16)
                        for q in range(8):
                            blk = g8 * 8 + q
                            S.op('pe', lambda e, psb=psb, q=q, blk=blk, b=b: e.transpose(psb[:, q * 128:(q + 1) * 128],
                                                                                      ub[b][:, blk * 128:(blk + 1) * 128], c.ident_b[:]),
                                 r=[('ub', b), 'ident_b'], w=[('ps', bank)])
                        if r_ == 0:
                            a_first = NB - 1 - g8 * 8
                            dst = sb_ap(TM[0], s * NB + a_first, [[-1, 8], [SN, 128]])
                        else:
                            dst = sb_ap(TM[r_], s * NB + g8 * 8, [[1, 8], [SN, 128]])
                        eng = 'act' if (tg % 2) else 'dve'
                        rkeys = [('ps', bank)]
                        wkeys = [('TM', r_, gi) for gi in range(ngrp)]
                        if eng == 'act':
                            S.op('act', lambda e, dst=dst, psb=psb: e.activation(out=dst, in_=psb.rearrange("p (q t) -> p q t", q=8), func=AF.Copy),
                                 r=rkeys, w=wkeys)
                        else:
                            S.op('dve', lambda e, dst=dst, psb=psb: e.tensor_copy(out=dst, in_=psb.rearrange("p (q t) -> p q t", q=8)),
                                 r=rkeys, w=wkeys)
            kc_i = 0
            for o in range(2):
                GB = HX1 if o == 0 else HX2
                gb_i = 1 if o == 0 else 2
                for gi in range(ngrp):
                    c0 = gi * gsz
                    n_c = min(gsz, 128 - c0)
                    bank = gi % 2
                    first = True
                    for ci in range(n_c):
                        ch = c0 + ci
                        kb = kc_i % 2
                        kc_i += 1
                        src = AP(c.G, ((l * 512 + o * 256 + ct * 128 + ch) * 2 * L), [[1, 128], [1, NLAG * 128]])
                        S.dma('sp', KS[kb][:], src, r=[('G', l, o, ct, 0), ('G', l, o, ct, 1)], w=[('KS', kb)])
                        for d in range(-(NB - 1), NB):
                            a0 = max(0, -d)
                            a1 = min(NB, NB - d)
                            n = a1 - a0
                            o_ap = sb_ap(c.ps[bank], ci * SN + a0 + d, [[NB, NSEQ], [1, n]])
                            r_ap = sb_ap(UR, ch * SN + a0, [[NB, NSEQ], [1, n]])
                            S.op('pe', lambda e, o_ap=o_ap, r_ap=r_ap, kb=kb, d=d, first=first: e.matmul(
                                o_ap, lhsT=KS[kb][:, (d + NB - 1) * 128:(d + NB) * 128], rhs=r_ap,
                                start=first, stop=False, skip_group_check=True),
                                r=[('KS', kb), ('TM', 0, gi)], w=[('ps', bank)])
                            first = False
                    ncol = n_c * SN
                    gflat = sb_ap(GB, c0 * SN, [[1, ncol]])
                    S.op('dve', lambda e, gflat=gflat, bank=bank, ncol=ncol: e.tensor_tensor(
                        out=gflat, in0=c.ps[bank][:, 0:ncol], in1=gflat, op=ALU.mult),
                        r=[('ps', bank), ('TM', gb_i, gi)], w=[('TM', gb_i, gi)])
                    if o == 0:
                        zb = 2 + (gi % 2)
                        S.op('pe', lambda e, zb=zb, gflat=gflat, ncol=ncol: e.matmul(c.ps[zb][:, 0:ncol], lhsT=c.J_b[:], rhs=gflat,
                                                                                    start=True, stop=True),
                             r=[('TM', 1, gi), 'J_b'], w=[('ps', zb)])
                        uflat = sb_ap(UR, c0 * SN, [[1, ncol]])
                        S.op('act', lambda e, zb=zb, uflat=uflat, ncol=ncol: e.activation(out=uflat, in_=c.ps[zb][:, 0:ncol], func=AF.Copy),
                             r=[('ps', zb)], w=[('TM', 0, gi)])
            ost = ub
            it = 0
            for s in range(NSEQ):
                b = it % 2
                it += 1
                for g8 in range(NB // 8):
                    bank = 6 + (tg % 2)
                    tg += 1
                    psb = c.ps[bank][:].bitcast(BF16)
                    for q in range(8):
                        a = g8 * 8 + q
                        i_ap = sb_ap(HX2, s * NB + a, [[SN, 128]])
                        S.op('pe', lambda e, psb=psb, q=q, i_ap=i_ap: e.transpose(psb[:, q * 128:(q + 1) * 128], i_ap, c.ident_b[:]),
                             r=[('TM', 2, gi) for gi in range(ngrp)] + ['ident_b'], w=[('ps', bank)])
                    S.op('act', lambda e, psb=psb, g8=g8, b=b: e.activation(out=ost[b][:, g8 * 1024:(g8 + 1) * 1024], in_=psb, func=AF.Copy),
                         r=[('ps', bank)], w=[('ub', b)])
                S.dma('pool', c.MIXT.ap()[s, 512 + ct * 128:512 + (ct + 1) * 128, :], ost[b][:], r=[('ub', b)], w=[('MIXT_h', s, ct)])
            S.barrier()


def pass_p3(c, l):
    nc, S, L, NSEQ = c.nc, c.S, c.L, c.NSEQ
    TW = 510
    tiles = [(a, min(a + TW, L)) for a in range(0, L, TW)]
    with ExitStack() as st:
        def alloc(name, shape, dt):
            return st.enter_context(nc.sbuf_tensor(uname(name), list(shape), dt))
        alloc_wstage(c, st)
        Wo = load_weight_bf16(c, st, "p3_Wo", c.w_out.ap()[l * D:(l + 1) * D, :], KC, D, 'Wo')
        gmf = alloc("p3_gm", [128, KC], F32)
        S.dma('sp', gmf[:], c.norm_ffn.ap()[l], w=['gmf'])
        Wu = load_weight_bf16(c, st, "p3_Wu", c.w_up.ap()[l * D:(l + 1) * D, :], KC, 2 * DFF, 'Wu', rowscale=gmf, rskey='gmf')
        fcw = alloc("p3_fcw", [128, NJ, 3], F32)
        S.dma('sp', fcw[:], c.ffn_conv.ap()[l], w=['fcw'])
        mx = alloc("p3_mx", [128, KC, 512], BF16)
        xt = alloc("p3_xt", [128, KC, 512], F32)
        xsq = alloc("p3_xsq", [128, KC, 512], BF16)
        rs = alloc("p3_rs", [128, 512], F32)
        h2 = alloc("p3_h2", [128, KC, 512], BF16)
        hid = alloc("p3_hid", [128, NJ, 512], BF16)
        acc = [alloc("p3_acc%d" % i, [128, 512], F32) for i in range(2)]
        gl = [alloc("p3_gl%d" % i, [128, 512], F32) for i in range(2)]
        rr = 0
        jj = 0
        for s in range(NSEQ):
            for (ta, tb_) in tiles:
                ntok = tb_ - ta
                ncol = ntok + 2
                lo = 1 if ta == 0 else 0
                hi = ncol - 1 if tb_ == L else ncol
                fm = lambda T: T.ap()[s].rearrange("(k p) t -> p k t", p=128)
                S.dma('sp', mx[:, :, lo:hi], fm(c.MIXT)[:, :, ta - 1 + lo:ta - 1 + hi], w=['mx'])
                S.dma('sp', xt[:, :, lo:hi], fm(c.XT)[:, :, ta - 1 + lo:ta - 1 + hi], w=['xt'])
                if lo == 1:
                    S.op('pool', lambda e: e.memset(mx[:, :, 0:1], 0.0), w=['mx'])
                    S.op('pool', lambda e: e.memset(xt[:, :, 0:1], 0.0), w=['xt'])
                if hi == ncol - 1:
                    S.op('pool', lambda e, ncol=ncol: e.memset(mx[:, :, ncol - 1:ncol], 0.0), w=['mx'])
                    S.op('pool', lambda e, ncol=ncol: e.memset(xt[:, :, ncol - 1:ncol], 0.0), w=['xt'])
                for m in range(KC):
                    bank = 1 + (rr % 4)
                    rr += 1
                    for k in range(KC):
                        S.op('pe', lambda e, k=k, m=m, bank=bank, ncol=ncol: e.matmul(
                            c.ps[bank][:, 0:ncol], lhsT=Wo[:, k, m * 128:(m + 1) * 128], rhs=mx[:, k, 0:ncol],
                            start=(k == 0), stop=(k == KC - 1)), r=['Wo', 'mx'], w=[('ps', bank)])
                    S.op('dve', lambda e, m=m, bank=bank, ncol=ncol: e.tensor_tensor(
                        out=xt[:, m, 0:ncol], in0=c.ps[bank][:, 0:ncol], in1=xt[:, m, 0:ncol], op=ALU.add),
                        r=[('ps', bank), 'xt'], w=['xt'])
                S.dma('pool', fm(c.X1T)[:, :, ta:tb_], xt[:, :, 1:1 + ntok], r=['xt'], w=[('X1T', s, ta)])
                S.op('act', lambda e, ncol=ncol: e.activation(out=xsq[:, :, 0:ncol], in_=xt[:, :, 0:ncol], func=AF.Square),
                     r=['xt'], w=['xsq'])
                for k in range(KC):
                    S.op('pe', lambda e, k=k, ncol=ncol: e.matmul(c.ps[0][:, 0:ncol], lhsT=c.ones_b[:], rhs=xsq[:, k, 0:ncol],
                                                                  start=(k == 0), stop=(k == KC - 1)),
                         r=['xsq', 'ones_b'], w=[('ps', 0)])
                rms_rstd(c, 0, rs, ncol, ('ps', 0), 'rs', D)
                for k in range(KC):
                    eng = 'dve' if k % 2 == 0 else 'pool'
                    S.op(eng, lambda e, k=k, ncol=ncol: e.tensor_tensor(
                        out=h2[:, k, 0:ncol], in0=xt[:, k, 0:ncol], in1=rs[:, 0:ncol], op=ALU.mult), r=['xt', 'rs'], w=['h2'])
                for j in range(NJ):
                    ab = jj % 2
                    jj += 1
                    bg = 1 + (rr % 4)
                    rr += 1
                    bu = 1 + (rr % 4)
                    rr += 1
                    for k in range(KC):
                        S.op('pe', lambda e, k=k, j=j, bg=bg, ncol=ncol: e.matmul(
                            c.ps[bg][:, 0:ncol], lhsT=Wu[:, k, j * 128:(j + 1) * 128], rhs=h2[:, k, 0:ncol],
                            start=(k == 0), stop=(k == KC - 1)), r=['Wu', 'h2'], w=[('ps', bg)])
                    for k in range(KC):
                        S.op('pe', lambda e, k=k, j=j, bu=bu, ncol=ncol: e.matmul(
                            c.ps[bu][:, 0:ncol], lhsT=Wu[:, k, DFF + j * 128:DFF + (j + 1) * 128], rhs=h2[:, k, 0:ncol],
                            start=(k == 0), stop=(k == KC - 1)), r=['Wu', 'h2'], w=[('ps', bu)])
                    S.op('act', lambda e, j=j, bg=bg, ab=ab, ntok=ntok: e.activation(
                        out=acc[ab][:, 0:ntok], in_=c.ps[bg][:, 1:1 + ntok], func=AF.Copy, scale=fcw[:, j, 1:2]),
                        r=[('ps', bg), 'fcw'], w=[('acc', ab)])
                    S.op('dve', lambda e, j=j, bg=bg, ab=ab, ntok=ntok: e.scalar_tensor_tensor(
                        out=acc[ab][:, 0:ntok], in0=c.ps[bg][:, 0:ntok], scalar=fcw[:, j, 0:1], in1=acc[ab][:, 0:ntok],
                        op0=ALU.mult, op1=ALU.add), r=[('ps', bg), 'fcw', ('acc', ab)], w=[('acc', ab)])
                    S.op('dve', lambda e, j=j, bg=bg, ab=ab, ntok=ntok: e.scalar_tensor_tensor(
                        out=acc[ab][:, 0:ntok], in0=c.ps[bg][:, 2:2 + ntok], scalar=fcw[:, j, 2:3], in1=acc[ab][:, 0:ntok],
                        op0=ALU.mult, op1=ALU.add), r=[('ps', bg), 'fcw', ('acc', ab)], w=[('acc', ab)])
                    S.op('act', lambda e, ab=ab, ntok=ntok: e.activation(out=gl[ab][:, 0:ntok], in_=acc[ab][:, 0:ntok],
                                                                        func=AF.Gelu_apprx_tanh), r=[('acc', ab)], w=[('gl', ab)])
                    S.op('dve', lambda e, j=j, bu=bu, ab=ab, ntok=ntok: e.tensor_tensor(
                        out=hid[:, j, 0:ntok], in0=c.ps[bu][:, 1:1 + ntok], in1=gl[ab][:, 0:ntok], op=ALU.mult),
                        r=[('ps', bu), ('gl', ab)], w=['hid'])
                S.dma('pool', c.HID.ap()[s].rearrange("(j p) t -> p j t", p=128)[:, :, ta:tb_], hid[:, :, 0:ntok],
                      r=['hid'], w=[('HID', s, ta)])
        S.barrier()


def pass_p4(c, l):
    nc, S, L, NSEQ, NT = c.nc, c.S, c.L, c.NSEQ, c.NT
    with ExitStack() as st:
        def alloc(name, shape, dt):
            return st.enter_context(nc.sbuf_tensor(uname(name), list(shape), dt))
        alloc_wstage(c, st)
        Wd = load_weight_bf16(c, st, "p4_Wd", c.w_down.ap()[l * DFF:(l + 1) * DFF, :], NJ, D, 'Wd')
        Wg = load_weight_bf16(c, st, "p4_Wg", c.ple_gate.ap()[l * D:(l + 1) * D, :], KC, D, 'Wg')
        Wp = load_weight_bf16(c, st, "p4_Wp", c.ple_w.ap()[l * 256:(l + 1) * 256, :], 2, D, 'Wp')
        pn = alloc("p4_pn", [128, KC], F32)
        S.dma('sp', pn[:], c.ple_norm.ap()[l], w=['pn'])
        hid_ = [alloc("p4_hid%d" % i_, [128, NJ, 512], BF16) for i_ in range(2)]
        x1_ = [alloc("p4_x1%d" % i_, [128, KC, 512], F32) for i_ in range(2)]
        pT_ = [alloc("p4_pT%d" % i_, [128, 2, 512], BF16) for i_ in range(2)]
        tcount = 0
        x2b = alloc("p4_x2b", [128, KC, 512], BF16)
        er = alloc("p4_er", [128, KC, 512], F32)
        esq = alloc("p4_esq", [128, KC, 512], BF16)
        sg = alloc("p4_sg", [128, KC, 512], BF16)
        rs = alloc("p4_rs", [128, 512], F32)
        tmp = [alloc("p4_tmp%d" % i, [128, 512], F32) for i in range(2)]
        rr = 0
        for s in range(NSEQ):
            for i in range(NT):
                t0 = i * 512
                fm = lambda T: T.ap()[s].rearrange("(k p) t -> p k t", p=128)[:, :, t0:t0 + 512]
                pb = tcount % 2
                tcount += 1
                hid, x1, pT = hid_[pb], x1_[pb], pT_[pb]
                khid, kx1, kpT = ('hid', pb), ('x1', pb), ('pT', pb)
                S.dma('sp', hid[:], c.HID.ap()[s].rearrange("(j p) t -> p j t", p=128)[:, :, t0:t0 + 512], w=[khid])
                S.dma('sp', x1[:], fm(c.X1T), w=[kx1])
                S.dma('sp', pT[:], c.PT.ap()[l, s].rearrange("(k p) t -> p k t", p=128)[:, :, t0:t0 + 512], w=[kpT])
                for m in range(KC):
                    bank = 1 + (rr % 4)
                    rr += 1
                    for j in range(NJ):
                        S.op('pe', lambda e, j=j, m=m, bank=bank, hid=hid: e.matmul(
                            c.ps[bank][:], lhsT=Wd[:, j, m * 128:(m + 1) * 128], rhs=hid[:, j, :],
                            start=(j == 0), stop=(j == NJ - 1)), r=['Wd', khid], w=[('ps', bank)])
                    S.op('dve', lambda e, m=m, bank=bank, x1=x1: e.tensor_tensor(out=x1[:, m, :], in0=c.ps[bank][:], in1=x1[:, m, :], op=ALU.add),
                         r=[('ps', bank), kx1], w=[kx1])
                    S.op('pool', lambda e, m=m, x1=x1: e.tensor_copy(out=x2b[:, m, :], in_=x1[:, m, :]), r=[kx1], w=['x2b'])
                for m in range(KC):
                    bank = 1 + (rr % 4)
                    rr += 1
                    for k in range(2):
                        S.op('pe', lambda e, k=k, m=m, bank=bank, pT=pT: e.matmul(
                            c.ps[bank][:], lhsT=Wp[:, k, m * 128:(m + 1) * 128], rhs=pT[:, k, :],
                            start=(k == 0), stop=(k == 1)), r=['Wp', kpT], w=[('ps', bank)])
                    S.op('act', lambda e, m=m, bank=bank: e.activation(out=er[:, m, :], in_=c.ps[bank][:], func=AF.Copy),
                         r=[('ps', bank)], w=['er'])
                    S.op('act', lambda e, m=m, bank=bank: e.activation(out=esq[:, m, :], in_=c.ps[bank][:], func=AF.Square),
                         r=[('ps', bank)], w=['esq'])
                for k in range(KC):
                    S.op('pe', lambda e, k=k: e.matmul(c.ps[0][:], lhsT=c.ones_b[:], rhs=esq[:, k, :],
                                                       start=(k == 0), stop=(k == KC - 1)), r=['esq', 'ones_b'], w=[('ps', 0)])
                rms_rstd(c, 0, rs, 512, ('ps', 0), 'rs', D)
                for m in range(KC):
                    bank = 1 + (rr % 4)
                    rr += 1
                    for k in range(KC):
                        S.op('pe', lambda e, k=k, m=m, bank=bank: e.matmul(
                            c.ps[bank][:], lhsT=Wg[:, k, m * 128:(m + 1) * 128], rhs=x2b[:, k, :],
                            start=(k == 0), stop=(k == KC - 1)), r=['Wg', 'x2b'], w=[('ps', bank)])
                    S.op('act', lambda e, m=m, bank=bank: e.activation(out=sg[:, m, :], in_=c.ps[bank][:], func=AF.Sigmoid),
                         r=[('ps', bank)], w=['sg'])
                    tb_ = m % 2
                    S.op('dve', lambda e, m=m, tb_=tb_: e.scalar_tensor_tensor(
                        out=tmp[tb_][:], in0=er[:, m, :], scalar=pn[:, m:m + 1], in1=rs[:], op0=ALU.mult, op1=ALU.mult),
                        r=['er', 'pn', 'rs'], w=[('tmp', tb_)])
                    S.op('pool', lambda e, m=m, tb_=tb_: e.tensor_tensor(out=tmp[tb_][:], in0=tmp[tb_][:], in1=sg[:, m, :], op=ALU.mult),
                         r=[('tmp', tb_), 'sg'], w=[('tmp', tb_)])
                    S.op('pool', lambda e, m=m, tb_=tb_, x1=x1: e.tensor_tensor(out=x1[:, m, :], in0=x1[:, m, :], in1=tmp[tb_][:], op=ALU.add),
                         r=[('tmp', tb_), kx1], w=[kx1])
                S.dma('pool', fm(c.XT), x1[:], r=[kx1], w=[('XT', s, i)])
        S.barrier()


def epilogue(c):
    nc, S, L, NSEQ, NT = c.nc, c.S, c.L, c.NSEQ, c.NT
    with ExitStack() as st:
        def alloc(name, shape, dt):
            return st.enter_context(nc.sbuf_tensor(uname(name), list(shape), dt))
        nf = alloc("ep_nf", [128, D], F32)
        S.dma('sp', nf[:], c.norm_final.ap().partition_broadcast(128), w=['nf'])
        xt = [alloc("ep_xt%d" % i, [128, KC, 512], F32) for i in range(2)]
        yt = [alloc("ep_yt%d" % i, [128, D], F32) for i in range(2)]
        sq = alloc("ep_sq", [128, D], F32)
        ss = alloc("ep_ss", [128, 4], F32)
        yo = [alloc("ep_yo%d" % i, [128, 4, D], F32) for i in range(2)]
        it = 0
        yb = 0
        for s in range(NSEQ):
            for i in range(NT):
                b = it % 2
                it += 1
                S.dma('sp', xt[b][:], c.XT.ap()[s].rearrange("(k p) t -> p k t", p=128)[:, :, i * 512:(i + 1) * 512],
                      r=[('XT', s, i)], w=[('xt', b)])
                for jb in range(4):
                    y = yb % 2
                    yb += 1
                    for half in range(2):
                        bank = 1 + ((yb * 2 + half) % 4)
                        for q in range(4):
                            k = half * 4 + q
                            S.op('pe', lambda e, bank=bank, q=q, k=k, jb=jb, b=b: e.transpose(
                                c.ps[bank][:, q * 128:(q + 1) * 128], xt[b][:, k, jb * 128:(jb + 1) * 128], c.ident_f[:]),
                                r=[('xt', b), 'ident_f'], w=[('ps', bank)])
                        if half == 0:
                            S.op('act', lambda e, bank=bank, y=y: e.activation(out=yt[y][:, 0:512], in_=c.ps[bank][:], func=AF.Copy),
                                 r=[('ps', bank)], w=[('yt', y)])
                        else:
                            S.op('dve', lambda e, bank=bank, y=y: e.tensor_copy(out=yt[y][:, 512:1024], in_=c.ps[bank][:]),
                                 r=[('ps', bank)], w=[('yt', y)])
                    S.op('pool', lambda e, y=y: e.tensor_tensor(out=sq[:], in0=yt[y][:], in1=yt[y][:], op=ALU.mult), r=[('yt', y)], w=['sq'])
                    S.op('dve', lambda e: e.tensor_reduce(out=ss[:, 0:1], in_=sq[:], axis=AX.X, op=ALU.add), r=['sq'], w=['ss'])
                    S.op('act', lambda e: e.activation(out=ss[:, 1:2], in_=ss[:, 0:1], func=AF.Sqrt, bias=EPS, scale=1.0 / D), r=['ss'], w=['ss'])
                    S.op('dve', lambda e: e.reciprocal(out=ss[:, 2:3], in_=ss[:, 1:2]), r=['ss'], w=['ss'])
                    S.op('dve', lambda e, y=y, jb=jb, b=b: e.scalar_tensor_tensor(out=yo[b][:, jb, :], in0=yt[y][:], scalar=ss[:, 2:3], in1=nf[:],
                                                                               op0=ALU.mult, op1=ALU.mult), r=[('yt', y), 'ss', 'nf'], w=[('yo', b)])
                S.dma('pool', c.y.ap()[s].rearrange("(j p) f -> p j f", p=128)[:, 4 * i:4 * i + 4, :], yo[b][:], r=[('yo', b)], w=[('y', s, i)])
        S.barrier()


def make_consts(L):
    f32 = np.float32
    k = {}
    k["k_ident"] = np.eye(128, dtype=f32)
    k["k_J"] = np.eye(128, dtype=f32)[::-1].copy()
    rot = np.zeros((128, 128), f32)
    for m in range(64):
        rot[m + 64, m] = -1.0
    for m in range(64, 128):
        rot[m - 64, m] = 1.0
    k["k_rot"] = rot
    half = 64
    inv = (np.float32(10000.0) ** (-np.arange(half, dtype=f32) / f32(half))).astype(f32)
    ang = (np.arange(L, dtype=f32)[None, :] * inv[:, None]).astype(f32)
    k["k_cos"] = np.concatenate([np.cos(ang), np.cos(ang)], 0).astype(f32)
    k["k_sin"] = np.concatenate([np.sin(ang), np.sin(ang)], 0).astype(f32)
    t = np.linspace(0.0, 1.0, L, dtype=f32)[:, None]
    bands = np.linspace(1e-4, 15, 16, dtype=f32)
    w = (f32(2.0 * math.pi / L) * np.arange(L, dtype=f32)[:, None] * bands[None, :]).astype(f32)
    feats = np.concatenate([t, np.cos(w), -np.sin(w)], -1).astype(f32).T
    k["k_feats"] = np.stack([feats, feats[:, ::-1]], 0).copy()
    deltas = np.abs(np.linspace(HY_MIN_DECAY, HY_MAX_DECAY, HYW, dtype=f32))
    k["k_negdelta"] = (-deltas).reshape(2, 128).T.copy().astype(f32)
    pos = np.arange(128, dtype=f32)
    k["k_df"] = np.maximum(pos[None, :] - pos[:, None], 0).astype(f32)
    k["k_db"] = np.maximum(pos[:, None] - pos[None, :], 0).astype(f32)
    k["k_kvec"] = np.stack([127.0 - pos, pos], 1).astype(f32)
    k["k_qrow"] = np.stack([pos + 1.0, 128.0 - pos], 0).astype(f32)
    ic = np.zeros((4, 16), f32)
    tt = np.arange(L)
    for g, win in enumerate(POOL_WINDOWS):
        lo = np.clip(tt - win // 2, 0, L - 1)
        hi = np.clip(tt + win // 2 - 1, 0, L - 1)
        cnt = (hi - lo + 1).astype(f32)
        ic[g, 0:8] = 1.0 / cnt[0:8]
        ic[g, 8:16] = 1.0 / cnt[L - 8:L]
    k["k_invcnt"] = ic.reshape(1, 64)
    return k


def layout_weights(W, DEPTH):
    f = lambda a: np.ascontiguousarray(np.asarray(a, dtype=np.float32))
    o = {}
    vec8 = lambda a: f(np.asarray(a).reshape(DEPTH, KC, 128).transpose(0, 2, 1))
    o["norm_mix"] = vec8(W["norm_mix"])
    o["norm_ffn"] = vec8(W["norm_ffn"])
    o["ple_norm"] = vec8(W["ple_norm"])
    o["w_in"] = f(W["w_in"])
    o["ret_decay_fwd"] = f(W["ret_decay_fwd"])
    o["ret_decay_bwd"] = f(W["ret_decay_bwd"])
    o["ret_gn"] = f(W["ret_gn"])
    o["hy_short_conv"] = f(np.asarray(W["hy_short_conv"]).reshape(DEPTH, 3, 6, 128).transpose(0, 3, 2, 1))
    o["hy_w1"] = f(W["hy_w1"])
    o["hy_b1"] = f(np.asarray(W["hy_b1"]).reshape(DEPTH, 64, 1))
    o["hy_freq"] = f(np.asarray(W["hy_freq"]).reshape(DEPTH, 64, 1))
    o["hy_w2"] = f(W["hy_w2"])
    o["hy_b2"] = f(np.asarray(W["hy_b2"]).reshape(DEPTH, 64, 1))
    o["hy_w3"] = f(W["hy_w3"])
    o["hy_bias"] = f(np.asarray(W["hy_bias"]).reshape(DEPTH, 2, 2, 128).transpose(0, 3, 1, 2).reshape(DEPTH, 128, 4))
    o["pool_w"] = f(W["pool_w"])
    o["pool_scale"] = f(np.asarray(W["pool_scale"]).reshape(DEPTH, 2, 128).transpose(0, 2, 1))
    o["w_out"] = f(W["w_out"])
    o["ffn_w_up"] = f(W["ffn_w_up"])
    o["ffn_conv"] = f(np.asarray(W["ffn_conv"]).reshape(DEPTH, 3, NJ, 128).transpose(0, 3, 2, 1))
    o["ffn_w_down"] = f(W["ffn_w_down"])
    o["ple_w"] = f(W["ple_w"])
    o["ple_gate_w"] = f(W["ple_gate_w"])
    o["norm_final"] = f(np.asarray(W["norm_final"]).reshape(1, D))
    return o


PADDED = ("w_in", "hy_w3", "w_out", "ffn_w_up", "ffn_w_down", "ple_w", "ple_gate_w", "k_cos", "k_sin", "k_feats")


def add_core_rows(m, cid):
    o = dict(m)
    for k in PADDED:
        a = np.asarray(o[k], dtype=np.float32)
        a2 = a.reshape(-1, a.shape[-1])
        o[k] = np.concatenate([a2, np.full((1, a2.shape[1]), float(cid), np.float32)], 0)
    return o


_CACHE = {}


def kernel(**inputs):
    L, DEPTH = L_FULL, DEPTH_FULL
    xp = np.asarray(inputs["x_prompt"], dtype=np.float32)
    xs = np.asarray(inputs["x_sample"], dtype=np.float32)
    pp = np.asarray(inputs["p_prompt"], dtype=np.float32)
    psm = np.asarray(inputs["p_sample"], dtype=np.float32)
    nP, nS = xp.shape[0], xs.shape[0]
    def seq_x(g):
        return xp[g] if g < nP else xs[g - nP]
    def seq_p(g):
        return pp[:, g] if g < nP else psm[:, g - nP]
    slots = []
    for cid in range(8):
        if cid < 4:
            slots.append([3 * cid, 3 * cid + 1, 3 * cid + 2])
        else:
            a = 12 + 2 * (cid - 4)
            slots.append([a, a + 1, a + 1])
    if "nc" not in _CACHE:
        _CACHE["nc"] = build(L, NSLOT, DEPTH)[0]
        _CACHE["consts"] = make_consts(L)
    nc = _CACHE["nc"]
    shared = dict(_CACHE["consts"])
    shared.update(layout_weights(inputs, DEPTH))
    in_maps = []
    for cid in range(8):
        m = add_core_rows(shared, cid)
        m["x"] = np.ascontiguousarray(np.stack([seq_x(g) for g in slots[cid]], 0))
        m["p"] = np.ascontiguousarray(np.stack([seq_p(g) for g in slots[cid]], 1))
        in_maps.append(m)
    res = run_bass_kernel_spmd(nc, in_maps, core_ids=list(range(8)))
    y_all = np.zeros((nP + nS, L, D), np.float32)
    for cid in range(8):
        y = np.asarray(res.results[cid]["y"])
        n_real = 3 if cid < 4 else 2
        for j in range(n_real):
            y_all[slots[cid][j]] = y[j]
    return (y_all[:nP].copy(), y_all[nP:].copy())
```

```python
import math
from contextlib import ExitStack
import numpy as np
import concourse.bass as bass
import concourse.mybir as mybir
from concourse.bass_utils import run_bass_kernel_spmd
from concourse.ap import AP

F32 = mybir.dt.float32
BF16 = mybir.dt.bfloat16
I32 = mybir.dt.int32
AF = mybir.ActivationFunctionType
ALU = mybir.AluOpType
AX = mybir.AxisListType

D = 1024
KC = 8
DEPTH_FULL = 4
L_FULL = 4096
NSLOT = 3
RW = 512
HYW = 256
INW = 3072
DFF = 2816
NJ = 22
EPS = 1e-6
PI = math.pi
HY_MIN_DECAY = math.log(1e-2) / 1.5
HY_MAX_DECAY = math.log(1e-2) / 0.3


class Sched:
    ENG = ('pe', 'act', 'dve', 'pool', 'sp')

    def __init__(s, nc, n_dma=40):
        s.nc = nc
        s.e = dict(pe=nc.tensor, act=nc.scalar, dve=nc.vector, pool=nc.gpsimd, sp=nc.sync)
        s.sem = {k: nc.alloc_semaphore("s_" + k) for k in ('pe', 'act', 'dve', 'pool')}
        s.cnt = {k: 0 for k in s.sem}
        s.dsem = [nc.alloc_semaphore("d%d" % i) for i in range(n_dma)]
        s.dcnt = [0] * n_dma
        s.drr = 0
        s.known = {k: {} for k in s.ENG}
        s.res = {}
        s.n_wait = 0
        s.n_ops = 0
        s.excl_ps = True

    def _semobj(s, name):
        return s.sem[name] if name in s.sem else s.dsem[name]

    def _wait(s, eng, name, val):
        if val <= 0 or s.known[eng].get(name, 0) >= val:
            return
        s.e[eng].wait_ge(s._semobj(name), val)
        s.known[eng][name] = val
        s.n_wait += 1

    def _deps(s, eng, reads, writes):
        best = {}
        for k in reads:
            r = s.res.get(k)
            if r and r[0]:
                n, v = r[0]
                if best.get(n, 0) < v:
                    best[n] = v
        for k in writes:
            r = s.res.get(k)
            if r:
                if r[0]:
                    n, v = r[0]
                    if best.get(n, 0) < v:
                        best[n] = v
                for n, v in r[1].items():
                    if best.get(n, 0) < v:
                        best[n] = v
        for n, v in best.items():
            if n == 'pe' and eng == 'pe':
                continue
            s._wait(eng, n, v)

    def _commit(s, ev, reads, writes):
        n, v = ev
        for k in reads:
            r = s.res.get(k)
            if r is None:
                r = s.res[k] = [None, {}]
            r[1][n] = v
        for k in writes:
            s.res[k] = [ev, {}]

    def op(s, eng, fn, r=(), w=()):
        if s.excl_ps:
            pr = [k for k in r if isinstance(k, tuple) and k[0] == 'ps']
            if pr:
                r = [k for k in r if k not in pr]
                w = list(w) + pr
        s._deps(eng, r, w)
        ins = fn(s.e[eng])
        s.cnt[eng] += 1
        ins.then_inc(s.sem[eng], 1)
        s._commit((eng, s.cnt[eng]), r, w)
        s.n_ops += 1

    def dma(s, q, out, in_, r=(), w=()):
        j = s.drr
        s.drr = (s.drr + 1) % len(s.dsem)
        s._wait(q, j, s.dcnt[j])
        s._deps(q, r, w)
        ins = s.e[q].dma_start(out=out, in_=in_)
        s.dcnt[j] += 16
        ins.then_inc(s.dsem[j], 16)
        s._commit((j, s.dcnt[j]), r, w)
        s.n_ops += 1

    def barrier(s, engs=None):
        for eng in (engs or s.ENG):
            for k in s.sem:
                s._wait(eng, k, s.cnt[k])
            for j in range(len(s.dsem)):
                s._wait(eng, j, s.dcnt[j])
        s.res = {}


def sb_ap(t, off, dims):
    pst = t[:].ap[0][0]
    return AP(t, off, [[pst, 128]] + [list(d) for d in dims])


class Ctx:
    pass


_UID = [0]


def uname(n):
    _UID[0] += 1
    return "%s_u%d" % (n, _UID[0])


ALL_STAGES = ('filt', 'tr', 'p1', 'ret', 'pool', 'hy', 'p3', 'p4', 'epi')


def build(L=L_FULL, NSEQ=NSLOT, DEPTH=DEPTH_FULL, taps=(), stages=ALL_STAGES):
    NT = L // 512
    NB = L // 128
    NLAG = 2 * NB - 1
    nc = bass.Bass("TRN2", target_bir_lowering=False)
    S = Sched(nc)
    c = Ctx()
    c.nc, c.S, c.L, c.NSEQ, c.DEPTH, c.NT, c.NB, c.NLAG = nc, S, L, NSEQ, DEPTH, NT, NB, NLAG
    c.taps = taps
    c.stages = stages

    def din(name, shape, dt=F32):
        return nc.dram_tensor(name, list(shape), dt, kind="ExternalInput")

    c.x = din("x", [NSEQ, L, D])
    c.p = din("p", [DEPTH, NSEQ, L, 256])
    c.norm_mix = din("norm_mix", [DEPTH, 128, KC])
    c.w_in = din("w_in", [DEPTH * D + 1, INW])
    c.dec_f = din("ret_decay_fwd", [DEPTH, 4])
    c.dec_b = din("ret_decay_bwd", [DEPTH, 4])
    c.ret_gn = din("ret_gn", [DEPTH, RW])
    c.hy_conv = din("hy_short_conv", [DEPTH, 128, 6, 3])
    c.hy_w1 = din("hy_w1", [DEPTH, 33, 64])
    c.hy_b1 = din("hy_b1", [DEPTH, 64, 1])
    c.hy_freq = din("hy_freq", [DEPTH, 64, 1])
    c.hy_w2 = din("hy_w2", [DEPTH, 64, 64])
    c.hy_b2 = din("hy_b2", [DEPTH, 64, 1])
    c.hy_w3 = din("hy_w3", [DEPTH * 64 + 1, 1024])
    c.hy_bias = din("hy_bias", [DEPTH, 128, 4])
    c.pool_w = din("pool_w", [DEPTH, 4, 64, 64])
    c.pool_scale = din("pool_scale", [DEPTH, 128, 2])
    c.w_out = din("w_out", [DEPTH * D + 1, D])
    c.norm_ffn = din("norm_ffn", [DEPTH, 128, KC])
    c.w_up = din("ffn_w_up", [DEPTH * D + 1, 2 * DFF])
    c.ffn_conv = din("ffn_conv", [DEPTH, 128, NJ, 3])
    c.w_down = din("ffn_w_down", [DEPTH * DFF + 1, D])
    c.ple_w = din("ple_w", [DEPTH * 256 + 1, D])
    c.ple_gate = din("ple_gate_w", [DEPTH * D + 1, D])
    c.ple_norm = din("ple_norm", [DEPTH, 128, KC])
    c.norm_final = din("norm_final", [1, D])
    c.k_ident = din("k_ident", [128, 128])
    c.k_J = din("k_J", [128, 128])
    c.k_rot = din("k_rot", [128, 128])
    c.k_cos = din("k_cos", [129, L])
    c.k_sin = din("k_sin", [129, L])
    c.k_feats = din("k_feats", [67, L])
    c.k_negdelta = din("k_negdelta", [128, 2])
    c.k_df = din("k_df", [128, 128])
    c.k_db = din("k_db", [128, 128])
    c.k_kvec = din("k_kvec", [128, 2])
    c.k_qrow = din("k_qrow", [2, 128])
    c.k_invcnt = din("k_invcnt", [1, 64])
    c.y = nc.dram_tensor("y", [NSEQ, L, D], F32, kind="ExternalOutput")

    def dscr(name, shape, dt):
        return nc.dram_tensor(name, list(shape), dt)

    c.XT = dscr("XT", [NSEQ, D, L], F32)
    c.PT = dscr("PT", [DEPTH, NSEQ, 256, L], BF16)
    c.QT = dscr("QT", [NSEQ, RW, L], BF16)
    c.KT = dscr("KT", [NSEQ, RW, L], BF16)
    c.V = dscr("V", [NSEQ, L, RW], BF16)
    c.GS = dscr("GS", [NSEQ, L, RW], BF16)
    c.HYT = dscr("HYT", [NSEQ, 768, L], F32)
    c.PLT = dscr("PLT", [NSEQ, 256, L], F32)
    c.MIXT = dscr("MIXT", [NSEQ, D, L], BF16)
    c.HID = dscr("HID", [NSEQ, DFF, L], BF16)
    c.X1T = dscr("X1T", [NSEQ, D, L], F32)
    c.G = dscr("G", [DEPTH, 512, 2 * L], BF16)
    c.dbg = {}
    for nm in taps:
        t_ = getattr(c, nm)
        c.dbg[nm] = nc.dram_tensor("dbg_" + nm, list(t_.shape), t_.dtype, kind="ExternalOutput")

    c.ps = [nc.alloc_psum_tensor("ps%d" % i, [128, 512], F32) for i in range(8)]

    with ExitStack() as gst:
        def galloc(name, shape, dt):
            return gst.enter_context(nc.sbuf_tensor(uname(name), list(shape), dt))
        c.ident_f = galloc("ident_f", [128, 128], F32)
        c.ident_b = galloc("ident_b", [128, 128], BF16)
        c.J_b = galloc("J_b", [128, 128], BF16)
        c.rot_b = galloc("rot_b", [128, 128], BF16)
        c.ones_b = galloc("ones_b", [128, 128], BF16)
        stg = galloc("cstage", [128, 128], F32)
        S.dma('sp', c.ident_f[:], c.k_ident.ap(), w=['ident_f'])
        S.op('dve', lambda e: e.tensor_copy(out=c.ident_b[:], in_=c.ident_f[:]), r=['ident_f'], w=['ident_b'])
        S.dma('sp', stg[:], c.k_J.ap(), w=['cstage'])
        S.op('dve', lambda e: e.tensor_copy(out=c.J_b[:], in_=stg[:]), r=['cstage'], w=['J_b'])
        S.dma('sp', stg[:], c.k_rot.ap(), w=['cstage'])
        S.op('dve', lambda e: e.tensor_copy(out=c.rot_b[:], in_=stg[:]), r=['cstage'], w=['rot_b'])
        S.op('dve', lambda e: e.memset(c.ones_b[:], 1.0), w=['ones_b'])
        S.barrier()

        st_ = c.stages
        if 'filt' in st_:
            prologue_filters(c)
        if 'tr' in st_:
            prologue_transposes(c)
        for l in range(DEPTH):
            if 'p1' in st_:
                pass_p1(c, l)
            if 'ret' in st_:
                pass_ret(c, l)
            if 'pool' in st_:
                pass_pool(c, l)
            if 'hy' in st_:
                pass_hyena(c, l)
            if 'p3' in st_:
                pass_p3(c, l)
            if 'p4' in st_:
                pass_p4(c, l)
        if 'epi' in st_:
            epilogue(c)
        S.barrier()
        for nm in taps:
            S.dma('sp', c.dbg[nm].ap(), getattr(c, nm).ap())
        S.barrier(['sp', 'pool'])
    return nc, c


def prologue_transposes(c):
    nc, S, L, NSEQ, DEPTH, NT = c.nc, c.S, c.L, c.NSEQ, c.DEPTH, c.NT
    with ExitStack() as st:
        def alloc(name, shape, dt):
            return st.enter_context(nc.sbuf_tensor(uname(name), list(shape), dt))
        xin = [alloc("pt_xin%d" % i, [128, 4, D], F32) for i in range(2)]
        xst = [alloc("pt_xst%d" % i, [128, KC, 512], F32) for i in range(2)]
        pin = [alloc("pt_pin%d" % i, [128, 4, 256], F32) for i in range(2)]
        pbf = [alloc("pt_pbf%d" % i, [128, 4, 256], BF16) for i in range(2)]
        pst = [alloc("pt_pst%d" % i, [128, 2, 512], BF16) for i in range(2)]
        it = 0
        for s in range(NSEQ):
            for i in range(NT):
                b = it % 2
                it += 1
                src = c.x.ap()[s].rearrange("(j p) f -> p j f", p=128)[:, 4 * i:4 * i + 4, :]
                S.dma('sp', xin[b][:], src, w=[('xin', b)])
                g = 0
                for jb in range(4):
                    for half in range(2):
                        bank = 1 + (g % 4)
                        g += 1
                        for q in range(4):
                            kc = half * 4 + q
                            S.op('pe', lambda e, bank=bank, q=q, kc=kc, jb=jb, b=b: e.transpose(
                                c.ps[bank][:, q * 128:(q + 1) * 128], xin[b][:, jb, kc * 128:(kc + 1) * 128], c.ident_f[:]),
                                r=[('xin', b)], w=[('ps', bank)])
                        eng = 'dve' if (g % 2) else 'act'
                        if eng == 'dve':
                            S.op('dve', lambda e, bank=bank, half=half, jb=jb, b=b: e.tensor_copy(
                                out=xst[b][:, half * 4:half * 4 + 4, jb * 128:(jb + 1) * 128],
                                in_=c.ps[bank][:].rearrange("p (q t) -> p q t", q=4)),
                                r=[('ps', bank)], w=[('xst', b)])
                        else:
                            S.op('act', lambda e, bank=bank, half=half, jb=jb, b=b: e.activation(
                                out=xst[b][:, half * 4:half * 4 + 4, jb * 128:(jb + 1) * 128],
                                in_=c.ps[bank][:].rearrange("p (q t) -> p q t", q=4), func=AF.Copy),
                                r=[('ps', bank)], w=[('xst', b)])
                dst = c.XT.ap()[s].rearrange("(k p) t -> p k t", p=128)[:, :, i * 512:(i + 1) * 512]
                S.dma('pool', dst, xst[b][:], r=[('xst', b)], w=[('XT', s, i)])
        it = 0
        for l in range(DEPTH):
            for s in range(NSEQ):
                for i in range(NT):
                    b = it % 2
                    it += 1
                    src = c.p.ap()[l, s].rearrange("(j p) f -> p j f", p=128)[:, 4 * i:4 * i + 4, :]
                    S.dma('sp', pin[b][:], src, w=[('pin', b)])
                    S.op('dve', lambda e, b=b: e.tensor_copy(out=pbf[b][:], in_=pin[b][:]), r=[('pin', b)], w=[('pbf', b)])
                    for kc in range(2):
                        bank = 5 + kc
                        psb = c.ps[bank][:].bitcast(BF16)
                        for jb in range(4):
                            S.op('pe', lambda e, psb=psb, jb=jb, kc=kc, b=b: e.transpose(
                                psb[:, jb * 128:(jb + 1) * 128], pbf[b][:, jb, kc * 128:(kc + 1) * 128], c.ident_b[:]),
                                r=[('pbf', b)], w=[('ps', bank)])
                        S.op('act', lambda e, psb=psb, kc=kc, b=b: e.activation(
                            out=pst[b][:, kc, :], in_=psb[:, 0:512], func=AF.Copy),
                            r=[('ps', bank)], w=[('pst', b)])
                    dst = c.PT.ap()[l, s].rearrange("(k p) t -> p k t", p=128)[:, :, i * 512:(i + 1) * 512]
                    S.dma('pool', dst, pst[b][:], r=[('pst', b)], w=[('PT', l, s, i)])
        S.barrier()


def load_weight_bf16(c, st, name, src_rows_ap, nk, ncols, key, stage_cols=1024, rowscale=None, rskey=None):
    nc, S = c.nc, c.S
    wb = st.enter_context(nc.sbuf_tensor(uname(name), [128, nk, ncols], BF16))
    if not hasattr(c, 'wstage'):
        raise RuntimeError("wstage missing")
    src = src_rows_ap.rearrange("(k p) n -> p k n", p=128)
    idx = 0
    for k in range(nk):
        for c0 in range(0, ncols, stage_cols):
            cw = min(stage_cols, ncols - c0)
            b = c.wstage_i % 2
            c.wstage_i += 1
            S.dma('sp', c.wstage[b][:, 0:cw], src[:, k, c0:c0 + cw], w=[('wstage', b)])
            eng = ('dve', 'pool', 'act')[idx % 3]
            idx += 1
            rk = [('wstage', b)] + ([rskey] if rskey else [])
            if rowscale is None:
                if eng == 'act':
                    S.op('act', lambda e, b=b, k=k, c0=c0, cw=cw: e.activation(
                        out=wb[:, k, c0:c0 + cw], in_=c.wstage[b][:, 0:cw], func=AF.Copy), r=rk, w=[key])
                else:
                    S.op(eng, lambda e, b=b, k=k, c0=c0, cw=cw: e.tensor_copy(
                        out=wb[:, k, c0:c0 + cw], in_=c.wstage[b][:, 0:cw]), r=rk, w=[key])
            else:
                if eng == 'act':
                    S.op('act', lambda e, b=b, k=k, c0=c0, cw=cw: e.activation(
                        out=wb[:, k, c0:c0 + cw], in_=c.wstage[b][:, 0:cw], func=AF.Copy, scale=rowscale[:, k:k + 1]), r=rk, w=[key])
                else:
                    S.op(eng, lambda e, b=b, k=k, c0=c0, cw=cw: e.tensor_scalar(
                        out=wb[:, k, c0:c0 + cw], in0=c.wstage[b][:, 0:cw], scalar1=rowscale[:, k:k + 1], scalar2=None, op0=ALU.mult),
                        r=rk, w=[key])
    return wb


def alloc_wstage(c, st, cols=1024):
    c.wstage = [st.enter_context(c.nc.sbuf_tensor(uname("wstage%d" % i), [128, cols], F32)) for i in range(2)]
    c.wstage_i = 0


def rms_rstd(c, ps_bank, rs, n, key_ps, key_rs, dim):
    S = c.S
    S.op('act', lambda e: e.activation(out=rs[:, 0:n], in_=c.ps[ps_bank][:, 0:n], func=AF.Sqrt,
                                       bias=EPS, scale=1.0 / dim), r=[key_ps], w=[key_rs])
    S.op('dve', lambda e: e.reciprocal(out=rs[:, 0:n], in_=rs[:, 0:n]), r=[key_rs], w=[key_rs])


def pass_p1(c, l):
    nc, S, L, NSEQ, NT = c.nc, c.S, c.L, c.NSEQ, c.NT
    with ExitStack() as st:
        def alloc(name, shape, dt):
            return st.enter_context(nc.sbuf_tensor(uname(name), list(shape), dt))
        alloc_wstage(c, st)
        gm = alloc("p1_gm", [128, KC], F32)
        S.dma('sp', gm[:], c.norm_mix.ap()[l], w=['gm'])
        Wb = load_weight_bf16(c, st, "p1_W", c.w_in.ap()[l * D:(l + 1) * D, :], KC, INW, 'Wb', rowscale=gm, rskey='gm')
        cos = alloc("p1_cos", [128, L], F32)
        sin = alloc("p1_sin", [128, L], F32)
        gn = alloc("p1_gn", [128, RW], F32)
        S.dma('sp', cos[:], c.k_cos.ap()[0:128, :], w=['cos'])
        S.dma('sp', sin[:], c.k_sin.ap()[0:128, :], w=['sin'])
        S.dma('sp', gn[:], c.ret_gn.ap()[l:l + 1, :].partition_broadcast(128), w=['gn'])
        xt_ = [alloc("p1_xt%d" % i_, [128, KC, 512], F32) for i_ in range(2)]
        xsq = alloc("p1_xsq", [128, KC, 512], BF16)
        rs = alloc("p1_rs", [128, 512], F32)
        h = [alloc("p1_h%d" % i, [128, KC, 512], BF16) for i in range(2)]
        qb = [alloc("p1_qb%d" % i, [128, 512], BF16) for i in range(2)]
        t1 = [alloc("p1_t1%d" % i, [128, 512], F32) for i in range(2)]
        t2 = [alloc("p1_t2%d" % i, [128, 512], F32) for i in range(2)]
        sl = [alloc("p1_sl%d" % i, [128, 512], F32) for i in range(2)]
        qk_out = alloc("p1_qko", [128, 8, 512], BF16)
        hy_out = alloc("p1_hyo", [128, 6, 512], F32)
        pl_out = alloc("p1_plo", [128, 2, 512], F32)
        v_out = alloc("p1_vo", [128, 4, 512], BF16)
        gs_out = alloc("p1_gso", [128, 4, 512], BF16)
        scl_q = 128.0 ** -0.5
        tiles = [(s_, i_) for s_ in range(NSEQ) for i_ in range(NT)]

        def prep(ti):
            s_, i_ = tiles[ti]
            hb = ti % 2
            xt = xt_[hb]
            t0 = i_ * 512
            S.dma('sp', xt[:], c.XT.ap()[s_].rearrange("(k p) t -> p k t", p=128)[:, :, t0:t0 + 512],
                  r=[('XT', s_, i_)], w=[('xt', hb)])
            S.op('act', lambda e: e.activation(out=xsq[:], in_=xt[:], func=AF.Square), r=[('xt', hb)], w=['xsq'])
            for k in range(KC):
                S.op('pe', lambda e, k=k: e.matmul(c.ps[0][:], lhsT=c.ones_b[:], rhs=xsq[:, k, :],
                                                   start=(k == 0), stop=(k == KC - 1)),
                     r=['xsq', 'ones_b'], w=[('ps', 0)])
            rms_rstd(c, 0, rs, 512, ('ps', 0), 'rs', D)
            for k in range(KC):
                eng = 'dve' if k % 2 == 0 else 'pool'
                S.op(eng, lambda e, k=k: e.tensor_tensor(
                    out=h[hb][:, k, :], in0=xt[:, k, :], in1=rs[:], op=ALU.mult), r=[('xt', hb), 'rs'], w=[('h', hb)])

        rr = 0
        prep(0)
        for ti, (s, i) in enumerate(tiles):
            t0 = i * 512
            hb = ti % 2
            mt = [('q', j, j * 128) for j in range(4)] + [('k', j, 512 + j * 128) for j in range(4)] + \
                 [('hy', j, 2048 + j * 128) for j in range(6)] + [('pl', j, 2816 + j * 128) for j in range(2)]
            pending = []
            for kind, j, col in mt:
                bank = 1 + (rr % 4)
                rr += 1
                for k in range(KC):
                    S.op('pe', lambda e, k=k, bank=bank, col=col: e.matmul(
                        c.ps[bank][:], lhsT=Wb[:, k, col:col + 128], rhs=h[hb][:, k, :],
                        start=(k == 0), stop=(k == KC - 1)), r=['Wb', ('h', hb)], w=[('ps', bank)])
                for fn_ in pending:
                    fn_()
                pending = []
                if kind in ('q', 'k'):
                    b2 = rr % 2
                    rb = 5 + b2
                    oi = j if kind == 'q' else 4 + j
                    sc = scl_q if kind == 'q' else 1.0
                    S.op('act', lambda e, bank=bank, b2=b2: e.activation(out=qb[b2][:], in_=c.ps[bank][:], func=AF.Copy),
                         r=[('ps', bank)], w=[('qb', b2)])

                    def rope_rest(bank=bank, b2=b2, rb=rb, oi=oi, sc=sc):
                        S.op('pe', lambda e: e.matmul(c.ps[rb][:], lhsT=c.rot_b[:], rhs=qb[b2][:], start=True, stop=True),
                             r=[('qb', b2), 'rot_b'], w=[('ps', rb)])
                        S.op('dve', lambda e: e.scalar_tensor_tensor(
                            out=t1[b2][:], in0=c.ps[bank][:], scalar=sc, in1=cos[:, t0:t0 + 512],
                            op0=ALU.mult, op1=ALU.mult), r=[('ps', bank), 'cos'], w=[('t1', b2)])
                        S.op('dve', lambda e: e.scalar_tensor_tensor(
                            out=t2[b2][:], in0=c.ps[rb][:], scalar=sc, in1=sin[:, t0:t0 + 512],
                            op0=ALU.mult, op1=ALU.mult), r=[('ps', rb), 'sin'], w=[('t2', b2)])
                        S.op('pool', lambda e: e.tensor_tensor(
                            out=qk_out[:, oi, :], in0=t1[b2][:], in1=t2[b2][:], op=ALU.add),
                            r=[('t1', b2), ('t2', b2)], w=['qk_out'])
                    pending.append(rope_rest)
                elif kind == 'hy':
                    S.op('act', lambda e, bank=bank, j=j: e.activation(out=hy_out[:, j, :], in_=c.ps[bank][:], func=AF.Copy),
                         r=[('ps', bank)], w=['hy_out'])
                else:
                    S.op('dve', lambda e, bank=bank, j=j: e.tensor_copy(out=pl_out[:, j, :], in_=c.ps[bank][:]),
                         r=[('ps', bank)], w=['pl_out'])
            for fn_ in pending:
                fn_()
            if ti + 1 < len(tiles):
                prep(ti + 1)
            for j in range(4):
                bank = 1 + (rr % 4)
                rr += 1
                for k in range(KC):
                    S.op('pe', lambda e, k=k, bank=bank, j=j: e.matmul(
                        c.ps[bank][:], lhsT=h[hb][:, k, j * 128:(j + 1) * 128], rhs=Wb[:, k, 1024:1536],
                        start=(k == 0), stop=(k == KC - 1)), r=['Wb', ('h', hb)], w=[('ps', bank)])
                S.op('dve', lambda e, bank=bank, j=j: e.tensor_copy(out=v_out[:, j, :], in_=c.ps[bank][:]),
                     r=[('ps', bank)], w=['v_out'])
                bank = 1 + (rr % 4)
                rr += 1
                b2 = rr % 2
                for k in range(KC):
                    S.op('pe', lambda e, k=k, bank=bank, j=j: e.matmul(
                        c.ps[bank][:], lhsT=h[hb][:, k, j * 128:(j + 1) * 128], rhs=Wb[:, k, 1536:2048],
                        start=(k == 0), stop=(k == KC - 1)), r=['Wb', ('h', hb)], w=[('ps', bank)])
                S.op('act', lambda e, bank=bank, b2=b2: e.activation(out=sl[b2][:], in_=c.ps[bank][:], func=AF.Silu),
                     r=[('ps', bank)], w=[('sl', b2)])
                S.op('pool', lambda e, b2=b2, j=j: e.tensor_tensor(out=gs_out[:, j, :], in0=sl[b2][:], in1=gn[:], op=ALU.mult),
                     r=[('sl', b2), 'gn'], w=['gs_out'])
            fm = lambda T, n: T.ap()[s].rearrange("(k p) t -> p k t", p=128)[:, 0:n, t0:t0 + 512]
            tm = lambda T: T.ap()[s].rearrange("(j p) f -> p j f", p=128)[:, 4 * i:4 * i + 4, :]
            S.dma('pool', fm(c.QT, 4), qk_out[:, 0:4, :], r=['qk_out'], w=[('QT', s, i)])
            S.dma('pool', fm(c.KT, 4), qk_out[:, 4:8, :], r=['qk_out'], w=[('KT', s, i)])
            S.dma('pool', fm(c.HYT, 6), hy_out[:], r=['hy_out'], w=[('HYT', s, i)])
            S.dma('pool', fm(c.PLT, 2), pl_out[:], r=['pl_out'], w=[('PLT', s, i)])
            S.dma('pool', tm(c.V), v_out[:], r=['v_out'], w=[('V', s, i)])
            S.dma('pool', tm(c.GS), gs_out[:], r=['gs_out'], w=[('GS', s, i)])
        S.barrier()


def pass_ret(c, l):
    nc, S, L, NSEQ, NT, NB = c.nc, c.S, c.L, c.NSEQ, c.NT, c.NB
    with ExitStack() as st:
        def alloc(name, shape, dt):
            return st.enter_context(nc.sbuf_tensor(uname(name), list(shape), dt))
        lgr = alloc("rt_lgr", [128, 8], F32)
        lg = alloc("rt_lg", [128, 8], F32)
        df = alloc("rt_df", [128, 128], F32)
        db = alloc("rt_db", [128, 128], F32)
        kvec = alloc("rt_kvec", [128, 2], F32)
        qrow = alloc("rt_qrow", [128, 2, 128], F32)
        mask = alloc("rt_mask", [128, 4, 128], F32)
        mtmp = alloc("rt_mtmp", [128, 128], F32)
        wk = alloc("rt_wk", [128, 2, 4], F32)
        wq = alloc("rt_wq", [128, 2, 4, 128], F32)
        decv = alloc("rt_decv", [128, 8], F32)
        S.dma('sp', lgr[:, 0:4], c.dec_f.ap()[l:l + 1, :].partition_broadcast(128), w=['lgr'])
        S.dma('sp', lgr[:, 4:8], c.dec_b.ap()[l:l + 1, :].partition_broadcast(128), w=['lgr'])
        S.dma('sp', df[:], c.k_df.ap(), w=['df'])
        S.dma('sp', db[:], c.k_db.ap(), w=['db'])
        S.dma('sp', kvec[:], c.k_kvec.ap(), w=['kvec'])
        S.dma('sp', qrow[:, 0, :], c.k_qrow.ap()[0:1, :].partition_broadcast(128), w=['qrow'])
        S.dma('sp', qrow[:, 1, :], c.k_qrow.ap()[1:2, :].partition_broadcast(128), w=['qrow'])
        S.op('act', lambda e: e.activation(out=lg[:], in_=lgr[:], func=AF.Exp, scale=-1.0), r=['lgr'], w=['lg'])
        S.op('act', lambda e: e.activation(out=lg[:], in_=lg[:], func=AF.Ln, bias=1.0, scale=1.0), r=['lg'], w=['lg'])
        S.op('dve', lambda e: e.tensor_scalar(out=lg[:], in0=lg[:], scalar1=-1.0, scalar2=None, op0=ALU.mult), r=['lg'], w=['lg'])
        for hh in range(4):
            S.op('dve', lambda e, hh=hh: e.tensor_scalar(out=mtmp[:], in0=df[:], scalar1=lg[:, hh:hh + 1], scalar2=None,
                                                         op0=ALU.mult), r=['df', 'lg'], w=['mtmp'])
            S.op('dve', lambda e, hh=hh: e.scalar_tensor_tensor(out=mtmp[:], in0=db[:], scalar=lg[:, 4 + hh:5 + hh], in1=mtmp[:],
                                                                op0=ALU.mult, op1=ALU.add), r=['db', 'lg', 'mtmp'], w=['mtmp'])
            S.op('act', lambda e, hh=hh: e.activation(out=mask[:, hh, :], in_=mtmp[:], func=AF.Exp), r=['mtmp'], w=['mask'])
            for d in range(2):
                S.op('act', lambda e, hh=hh, d=d: e.activation(out=wq[:, d, hh, :], in_=qrow[:, d, :], func=AF.Exp,
                                                               scale=lg[:, 4 * d + hh:4 * d + hh + 1]),
                     r=['qrow', 'lg'], w=['wq'])
        for d in range(2):
            S.op('act', lambda e, d=d: e.activation(out=wk[:, d, :], in_=lg[:, 4 * d:4 * d + 4], func=AF.Exp,
                                                    scale=kvec[:, d:d + 1]), r=['kvec', 'lg'], w=['wk'])
        S.op('act', lambda e: e.activation(out=decv[:], in_=lg[:], func=AF.Exp, scale=128.0), r=['lg'], w=['decv'])

        QTs = alloc("rt_QT", [128, 4, L], BF16)
        KTs = alloc("rt_KT", [128, 4, L], BF16)
        Vs = alloc("rt_V", [128, NB, RW], BF16)
        SB = alloc("rt_SB", [128, NB, 4, 128], BF16)
        stt = alloc("rt_st", [128, 4, 128], F32)
        stb = alloc("rt_stb", [128, 4, 128], BF16)
        ktw = [alloc("rt_ktw%d" % i, [128, 4, 128], BF16) for i in range(2)]
        PT_ = [alloc("rt_PT%d" % i, [128, 4, 128], BF16) for i in range(2)]
        qf = [alloc("rt_qf%d" % i, [128, 4, 512], BF16) for i in range(2)]
        qbk = [alloc("rt_qbk%d" % i, [128, 4, 512], BF16) for i in range(2)]
        GSt = [alloc("rt_GS%d" % i, [128, 4, RW], BF16) for i in range(2)]
        osq = alloc("rt_osq", [128, 4, 128], F32)
        sm = alloc("rt_sm", [128, 8], F32)
        sm2 = alloc("rt_sm2", [128, 20], F32)
        tmpo = [alloc("rt_tmpo%d" % i, [128, 4, 128], F32) for i in range(2)]
        rtok = [alloc("rt_rtok%d" % i, [128, 4, 128], BF16) for i in range(2)]
        mixst = [alloc("rt_mix%d" % i, [128, 4, 512], BF16) for i in range(2)]
        psT = c.ps[0][:].bitcast(BF16)
        psR = c.ps[5][:].bitcast(BF16)

        def bc4(t, off):
            return sb_ap(t, off, [[1, 4], [0, 128]])

        def k_transposes(n, w_dir, kb):
            for hh in range(4):
                S.op('pe', lambda e, hh=hh: e.transpose(psT[:, hh * 128:(hh + 1) * 128], KTs[:, hh, n * 128:(n + 1) * 128],
                                                        c.ident_b[:]), r=['KTs', 'ident_b'], w=[('ps', 0)])
            S.op('dve', lambda e: e.tensor_tensor(out=ktw[kb][:], in0=psT[:, 0:512].rearrange("p (h t) -> p h t", h=4),
                                                  in1=bc4(wk, w_dir * 4), op=ALU.mult),
                 r=[('ps', 0), 'wk'], w=[('ktw', kb)])

        def kv_update(n, d, kb):
            for hh in range(4):
                S.op('pe', lambda e, hh=hh: e.matmul(c.ps[4][:, hh * 128:(hh + 1) * 128], lhsT=ktw[kb][:, hh, :],
                                                     rhs=Vs[:, n, hh * 128:(hh + 1) * 128], start=True, stop=True),
                     r=[('ktw', kb), 'Vs'], w=[('ps', 4)])
            S.op('pool', lambda e: e.tensor_tensor(out=stt[:], in0=stt[:], in1=bc4(decv, d * 4), op=ALU.mult),
                 r=['stt', 'decv'], w=['stt'])
            S.op('dve', lambda e: e.tensor_tensor(out=stt[:], in0=stt[:], in1=c.ps[4][:].rearrange("p (h t) -> p h t", h=4),
                                                  op=ALU.add), r=['stt', ('ps', 4)], w=['stt'])

        def sT_bank(n):
            return (1, 6)[n % 2]

        def emit_scores(n):
            kb = n % 2
            k_transposes(n, 0, kb)
            sb_ = sT_bank(n)
            for hh in range(4):
                S.op('pe', lambda e, hh=hh: e.matmul(c.ps[sb_][:, hh * 128:(hh + 1) * 128],
                                                     lhsT=KTs[:, hh, n * 128:(n + 1) * 128],
                                                     rhs=QTs[:, hh, n * 128:(n + 1) * 128], start=True, stop=True),
                     r=['KTs', 'QTs'], w=[('ps', sb_)])
            S.op('dve', lambda e: e.tensor_tensor(out=PT_[kb][:], in0=c.ps[sb_][:].rearrange("p (h t) -> p h t", h=4),
                                                  in1=mask[:], op=ALU.mult), r=[('ps', sb_), 'mask'], w=[('PT_', kb)])

        it = 0
        for s in range(NSEQ):
            S.dma('sp', QTs[:], c.QT.ap()[s].rearrange("(h p) t -> p h t", p=128), r=[('QT', s, i) for i in range(NT)], w=['QTs'])
            S.dma('sp', KTs[:], c.KT.ap()[s].rearrange("(h p) t -> p h t", p=128), r=[('KT', s, i) for i in range(NT)], w=['KTs'])
            S.dma('sp', Vs[:], c.V.ap()[s].rearrange("(n p) f -> p n f", p=128), r=[('V', s, i) for i in range(NT)], w=['Vs'])
            S.op('pool', lambda e: e.memset(stt[:], 0.0), w=['stt'])
            if NB > 1:
                k_transposes(NB - 1, 1, (NB - 1) % 2)
            for n in range(NB - 1, -1, -1):
                S.op('act', lambda e, n=n: e.activation(out=SB[:, n, :, :], in_=stt[:], func=AF.Copy), r=['stt'], w=['SB'])
                if n > 0:
                    if n - 1 > 0:
                        k_transposes(n - 1, 1, (n - 1) % 2)
                    kv_update(n, 1, n % 2)
            S.op('pool', lambda e: e.memset(stt[:], 0.0), w=['stt'])
            S.op('pool', lambda e: e.memset(stb[:], 0.0), w=['stb'])
            pending = []
            emit_scores(0)
            for i in range(NT):
                t0 = i * 512
                tb = it % 2
                it += 1
                S.dma('sp', GSt[tb][:], c.GS.ap()[s].rearrange("(j p) f -> p j f", p=128)[:, 4 * i:4 * i + 4, :],
                      r=[('GS', s, i)], w=[('GSt', tb)])
                for d, dst in ((0, qf), (1, qbk)):
                    eng = 'dve' if d == 0 else 'pool'
                    S.op(eng, lambda e, d=d, dst=dst: e.tensor_tensor(
                        out=dst[tb][:].rearrange("p h (j t) -> p h j t", j=4),
                        in0=QTs[:, :, t0:t0 + 512].rearrange("p h (j t) -> p h j t", j=4),
                        in1=sb_ap(wq, d * 512, [[128, 4], [0, 4], [1, 128]]), op=ALU.mult),
                        r=['QTs', 'wq'], w=[(('qf', 'qbk')[d], tb)])
                for j in range(4):
                    n = 4 * i + j
                    kb = n % 2
                    ob = 2 + (n % 2)
                    ub = n % 2
                    for hh in range(4):
                        o_ap = c.ps[ob][:, hh * 128:(hh + 1) * 128]
                        S.op('pe', lambda e, hh=hh, o_ap=o_ap: e.matmul(o_ap, lhsT=PT_[kb][:, hh, :],
                                                                      rhs=Vs[:, n, hh * 128:(hh + 1) * 128],
                                                                      start=True, stop=False),
                             r=[('PT_', kb), 'Vs'], w=[('ps', ob)])
                        S.op('pe', lambda e, hh=hh, o_ap=o_ap: e.matmul(o_ap, lhsT=qf[tb][:, hh, j * 128:(j + 1) * 128],
                                                                      rhs=stb[:, hh, :], start=False, stop=False),
                             r=[('qf', tb), 'stb'], w=[('ps', ob)])
                        S.op('pe', lambda e, hh=hh, o_ap=o_ap: e.matmul(o_ap, lhsT=qbk[tb][:, hh, j * 128:(j + 1) * 128],
                                                                      rhs=SB[:, n, hh, :], start=False, stop=True),
                             r=[('qbk', tb), 'SB'], w=[('ps', ob)])
                    if n < NB - 1:
                        kv_update(n, 0, kb)
                        S.op('act', lambda e: e.activation(out=stb[:], in_=stt[:], func=AF.Copy), r=['stt'], w=['stb'])
                        emit_scores(n + 1)
                    for fn_ in pending:
                        fn_()
                    pending = []
                    o3 = c.ps[ob][:].rearrange("p (h t) -> p h t", h=4)
                    S.op('dve', lambda e, o3=o3: e.tensor_reduce(out=sm[:, 0:4], in_=o3, axis=AX.X, op=ALU.add),
                         r=[('ps', ob)], w=['sm'])
                    S.op('act', lambda e, o3=o3: e.activation(out=osq[:], in_=o3, func=AF.Square), r=[('ps', ob)], w=['osq'])
                    S.op('dve', lambda e: e.tensor_reduce(out=sm[:, 4:8], in_=osq[:], axis=AX.X, op=ALU.add), r=['osq'], w=['sm'])
                    S.op('dve', lambda e: e.tensor_scalar(out=sm2[:, 0:4], in0=sm[:, 0:4], scalar1=-1.0 / 128, scalar2=None,
                                                          op0=ALU.mult), r=['sm'], w=['sm2'])
                    S.op('dve', lambda e: e.tensor_tensor(out=sm2[:, 4:8], in0=sm2[:, 0:4], in1=sm2[:, 0:4], op=ALU.mult),
                         r=['sm2'], w=['sm2'])
                    S.op('dve', lambda e: e.scalar_tensor_tensor(out=sm2[:, 8:12], in0=sm[:, 4:8], scalar=1.0 / 128, in1=sm2[:, 4:8],
                                                                 op0=ALU.mult, op1=ALU.subtract), r=['sm', 'sm2'], w=['sm2'])
                    S.op('act', lambda e: e.activation(out=sm2[:, 12:16], in_=sm2[:, 8:12], func=AF.Sqrt, bias=EPS, scale=1.0),
                         r=['sm2'], w=['sm2'])
                    S.op('dve', lambda e: e.reciprocal(out=sm2[:, 16:20], in_=sm2[:, 12:16]), r=['sm2'], w=['sm2'])
                    S.op('dve', lambda e, o3=o3, ub=ub: e.tensor_tensor(out=tmpo[ub][:], in0=o3, in1=bc4(sm2, 0), op=ALU.add),
                         r=[('ps', ob), 'sm2'], w=[('tmpo', ub)])
                    S.op('pool', lambda e, ub=ub: e.tensor_tensor(out=tmpo[ub][:], in0=tmpo[ub][:], in1=bc4(sm2, 16), op=ALU.mult),
                         r=[('tmpo', ub), 'sm2'], w=[('tmpo', ub)])
                    S.op('pool', lambda e, ub=ub, j=j, tb=tb: e.tensor_tensor(
                        out=rtok[ub][:], in0=tmpo[ub][:], in1=GSt[tb][:, j, :].rearrange("p (h t) -> p h t", h=4), op=ALU.mult),
                        r=[('tmpo', ub), ('GSt', tb)], w=[('rtok', ub)])

                    def finish(ub=ub, j=j, tb=tb, s=s, i=i, t0=t0):
                        for hh in range(4):
                            S.op('pe', lambda e, hh=hh: e.transpose(psR[:, hh * 128:(hh + 1) * 128], rtok[ub][:, hh, :], c.ident_b[:]),
                                 r=[('rtok', ub), 'ident_b'], w=[('ps', 5)])
                        S.op('act', lambda e: e.activation(out=mixst[tb][:, :, j * 128:(j + 1) * 128],
                                                           in_=psR[:, 0:512].rearrange("p (h t) -> p h t", h=4), func=AF.Copy),
                             r=[('ps', 5)], w=[('mixst', tb)])
                        if j == 3:
                            S.dma('pool', c.MIXT.ap()[s].rearrange("(k p) t -> p k t", p=128)[:, 0:4, t0:t0 + 512], mixst[tb][:],
                                  r=[('mixst', tb)], w=[('MIXT_r', s, i)])
                    pending.append(finish)
            for fn_ in pending:
                fn_()
            pending = []
        S.barrier()


POOL_WINDOWS = (2, 4, 8, 16)


def pass_pool(c, l):
    nc, S, L, NSEQ, NT = c.nc, c.S, c.L, c.NSEQ, c.NT
    LP = L + 16
    with ExitStack() as st:
        def alloc(name, shape, dt):
            return st.enter_context(nc.sbuf_tensor(uname(name), list(shape), dt))
        U = alloc("pl_U", [128, LP], F32)
        W2 = alloc("pl_W2", [128, LP], F32)
        W4 = alloc("pl_W4", [128, LP], F32)
        W8 = alloc("pl_W8", [128, LP], F32)
        W16 = alloc("pl_W16", [128, LP], F32)
        Wn = {2: W2, 4: W4, 8: W8, 16: W16}
        M = alloc("pl_M", [128, L], F32)
        Mb = alloc("pl_Mb", [128, L], BF16)
        O = alloc("pl_O", [128, L], BF16)
        ic = alloc("pl_ic", [128, 4, 16], F32)
        wst = alloc("pl_wst", [128, 2, 128], F32)
        wpb = alloc("pl_wpb", [128, 2, 128], BF16)
        psc = alloc("pl_psc", [128, 2], F32)
        S.dma('sp', ic[:].rearrange("p a b -> p (a b)"), c.k_invcnt.ap().partition_broadcast(128), w=['ic'])
        S.dma('sp', psc[:], c.pool_scale.ap()[l], w=['psc'])
        S.op('dve', lambda e: e.memset(wst[:], 0.0), w=['wst'])
        for g in range(4):
            ct, hf = g // 2, g % 2
            S.dma('sp', wst[hf * 64:(hf + 1) * 64, ct, hf * 64:(hf + 1) * 64], c.pool_w.ap()[l, g], w=['wst'])
        S.op('dve', lambda e: e.tensor_copy(out=wpb[:], in_=wst[:]), r=['wst'], w=['wpb'])
        S.op('pool', lambda e: e.memset(U[:], 0.0), w=['U'])
        rr = 0
        for s in range(NSEQ):
            for ct in range(2):
                S.dma('sp', U[:, 8:8 + L], c.PLT.ap()[s, ct * 128:(ct + 1) * 128, :], r=[('PLT', s, i) for i in range(NT)], w=['U'])
                S.op('dve', lambda e: e.tensor_tensor(out=W2[:, 1:LP], in0=U[:, 0:LP - 1], in1=U[:, 1:LP], op=ALU.add),
                     r=['U'], w=['W2'])
                S.op('pool', lambda e: e.tensor_tensor(out=W4[:, 2:LP - 1], in0=W2[:, 1:LP - 2], in1=W2[:, 3:LP], op=ALU.add),
                     r=['W2'], w=['W4'])
                S.op('dve', lambda e: e.tensor_tensor(out=W8[:, 4:LP - 3], in0=W4[:, 2:LP - 5], in1=W4[:, 6:LP - 1], op=ALU.add),
                     r=['W4'], w=['W8'])
                S.op('pool', lambda e: e.tensor_tensor(out=W16[:, 8:LP - 7], in0=W8[:, 4:LP - 11], in1=W8[:, 12:LP - 3], op=ALU.add),
                     r=['W8'], w=['W16'])
                for hf in range(2):
                    g = ct * 2 + hf
                    w = POOL_WINDOWS[g]
                    Wt = Wn[w]
                    p0, p1 = hf * 64, (hf + 1) * 64
                    eng = 'dve' if hf == 0 else 'pool'
                    S.op('dve', lambda e, Wt=Wt, w=w, p0=p0, p1=p1: e.scalar_tensor_tensor(
                        out=M[p0:p1, :], in0=Wt[p0:p1, 8:8 + L], scalar=1.0 / w, in1=U[p0:p1, 8:8 + L],
                        op0=ALU.mult, op1=ALU.subtract), r=['W%d' % w, 'U'], w=['M'])
                    for (a0, io) in ((0, 0), (L - 8, 8)):
                        S.op(eng, lambda e, Wt=Wt, p0=p0, p1=p1, a0=a0, io=io, g=g: e.tensor_tensor(
                            out=M[p0:p1, a0:a0 + 8], in0=Wt[p0:p1, 8 + a0:16 + a0], in1=ic[p0:p1, g, io:io + 8], op=ALU.mult),
                            r=['W%d' % w, 'ic', 'M'], w=['M'])
                        S.op(eng, lambda e, p0=p0, p1=p1, a0=a0: e.tensor_tensor(
                            out=M[p0:p1, a0:a0 + 8], in0=M[p0:p1, a0:a0 + 8], in1=U[p0:p1, 8 + a0:16 + a0], op=ALU.subtract),
                            r=['U', 'M'], w=['M'])
                S.op('act', lambda e: e.activation(out=Mb[:], in_=M[:], func=AF.Copy), r=['M'], w=['Mb'])
                for i in range(NT):
                    bank = 1 + (rr % 4)
                    rr += 1
                    S.op('pe', lambda e, bank=bank, i=i, ct=ct: e.matmul(c.ps[bank][:], lhsT=wpb[:, ct, :], rhs=Mb[:, i * 512:(i + 1) * 512],
                                                                    start=True, stop=True), r=['wpb', 'Mb'], w=[('ps', bank)])
                    S.op('act', lambda e, bank=bank, i=i, ct=ct: e.activation(out=O[:, i * 512:(i + 1) * 512], in_=c.ps[bank][:],
                                                                         func=AF.Copy, scale=psc[:, ct:ct + 1]),
                         r=[('ps', bank), 'psc'], w=['O'])
                S.dma('pool', c.MIXT.ap()[s, 768 + ct * 128:768 + (ct + 1) * 128, :], O[:], r=['O'], w=[('MIXT_p', s, ct)])
        S.barrier()


def prologue_filters(c):
    nc, S, L, DEPTH, NT = c.nc, c.S, c.L, c.DEPTH, c.NT
    for l in range(DEPTH):
        with ExitStack() as st:
            def alloc(name, shape, dt):
                return st.enter_context(nc.sbuf_tensor(uname(name), list(shape), dt))
            w1 = alloc("hf_w1", [33, 64], F32)
            w2 = alloc("hf_w2", [64, 64], F32)
            w3 = alloc("hf_w3", [64, 1024], F32)
            b1 = alloc("hf_b1", [64, 1], F32)
            b2 = alloc("hf_b2", [64, 1], F32)
            fr = alloc("hf_fr", [64, 1], F32)
            fb = alloc("hf_fb", [64, 2], F32)
            negd = alloc("hf_negd", [128, 2], F32)
            bias = alloc("hf_bias", [128, 4], F32)
            S.dma('sp', w1[:], c.hy_w1.ap()[l], w=['w1'])
            S.dma('sp', w2[:], c.hy_w2.ap()[l], w=['w2'])
            S.dma('sp', w3[:], c.hy_w3.ap()[l * 64:(l + 1) * 64, :], w=['w3'])
            S.dma('sp', b1[:], c.hy_b1.ap()[l], w=['b1'])
            S.dma('sp', b2[:], c.hy_b2.ap()[l], w=['b2'])
            S.dma('sp', fr[:], c.hy_freq.ap()[l], w=['fr'])
            S.dma('sp', negd[:], c.k_negdelta.ap(), w=['negd'])
            S.dma('sp', bias[:], c.hy_bias.ap()[l], w=['bias'])
            S.op('dve', lambda e: e.tensor_tensor(out=fb[:, 0:1], in0=b1[:], in1=fr[:], op=ALU.mult), r=['b1', 'fr'], w=['fb'])
            S.op('dve', lambda e: e.tensor_tensor(out=fb[:, 1:2], in0=b2[:], in1=fr[:], op=ALU.mult), r=['b2', 'fr', 'fb'], w=['fb'])
            FB = [[[alloc("hf_FB%d%d%d" % (g, o, ct), [128, L], F32) for ct in range(2)] for o in range(2)] for g in range(2)]
            feats_g = [alloc("hf_feats%d" % i, [33, 512], F32) for i in range(2)]
            tb_g = [alloc("hf_tb%d" % i, [128, 512], F32) for i in range(2)]
            a_g = [alloc("hf_a%d" % i, [64, 512], F32) for i in range(2)]
            ki_g = [alloc("hf_ki%d" % i, [64, 512], I32) for i in range(2)]
            r_g = [alloc("hf_r%d" % i, [64, 512], F32) for i in range(2)]
            h1_g = [alloc("hf_h1%d" % i, [64, 512], F32) for i in range(2)]
            h2_g = [alloc("hf_h2%d" % i, [64, 512], F32) for i in range(2)]
            dec_g = [[alloc("hf_dec%d%d" % (g_, i), [128, 512], F32) for i in range(2)] for g_ in range(2)]
            asum = alloc("hf_asum", [128, 8], F32)
            tot = alloc("hf_tot", [128, 4], F32)
            stg = [alloc("hf_stg%d" % i, [128, L], BF16) for i in range(2)]

            def sin_layer(psb, fcol, dst, dkey, g):
                a_sb, ki, rr_ = a_g[g], ki_g[g], r_g[g]
                S.op('dve', lambda e: e.tensor_scalar(out=a_sb[:], in0=c.ps[psb][0:64, :], scalar1=fr[:, 0:1], scalar2=fb[:, fcol:fcol + 1],
                                                      op0=ALU.mult, op1=ALU.add), r=[('ps', psb), 'fr', 'fb'], w=[('a_sb', g)])
                S.op('dve', lambda e: e.tensor_scalar(out=ki[:], in0=a_sb[:], scalar1=float(1.0 / (2 * PI)), scalar2=None, op0=ALU.mult),
                     r=[('a_sb', g)], w=[('ki', g)])
                S.op('dve', lambda e: e.scalar_tensor_tensor(out=rr_[:], in0=ki[:], scalar=float(-2 * PI), in1=a_sb[:],
                                                             op0=ALU.mult, op1=ALU.add), r=[('ki', g), ('a_sb', g)], w=[('rr_', g)])
                S.op('dve', lambda e: e.tensor_scalar(out=rr_[:], in0=rr_[:], scalar1=-3.141592, scalar2=3.141592,
                                                      op0=ALU.max, op1=ALU.min), r=[('rr_', g)], w=[('rr_', g)])
                S.op('act', lambda e: e.activation(out=dst[:], in_=rr_[:], func=AF.Sin), r=[('rr_', g)], w=[dkey])

            rb = 0
            for i in range(NT):
                for g in range(2):
                    feats, tb, h1, h2, dec = feats_g[g], tb_g[g], h1_g[g], h2_g[g], dec_g[g]
                    pa, pb_ = (0, 1) if g == 0 else (6, 7)
                    S.dma('sp', feats[:], c.k_feats.ap()[g * 33:(g + 1) * 33, i * 512:(i + 1) * 512], w=[('feats', g)])
                    S.dma('sp', tb[:], c.k_feats.ap()[g * 33:g * 33 + 1, i * 512:(i + 1) * 512].partition_broadcast(128), w=[('tb', g)])
                    S.op('pe', lambda e, feats=feats, pa=pa: e.matmul(c.ps[pa][0:64, :], lhsT=w1[:], rhs=feats[:], start=True, stop=True),
                         r=['w1', ('feats', g)], w=[('ps', pa)])
                    sin_layer(pa, 0, h1, ('h1', g), g)
                    S.op('pe', lambda e, h1=h1, pb_=pb_: e.matmul(c.ps[pb_][0:64, :], lhsT=w2[:], rhs=h1[:], start=True, stop=True),
                         r=['w2', ('h1', g)], w=[('ps', pb_)])
                    sin_layer(pb_, 1, h2, ('h2', g), g)
                    for ct in range(2):
                        S.op('act', lambda e, ct=ct, dec=dec, tb=tb: e.activation(out=dec[ct][:], in_=tb[:], func=AF.Exp, scale=negd[:, ct:ct + 1]),
                             r=[('tb', g), 'negd'], w=[('dec', g, ct)])
                    for o in range(2):
                        for ct in range(2):
                            col = o * 512 + g * 256 + ct * 128
                            bank = 2 + (rb % 4)
                            rb += 1
                            S.op('pe', lambda e, bank=bank, col=col, h2=h2: e.matmul(c.ps[bank][:], lhsT=w3[:, col:col + 128], rhs=h2[:],
                                                                                    start=True, stop=True), r=['w3', ('h2', g)], w=[('ps', bank)])
                            S.op('dve', lambda e, bank=bank, g=g, o=o, ct=ct, i=i, dec=dec: e.tensor_tensor(
                                out=FB[g][o][ct][:, i * 512:(i + 1) * 512], in0=c.ps[bank][:], in1=dec[ct][:], op=ALU.mult),
                                r=[('ps', bank), ('dec', g, ct)], w=[('FB', g, o, ct)])
            for g in range(2):
                n = L if g == 0 else L - 1
                for o in range(2):
                    for ct in range(2):
                        idx = g * 4 + o * 2 + ct
                        S.op('dve', lambda e, g=g, o=o, ct=ct, idx=idx, n=n: e.tensor_reduce(
                            out=asum[:, idx:idx + 1], in_=FB[g][o][ct][:, 0:n], axis=AX.X, op=ALU.add, apply_absolute_value=True),
                            r=[('FB', g, o, ct)], w=['asum'])
            S.op('dve', lambda e: e.tensor_tensor(out=tot[:], in0=asum[:, 0:4], in1=asum[:, 4:8], op=ALU.add), r=['asum'], w=['tot'])
            S.op('dve', lambda e: e.reciprocal(out=tot[:], in_=tot[:]), r=['tot'], w=['tot'])
            sb_i = 0
            for o in range(2):
                for ct in range(2):
                    oc = o * 2 + ct
                    S.op('dve', lambda e, o=o, ct=ct, oc=oc: e.tensor_scalar(out=FB[0][o][ct][:], in0=FB[0][o][ct][:], scalar1=tot[:, oc:oc + 1],
                                                                          scalar2=None, op0=ALU.mult), r=[('FB', 0, o, ct), 'tot'], w=[('FB', 0, o, ct)])
                    S.op('dve', lambda e, o=o, ct=ct, oc=oc: e.tensor_tensor(out=FB[0][o][ct][:, 0:1], in0=FB[0][o][ct][:, 0:1], in1=bias[:, oc:oc + 1],
                                                                          op=ALU.add), r=[('FB', 0, o, ct), 'bias'], w=[('FB', 0, o, ct)])
                    rows = c.G.ap()[l, o * 256 + ct * 128:o * 256 + (ct + 1) * 128, :]
                    b = sb_i % 2
                    sb_i += 1
                    S.op('act', lambda e, o=o, ct=ct, b=b: e.activation(out=stg[b][:], in_=FB[0][o][ct][:], func=AF.Copy),
                         r=[('FB', 0, o, ct)], w=[('stg', b)])
                    S.dma('pool', rows[:, L - 1:2 * L - 1], stg[b][:], r=[('stg', b)], w=[('G', l, o, ct, 0)])
                    b = sb_i % 2
                    sb_i += 1
                    S.op('pool', lambda e, o=o, ct=ct, oc=oc, b=b: e.tensor_scalar(out=stg[b][:], in0=FB[1][o][ct][:], scalar1=tot[:, oc:oc + 1],
                                                                               scalar2=None, op0=ALU.mult), r=[('FB', 1, o, ct), 'tot'], w=[('stg', b)])
                    S.dma('pool', rows[:, 0:L - 1], stg[b][:, 0:L - 1], r=[('stg', b)], w=[('G', l, o, ct, 1)])
            S.barrier()


def pass_hyena(c, l):
    nc, S, L, NSEQ, NT, NB, NLAG = c.nc, c.S, c.L, c.NSEQ, c.NT, c.NB, c.NLAG
    SN = NSEQ * NB
    gsz = max(1, min(128, 512 // SN))
    ngrp = (128 + gsz - 1) // gsz
    for ct in range(2):
        with ExitStack() as st:
            def alloc(name, shape, dt):
                return st.enter_context(nc.sbuf_tensor(uname(name), list(shape), dt))
            cw = alloc("hy_cw", [128, 6, 3], F32)
            S.dma('sp', cw[:], c.hy_conv.ap()[l], w=['cw'])
            X = [alloc("hy_X%d" % i, [128, L + 2], F32) for i in range(2)]
            acc1 = alloc("hy_acc1", [128, L], F32)
            acc2 = alloc("hy_acc2", [128, L], F32)
            ub = [alloc("hy_ub%d" % i, [128, L], BF16) for i in range(2)]
            UR = alloc("hy_UR", [128, 128, NSEQ, NB], BF16)
            HX1 = alloc("hy_HX1", [128, 128, NSEQ, NB], BF16)
            HX2 = alloc("hy_HX2", [128, 128, NSEQ, NB], BF16)
            KS = [alloc("hy_KS%d" % i, [128, NLAG * 128], BF16) for i in range(2)]
            TM = [UR, HX1, HX2]
            for b in range(2):
                S.op('pool', lambda e, b=b: e.memset(X[b][:, 0:1], 0.0), w=[('X', b)])
                S.op('pool', lambda e, b=b: e.memset(X[b][:, L + 1:L + 2], 0.0), w=[('X', b)])
            it = 0
            tg = 0
            for s in range(NSEQ):
                for r_ in range(3):
                    b = it % 2
                    it += 1
                    tile = r_ * 2 + ct
                    rows = r_ * 256 + ct * 128
                    S.dma('sp', X[b][:, 1:L + 1], c.HYT.ap()[s, rows:rows + 128, :], r=[('HYT', s, i) for i in range(NT)], w=[('X', b)])
                    S.op('act', lambda e, b=b, tile=tile: e.activation(out=acc1[:], in_=X[b][:, 1:L + 1], func=AF.Copy,
                                                                     scale=cw[:, tile, 1:2]), r=[('X', b), 'cw'], w=['acc1'])
                    S.op('dve', lambda e, b=b, tile=tile: e.scalar_tensor_tensor(out=acc2[:], in0=X[b][:, 0:L], scalar=cw[:, tile, 0:1],
                                                                               in1=acc1[:], op0=ALU.mult, op1=ALU.add),
                         r=[('X', b), 'cw', 'acc1'], w=['acc2'])
                    if r_ == 0:
                        o_ap = sb_ap(ub[b], L - 1, [[-1, L]])
                    else:
                        o_ap = ub[b][:]
                    S.op('dve', lambda e, b=b, tile=tile, o_ap=o_ap: e.scalar_tensor_tensor(
                        out=o_ap, in0=X[b][:, 2:L + 2], scalar=cw[:, tile, 2:3], in1=acc2[:], op0=ALU.mult, op1=ALU.add),
                        r=[('X', b), 'cw', 'acc2'], w=[('ub', b)])
                    for g8 in range(NB // 8):
                        bank = 6 + (tg % 2)
                        tg += 1
                        psb = c.ps[bank][:].bitcast(BF16)
                        for q in range(8):
                            blk = g8 * 8 + q
                            S.op('pe', lambda e, psb=psb, q=q, blk=blk, b=b: e.transpose(psb[:, q * 128:(q + 1) * 128],
                                                                                      ub[b][:, blk * 128:(blk + 1) * 128], c.ident_b[:]),
                                 r=[('ub', b), 'ident_b'], w=[('ps', bank)])
                        if r_ == 0:
                            a_first = NB - 1 - g8 * 8
                            dst = sb_ap(TM[0], s * NB + a_first, [[-1, 8], [SN, 128]])
                        else:
                            dst = sb_ap(TM[r_], s * NB + g8 * 8, [[1, 8], [SN, 128]])
                        eng = 'act' if (tg % 2) else 'dve'
                        rkeys = [('ps', bank)]
                        wkeys = [('TM', r_, gi) for gi in range(ngrp)]
                        if eng == 'act':
                            S.op('act', lambda e, dst=dst, psb=psb: e.activation(out=dst, in_=psb.rearrange("p (q t) -> p q t", q=8), func=AF.Copy),
                                 r=rkeys, w=wkeys)
                        else:
                            S.op('dve', lambda e, dst=dst, psb=psb: e.tensor_copy(out=dst, in_=psb.rearrange("p (q t) -> p q t", q=8)),
                                 r=rkeys, w=wkeys)
            kc_i = 0
            for o in range(2):
                GB = HX1 if o == 0 else HX2
                gb_i = 1 if o == 0 else 2
                for gi in range(ngrp):
                    c0 = gi * gsz
                    n_c = min(gsz, 128 - c0)
                    bank = gi % 2
                    first = True
                    for ci in range(n_c):
                        ch = c0 + ci
                        kb = kc_i % 2
                        kc_i += 1
                        src = AP(c.G, ((l * 512 + o * 256 + ct * 128 + ch) * 2 * L), [[1, 128], [1, NLAG * 128]])
                        S.dma('sp', KS[kb][:], src, r=[('G', l, o, ct, 0), ('G', l, o, ct, 1)], w=[('KS', kb)])
                        for d in range(-(NB - 1), NB):
                            a0 = max(0, -d)
                            a1 = min(NB, NB - d)
                            n = a1 - a0
                            o_ap = sb_ap(c.ps[bank], ci * SN + a0 + d, [[NB, NSEQ], [1, n]])
                            r_ap = sb_ap(UR, ch * SN + a0, [[NB, NSEQ], [1, n]])
                            S.op('pe', lambda e, o_ap=o_ap, r_ap=r_ap, kb=kb, d=d, first=first: e.matmul(
                                o_ap, lhsT=KS[kb][:, (d + NB - 1) * 128:(d + NB) * 128], rhs=r_ap,
                                start=first, stop=False, skip_group_check=True),
                                r=[('KS', kb), ('TM', 0, gi)], w=[('ps', bank)])
                            first = False
                    ncol = n_c * SN
                    gflat = sb_ap(GB, c0 * SN, [[1, ncol]])
                    S.op('dve', lambda e, gflat=gflat, bank=bank, ncol=ncol: e.tensor_tensor(
                        out=gflat, in0=c.ps[bank][:, 0:ncol], in1=gflat, op=ALU.mult),
                        r=[('ps', bank), ('TM', gb_i, gi)], w=[('TM', gb_i, gi)])
                    if o == 0:
                        zb = 2 + (gi % 2)
                        S.op('pe', lambda e, zb=zb, gflat=gflat, ncol=ncol: e.matmul(c.ps[zb][:, 0:ncol], lhsT=c.J_b[:], rhs=gflat,
                                                                                    start=True, stop=True),
                             r=[('TM', 1, gi), 'J_b'], w=[('ps', zb)])
                        uflat = sb_ap(UR, c0 * SN, [[1, ncol]])
                        S.op('act', lambda e, zb=zb, uflat=uflat, ncol=ncol: e.activation(out=uflat, in_=c.ps[zb][:, 0:ncol], func=AF.Copy),
                             r=[('ps', zb)], w=[('TM', 0, gi)])
            ost = ub
            it = 0
            for s in range(NSEQ):
                b = it % 2
                it += 1
                for g8 in range(NB // 8):
                    bank = 6 + (tg % 2)
                    tg += 1
                    psb = c.ps[bank][:].bitcast(BF16)
                    for q in range(8):
                        a = g8 * 8 + q
                        i_ap = sb_ap(HX2, s * NB + a, [[SN, 128]])
                        S.op('pe', lambda e, psb=psb, q=q, i_ap=i_ap: e.transpose(psb[:, q * 128:(q + 1) * 128], i_ap, c.ident_b[:]),
                             r=[('TM', 2, gi) for gi in range(ngrp)] + ['ident_b'], w=[('ps', bank)])
                    S.op('act', lambda e, psb=psb, g8=g8, b=b: e.activation(out=ost[b][:, g8 * 1024:(g8 + 1) * 1024], in_=psb, func=AF.Copy),
                         r=[('ps', bank)], w=[('ub', b)])
                S.dma('pool', c.MIXT.ap()[s, 512 + ct * 128:512 + (ct + 1) * 128, :], ost[b][:], r=[('ub', b)], w=[('MIXT_h', s, ct)])
            S.barrier()


def pass_p3(c, l):
    nc, S, L, NSEQ = c.nc, c.S, c.L, c.NSEQ
    TW = 510
    tiles = [(a, min(a + TW, L)) for a in range(0, L, TW)]
    with ExitStack() as st:
        def alloc(name, shape, dt):
            return st.enter_context(nc.sbuf_tensor(uname(name), list(shape), dt))
        alloc_wstage(c, st)
        Wo = load_weight_bf16(c, st, "p3_Wo", c.w_out.ap()[l * D:(l + 1) * D, :], KC, D, 'Wo')
        gmf = alloc("p3_gm", [128, KC], F32)
        S.dma('sp', gmf[:], c.norm_ffn.ap()[l], w=['gmf'])
        Wu = load_weight_bf16(c, st, "p3_Wu", c.w_up.ap()[l * D:(l + 1) * D, :], KC, 2 * DFF, 'Wu', rowscale=gmf, rskey='gmf')
        fcw = alloc("p3_fcw", [128, NJ, 3], F32)
        S.dma('sp', fcw[:], c.ffn_conv.ap()[l], w=['fcw'])
        mx = alloc("p3_mx", [128, KC, 512], BF16)
        xt = alloc("p3_xt", [128, KC, 512], F32)
        xsq = alloc("p3_xsq", [128, KC, 512], BF16)
        rs = alloc("p3_rs", [128, 512], F32)
        h2 = alloc("p3_h2", [128, KC, 512], BF16)
        hid = alloc("p3_hid", [128, NJ, 512], BF16)
        acc = [alloc("p3_acc%d" % i, [128, 512], F32) for i in range(2)]
        gl = [alloc("p3_gl%d" % i, [128, 512], F32) for i in range(2)]
        rr = 0
        jj = 0
        for s in range(NSEQ):
            for (ta, tb_) in tiles:
                ntok = tb_ - ta
                ncol = ntok + 2
                lo = 1 if ta == 0 else 0
                hi = ncol - 1 if tb_ == L else ncol
                fm = lambda T: T.ap()[s].rearrange("(k p) t -> p k t", p=128)
                S.dma('sp', mx[:, :, lo:hi], fm(c.MIXT)[:, :, ta - 1 + lo:ta - 1 + hi], w=['mx'])
                S.dma('sp', xt[:, :, lo:hi], fm(c.XT)[:, :, ta - 1 + lo:ta - 1 + hi], w=['xt'])
                if lo == 1:
                    S.op('pool', lambda e: e.memset(mx[:, :, 0:1], 0.0), w=['mx'])
                    S.op('pool', lambda e: e.memset(xt[:, :, 0:1], 0.0), w=['xt'])
                if hi == ncol - 1:
                    S.op('pool', lambda e, ncol=ncol: e.memset(mx[:, :, ncol - 1:ncol], 0.0), w=['mx'])
                    S.op('pool', lambda e, ncol=ncol: e.memset(xt[:, :, ncol - 1:ncol], 0.0), w=['xt'])
                for m in range(KC):
                    bank = 1 + (rr % 6)
                    rr += 1
                    for k in range(KC):
                        S.op('pe', lambda e, k=k, m=m, bank=bank, ncol=ncol: e.matmul(
                            c.ps[bank][:, 0:ncol], lhsT=Wo[:, k, m * 128:(m + 1) * 128], rhs=mx[:, k, 0:ncol],
                            start=(k == 0), stop=(k == KC - 1)), r=['Wo', 'mx'], w=[('ps', bank)])
                    S.op('dve', lambda e, m=m, bank=bank, ncol=ncol: e.tensor_tensor(
                        out=xt[:, m, 0:ncol], in0=c.ps[bank][:, 0:ncol], in1=xt[:, m, 0:ncol], op=ALU.add),
                        r=[('ps', bank), 'xt'], w=['xt'])
                S.dma('pool', fm(c.X1T)[:, :, ta:tb_], xt[:, :, 1:1 + ntok], r=['xt'], w=[('X1T', s, ta)])
                S.op('act', lambda e, ncol=ncol: e.activation(out=xsq[:, :, 0:ncol], in_=xt[:, :, 0:ncol], func=AF.Square),
                     r=['xt'], w=['xsq'])
                for k in range(KC):
                    S.op('pe', lambda e, k=k, ncol=ncol: e.matmul(c.ps[0][:, 0:ncol], lhsT=c.ones_b[:], rhs=xsq[:, k, 0:ncol],
                                                                  start=(k == 0), stop=(k == KC - 1)),
                         r=['xsq', 'ones_b'], w=[('ps', 0)])
                rms_rstd(c, 0, rs, ncol, ('ps', 0), 'rs', D)
                for k in range(KC):
                    eng = 'dve' if k % 2 == 0 else 'pool'
                    S.op(eng, lambda e, k=k, ncol=ncol: e.tensor_tensor(
                        out=h2[:, k, 0:ncol], in0=xt[:, k, 0:ncol], in1=rs[:, 0:ncol], op=ALU.mult), r=['xt', 'rs'], w=['h2'])
                for j in range(NJ):
                    ab = jj % 2
                    jj += 1
                    bg = 1 + (rr % 6)
                    rr += 1
                    bu = 1 + (rr % 6)
                    rr += 1
                    for k in range(KC):
                        S.op('pe', lambda e, k=k, j=j, bg=bg, ncol=ncol: e.matmul(
                            c.ps[bg][:, 0:ncol], lhsT=Wu[:, k, j * 128:(j + 1) * 128], rhs=h2[:, k, 0:ncol],
                            start=(k == 0), stop=(k == KC - 1)), r=['Wu', 'h2'], w=[('ps', bg)])
                    for k in range(KC):
                        S.op('pe', lambda e, k=k, j=j, bu=bu, ncol=ncol: e.matmul(
                            c.ps[bu][:, 0:ncol], lhsT=Wu[:, k, DFF + j * 128:DFF + (j + 1) * 128], rhs=h2[:, k, 0:ncol],
                            start=(k == 0), stop=(k == KC - 1)), r=['Wu', 'h2'], w=[('ps', bu)])
                    S.op('act', lambda e, j=j, bg=bg, ab=ab, ntok=ntok: e.activation(
                        out=acc[ab][:, 0:ntok], in_=c.ps[bg][:, 1:1 + ntok], func=AF.Copy, scale=fcw[:, j, 1:2]),
                        r=[('ps', bg), 'fcw'], w=[('acc', ab)])
                    S.op('dve', lambda e, j=j, bg=bg, ab=ab, ntok=ntok: e.scalar_tensor_tensor(
                        out=acc[ab][:, 0:ntok], in0=c.ps[bg][:, 0:ntok], scalar=fcw[:, j, 0:1], in1=acc[ab][:, 0:ntok],
                        op0=ALU.mult, op1=ALU.add), r=[('ps', bg), 'fcw', ('acc', ab)], w=[('acc', ab)])
                    S.op('dve', lambda e, j=j, bg=bg, ab=ab, ntok=ntok: e.scalar_tensor_tensor(
                        out=acc[ab][:, 0:ntok], in0=c.ps[bg][:, 2:2 + ntok], scalar=fcw[:, j, 2:3], in1=acc[ab][:, 0:ntok],
                        op0=ALU.mult, op1=ALU.add), r=[('ps', bg), 'fcw', ('acc', ab)], w=[('acc', ab)])
                    S.op('act', lambda e, ab=ab, ntok=ntok: e.activation(out=gl[ab][:, 0:ntok], in_=acc[ab][:, 0:ntok],
                                                                        func=AF.Gelu_apprx_tanh), r=[('acc', ab)], w=[('gl', ab)])
                    S.op('dve', lambda e, j=j, bu=bu, ab=ab, ntok=ntok: e.tensor_tensor(
                        out=hid[:, j, 0:ntok], in0=c.ps[bu][:, 1:1 + ntok], in1=gl[ab][:, 0:ntok], op=ALU.mult),
                        r=[('ps', bu), ('gl', ab)], w=['hid'])
                S.dma('pool', c.HID.ap()[s].rearrange("(j p) t -> p j t", p=128)[:, :, ta:tb_], hid[:, :, 0:ntok],
                      r=['hid'], w=[('HID', s, ta)])
        S.barrier()


def pass_p4(c, l):
    nc, S, L, NSEQ, NT = c.nc, c.S, c.L, c.NSEQ, c.NT
    with ExitStack() as st:
        def alloc(name, shape, dt):
            return st.enter_context(nc.sbuf_tensor(uname(name), list(shape), dt))
        alloc_wstage(c, st)
        Wd = load_weight_bf16(c, st, "p4_Wd", c.w_down.ap()[l * DFF:(l + 1) * DFF, :], NJ, D, 'Wd')
        Wg = load_weight_bf16(c, st, "p4_Wg", c.ple_gate.ap()[l * D:(l + 1) * D, :], KC, D, 'Wg')
        Wp = load_weight_bf16(c, st, "p4_Wp", c.ple_w.ap()[l * 256:(l + 1) * 256, :], 2, D, 'Wp')
        pn = alloc("p4_pn", [128, KC], F32)
        S.dma('sp', pn[:], c.ple_norm.ap()[l], w=['pn'])
        hid_ = [alloc("p4_hid%d" % i_, [128, NJ, 512], BF16) for i_ in range(2)]
        x1_ = [alloc("p4_x1%d" % i_, [128, KC, 512], F32) for i_ in range(2)]
        pT_ = [alloc("p4_pT%d" % i_, [128, 2, 512], BF16) for i_ in range(2)]
        tcount = 0
        x2b = alloc("p4_x2b", [128, KC, 512], BF16)
        er = alloc("p4_er", [128, KC, 512], F32)
        esq = alloc("p4_esq", [128, KC, 512], BF16)
        sg = alloc("p4_sg", [128, KC, 512], BF16)
        rs = alloc("p4_rs", [128, 512], F32)
        tmp = [alloc("p4_tmp%d" % i, [128, 512], F32) for i in range(2)]
        rr = 0
        for s in range(NSEQ):
            for i in range(NT):
                t0 = i * 512
                fm = lambda T: T.ap()[s].rearrange("(k p) t -> p k t", p=128)[:, :, t0:t0 + 512]
                pb = tcount % 2
                tcount += 1
                hid, x1, pT = hid_[pb], x1_[pb], pT_[pb]
                khid, kx1, kpT = ('hid', pb), ('x1', pb), ('pT', pb)
                S.dma('sp', hid[:], c.HID.ap()[s].rearrange("(j p) t -> p j t", p=128)[:, :, t0:t0 + 512], w=[khid])
                S.dma('sp', x1[:], fm(c.X1T), w=[kx1])
                S.dma('sp', pT[:], c.PT.ap()[l, s].rearrange("(k p) t -> p k t", p=128)[:, :, t0:t0 + 512], w=[kpT])
                for m in range(KC):
                    bank = 1 + (rr % 7)
                    rr += 1
                    for j in range(NJ):
                        S.op('pe', lambda e, j=j, m=m, bank=bank, hid=hid: e.matmul(
                            c.ps[bank][:], lhsT=Wd[:, j, m * 128:(m + 1) * 128], rhs=hid[:, j, :],
                            start=(j == 0), stop=(j == NJ - 1)), r=['Wd', khid], w=[('ps', bank)])
                    S.op('dve', lambda e, m=m, bank=bank, x1=x1: e.tensor_tensor(out=x1[:, m, :], in0=c.ps[bank][:], in1=x1[:, m, :], op=ALU.add),
                         r=[('ps', bank), kx1], w=[kx1])
                    S.op('pool', lambda e, m=m, x1=x1: e.tensor_copy(out=x2b[:, m, :], in_=x1[:, m, :]), r=[kx1], w=['x2b'])
                for m in range(KC):
                    bank = 1 + (rr % 7)
                    rr += 1
                    for k in range(2):
                        S.op('pe', lambda e, k=k, m=m, bank=bank, pT=pT: e.matmul(
                            c.ps[bank][:], lhsT=Wp[:, k, m * 128:(m + 1) * 128], rhs=pT[:, k, :],
                            start=(k == 0), stop=(k == 1)), r=['Wp', kpT], w=[('ps', bank)])
                    S.op('act', lambda e, m=m, bank=bank: e.activation(out=er[:, m, :], in_=c.ps[bank][:], func=AF.Copy),
                         r=[('ps', bank)], w=['er'])
                    S.op('act', lambda e, m=m, bank=bank: e.activation(out=esq[:, m, :], in_=c.ps[bank][:], func=AF.Square),
                         r=[('ps', bank)], w=['esq'])
                for k in range(KC):
                    S.op('pe', lambda e, k=k: e.matmul(c.ps[0][:], lhsT=c.ones_b[:], rhs=esq[:, k, :],
                                                       start=(k == 0), stop=(k == KC - 1)), r=['esq', 'ones_b'], w=[('ps', 0)])
                rms_rstd(c, 0, rs, 512, ('ps', 0), 'rs', D)
                for m in range(KC):
                    bank = 1 + (rr % 7)
                    rr += 1
                    for k in range(KC):
                        S.op('pe', lambda e, k=k, m=m, bank=bank: e.matmul(
                            c.ps[bank][:], lhsT=Wg[:, k, m * 128:(m + 1) * 128], rhs=x2b[:, k, :],
                            start=(k == 0), stop=(k == KC - 1)), r=['Wg', 'x2b'], w=[('ps', bank)])
                    S.op('act', lambda e, m=m, bank=bank: e.activation(out=sg[:, m, :], in_=c.ps[bank][:], func=AF.Sigmoid),
                         r=[('ps', bank)], w=['sg'])
                    tb_ = m % 2
                    S.op('dve', lambda e, m=m, tb_=tb_: e.scalar_tensor_tensor(
                        out=tmp[tb_][:], in0=er[:, m, :], scalar=pn[:, m:m + 1], in1=rs[:], op0=ALU.mult, op1=ALU.mult),
                        r=['er', 'pn', 'rs'], w=[('tmp', tb_)])
                    S.op('pool', lambda e, m=m, tb_=tb_: e.tensor_tensor(out=tmp[tb_][:], in0=tmp[tb_][:], in1=sg[:, m, :], op=ALU.mult),
                         r=[('tmp', tb_), 'sg'], w=[('tmp', tb_)])
                    S.op('pool', lambda e, m=m, tb_=tb_, x1=x1: e.tensor_tensor(out=x1[:, m, :], in0=x1[:, m, :], in1=tmp[tb_][:], op=ALU.add),
                         r=[('tmp', tb_), kx1], w=[kx1])
                S.dma('pool', fm(c.XT), x1[:], r=[kx1], w=[('XT', s, i)])
        S.barrier()


def epilogue(c):
    nc, S, L, NSEQ, NT = c.nc, c.S, c.L, c.NSEQ, c.NT
    with ExitStack() as st:
        def alloc(name, shape, dt):
            return st.enter_context(nc.sbuf_tensor(uname(name), list(shape), dt))
        nf = alloc("ep_nf", [128, D], F32)
        S.dma('sp', nf[:], c.norm_final.ap().partition_broadcast(128), w=['nf'])
        xt = [alloc("ep_xt%d" % i, [128, KC, 512], F32) for i in range(2)]
        yt = [alloc("ep_yt%d" % i, [128, D], F32) for i in range(2)]
        sq = alloc("ep_sq", [128, D], F32)
        ss = alloc("ep_ss", [128, 4], F32)
        yo = [alloc("ep_yo%d" % i, [128, 4, D], F32) for i in range(2)]
        it = 0
        yb = 0
        for s in range(NSEQ):
            for i in range(NT):
                b = it % 2
                it += 1
                S.dma('sp', xt[b][:], c.XT.ap()[s].rearrange("(k p) t -> p k t", p=128)[:, :, i * 512:(i + 1) * 512],
                      r=[('XT', s, i)], w=[('xt', b)])
                for jb in range(4):
                    y = yb % 2
                    yb += 1
                    for half in range(2):
                        bank = 1 + ((yb * 2 + half) % 4)
                        for q in range(4):
                            k = half * 4 + q
                            S.op('pe', lambda e, bank=bank, q=q, k=k, jb=jb, b=b: e.transpose(
                                c.ps[bank][:, q * 128:(q + 1) * 128], xt[b][:, k, jb * 128:(jb + 1) * 128], c.ident_f[:]),
                                r=[('xt', b), 'ident_f'], w=[('ps', bank)])
                        if half == 0:
                            S.op('act', lambda e, bank=bank, y=y: e.activation(out=yt[y][:, 0:512], in_=c.ps[bank][:], func=AF.Copy),
                                 r=[('ps', bank)], w=[('yt', y)])
                        else:
                            S.op('dve', lambda e, bank=bank, y=y: e.tensor_copy(out=yt[y][:, 512:1024], in_=c.ps[bank][:]),
                                 r=[('ps', bank)], w=[('yt', y)])
                    S.op('pool', lambda e, y=y: e.tensor_tensor(out=sq[:], in0=yt[y][:], in1=yt[y][:], op=ALU.mult), r=[('yt', y)], w=['sq'])
                    S.op('dve', lambda e: e.tensor_reduce(out=ss[:, 0:1], in_=sq[:], axis=AX.X, op=ALU.add), r=['sq'], w=['ss'])
                    S.op('act', lambda e: e.activation(out=ss[:, 1:2], in_=ss[:, 0:1], func=AF.Sqrt, bias=EPS, scale=1.0 / D), r=['ss'], w=['ss'])
                    S.op('dve', lambda e: e.reciprocal(out=ss[:, 2:3], in_=ss[:, 1:2]), r=['ss'], w=['ss'])
                    S.op('dve', lambda e, y=y, jb=jb, b=b: e.scalar_tensor_tensor(out=yo[b][:, jb, :], in0=yt[y][:], scalar=ss[:, 2:3], in1=nf[:],
                                                                               op0=ALU.mult, op1=ALU.mult), r=[('yt', y), 'ss', 'nf'], w=[('yo', b)])
                S.dma('pool', c.y.ap()[s].rearrange("(j p) f -> p j f", p=128)[:, 4 * i:4 * i + 4, :], yo[b][:], r=[('yo', b)], w=[('y', s, i)])
        S.barrier()


def make_consts(L):
    f32 = np.float32
    k = {}
    k["k_ident"] = np.eye(128, dtype=f32)
    k["k_J"] = np.eye(128, dtype=f32)[::-1].copy()
    rot = np.zeros((128, 128), f32)
    for m in range(64):
        rot[m + 64, m] = -1.0
    for m in range(64, 128):
        rot[m - 64, m] = 1.0
    k["k_rot"] = rot
    half = 64
    inv = (np.float32(10000.0) ** (-np.arange(half, dtype=f32) / f32(half))).astype(f32)
    ang = (np.arange(L, dtype=f32)[None, :] * inv[:, None]).astype(f32)
    k["k_cos"] = np.concatenate([np.cos(ang), np.cos(ang)], 0).astype(f32)
    k["k_sin"] = np.concatenate([np.sin(ang), np.sin(ang)], 0).astype(f32)
    t = np.linspace(0.0, 1.0, L, dtype=f32)[:, None]
    bands = np.linspace(1e-4, 15, 16, dtype=f32)
    w = (f32(2.0 * math.pi / L) * np.arange(L, dtype=f32)[:, None] * bands[None, :]).astype(f32)
    feats = np.concatenate([t, np.cos(w), -np.sin(w)], -1).astype(f32).T
    k["k_feats"] = np.stack([feats, feats[:, ::-1]], 0).copy()
    deltas = np.abs(np.linspace(HY_MIN_DECAY, HY_MAX_DECAY, HYW, dtype=f32))
    k["k_negdelta"] = (-deltas).reshape(2, 128).T.copy().astype(f32)
    pos = np.arange(128, dtype=f32)
    k["k_df"] = np.maximum(pos[None, :] - pos[:, None], 0).astype(f32)
    k["k_db"] = np.maximum(pos[:, None] - pos[None, :], 0).astype(f32)
    k["k_kvec"] = np.stack([127.0 - pos, pos], 1).astype(f32)
    k["k_qrow"] = np.stack([pos + 1.0, 128.0 - pos], 0).astype(f32)
    ic = np.zeros((4, 16), f32)
    tt = np.arange(L)
    for g, win in enumerate(POOL_WINDOWS):
        lo = np.clip(tt - win // 2, 0, L - 1)
        hi = np.clip(tt + win // 2 - 1, 0, L - 1)
        cnt = (hi - lo + 1).astype(f32)
        ic[g, 0:8] = 1.0 / cnt[0:8]
        ic[g, 8:16] = 1.0 / cnt[L - 8:L]
    k["k_invcnt"] = ic.reshape(1, 64)
    return k


def layout_weights(W, DEPTH):
    f = lambda a: np.ascontiguousarray(np.asarray(a, dtype=np.float32))
    o = {}
    vec8 = lambda a: f(np.asarray(a).reshape(DEPTH, KC, 128).transpose(0, 2, 1))
    o["norm_mix"] = vec8(W["norm_mix"])
    o["norm_ffn"] = vec8(W["norm_ffn"])
    o["ple_norm"] = vec8(W["ple_norm"])
    o["w_in"] = f(W["w_in"])
    o["ret_decay_fwd"] = f(W["ret_decay_fwd"])
    o["ret_decay_bwd"] = f(W["ret_decay_bwd"])
    o["ret_gn"] = f(W["ret_gn"])
    o["hy_short_conv"] = f(np.asarray(W["hy_short_conv"]).reshape(DEPTH, 3, 6, 128).transpose(0, 3, 2, 1))
    o["hy_w1"] = f(W["hy_w1"])
    o["hy_b1"] = f(np.asarray(W["hy_b1"]).reshape(DEPTH, 64, 1))
    o["hy_freq"] = f(np.asarray(W["hy_freq"]).reshape(DEPTH, 64, 1))
    o["hy_w2"] = f(W["hy_w2"])
    o["hy_b2"] = f(np.asarray(W["hy_b2"]).reshape(DEPTH, 64, 1))
    o["hy_w3"] = f(W["hy_w3"])
    o["hy_bias"] = f(np.asarray(W["hy_bias"]).reshape(DEPTH, 2, 2, 128).transpose(0, 3, 1, 2).reshape(DEPTH, 128, 4))
    o["pool_w"] = f(W["pool_w"])
    o["pool_scale"] = f(np.asarray(W["pool_scale"]).reshape(DEPTH, 2, 128).transpose(0, 2, 1))
    o["w_out"] = f(W["w_out"])
    o["ffn_w_up"] = f(W["ffn_w_up"])
    o["ffn_conv"] = f(np.asarray(W["ffn_conv"]).reshape(DEPTH, 3, NJ, 128).transpose(0, 3, 2, 1))
    o["ffn_w_down"] = f(W["ffn_w_down"])
    o["ple_w"] = f(W["ple_w"])
    o["ple_gate_w"] = f(W["ple_gate_w"])
    o["norm_final"] = f(np.asarray(W["norm_final"]).reshape(1, D))
    return o


PADDED = ("w_in", "hy_w3", "w_out", "ffn_w_up", "ffn_w_down", "ple_w", "ple_gate_w", "k_cos", "k_sin", "k_feats")


def add_core_rows(m, cid):
    o = dict(m)
    for k in PADDED:
        a = np.asarray(o[k], dtype=np.float32)
        a2 = a.reshape(-1, a.shape[-1])
        o[k] = np.concatenate([a2, np.full((1, a2.shape[1]), float(cid), np.float32)], 0)
    return o


_CACHE = {}


def kernel(**inputs):
    L, DEPTH = L_FULL, DEPTH_FULL
    xp = np.asarray(inputs["x_prompt"], dtype=np.float32)
    xs = np.asarray(inputs["x_sample"], dtype=np.float32)
    pp = np.asarray(inputs["p_prompt"], dtype=np.float32)
    psm = np.asarray(inputs["p_sample"], dtype=np.float32)
    nP, nS = xp.shape[0], xs.shape[0]
    def seq_x(g):
        return xp[g] if g < nP else xs[g - nP]
    def seq_p(g):
        return pp[:, g] if g < nP else psm[:, g - nP]
    slots = []
    for cid in range(8):
        if cid < 4:
            slots.append([3 * cid, 3 * cid + 1, 3 * cid + 2])
        else:
            a = 12 + 2 * (cid - 4)
            slots.append([a, a + 1, a + 1])
    if "nc" not in _CACHE:
        _CACHE["nc"] = build(L, NSLOT, DEPTH)[0]
        _CACHE["consts"] = make_consts(L)
    nc = _CACHE["nc"]
    shared = dict(_CACHE["consts"])
    shared.update(layout_weights(inputs, DEPTH))
    in_maps = []
    for cid in range(8):
        m = add_core_rows(shared, cid)
        m["x"] = np.ascontiguousarray(np.stack([seq_x(g) for g in slots[cid]], 0))
        m["p"] = np.ascontiguousarray(np.stack([seq_p(g) for g in slots[cid]], 1))
        in_maps.append(m)
    res = run_bass_kernel_spmd(nc, in_maps, core_ids=list(range(8)))
    y_all = np.zeros((nP + nS, L, D), np.float32)
    for cid in range(8):
        y = np.asarray(res.results[cid]["y"])
        n_real = 3 if cid < 4 else 2
        for j in range(n_real):
            y_all[slots[cid][j]] = y[j]
    return (y_all[:nP].copy(), y_all[nP:].copy())
```
